# Optimizing a Trainium2 kernel written in Bass

```python
import math
import jax
import jax.numpy as jnp
from jax import lax
import numpy as np

D_MODEL = 1024
BATCH = 16
SEQ = 4096
DEPTH = 2

GRID_W = 64
CTX_LEN = 256

BRANCH_W = 512
N_BRANCH = 3
LRU_BLOCKS = 8
LRU_BLOCK_W = BRANCH_W // LRU_BLOCKS
LRU_CONV_W = 4
LRU_PAD = (2, 1)
LRU_C = 8.0
HY_ORDER = 2
HY_PROJ = (HY_ORDER + 1) * BRANCH_W
HY_CONV_W = 3
HY_PAD = (1, 1)
HY_BANDS = 16
HY_EMB = 1 + 2 * HY_BANDS
HY_FFN = 64
HY_TARGET = 1e-2
HY_FAST_DECAY = 0.3
HY_SLOW_DECAY = 1.5
SSM_HEADDIM = 64
SSM_HEADS = BRANCH_W // SSM_HEADDIM
SSM_GROUPS = 2
SSM_STATE = 128
SSM_CONV_W = 4
SSM_PAD = (2, 1)
SSM_CHUNK = 128
SSM_BC = SSM_GROUPS * SSM_STATE
SSM_XBC = BRANCH_W + 2 * SSM_BC
FFN_HIDDEN = 2816
FFN_CONV_W = 3
RMS_EPS = 1e-6

STATE_COLS = BRANCH_W + SSM_XBC + 2 * SSM_HEADS
OUT_COLS = BRANCH_W + HY_PROJ + BRANCH_W + N_BRANCH * D_MODEL
IN_COLS = STATE_COLS + OUT_COLS

kernel_name = 'hybrid_lru_hyena_ssd_diffusion_block'


def _rmsnorm(u, g):
    u32 = u.astype(jnp.float32)
    y = u32 * lax.rsqrt(jnp.mean(u32 * u32, axis=-1, keepdims=True) + RMS_EPS)
    return (y * g.astype(jnp.float32)).astype(u.dtype)


def _dwconv1d(u, w, b, pad):
    out = lax.conv_general_dilated(u, w[:, None, :], window_strides=(1,), padding=(pad,),
                                   dimension_numbers=('NWC', 'WIO', 'NWC'),
                                   feature_group_count=u.shape[-1])
    return out + b


def _dwconv_grid(u, w, b, grid_h, grid_w):
    bsz, length, ch = u.shape
    img = u.reshape(bsz, grid_h, grid_w, ch)
    out = lax.conv_general_dilated(img, w[:, :, None, :], (1, 1), ((1, 1), (1, 1)),
                                   dimension_numbers=('NHWC', 'HWIO', 'NHWC'),
                                   feature_group_count=ch)
    return out.reshape(bsz, length, ch) + b


def _lin_combine(left, right):
    a_l, b_l = left
    a_r, b_r = right
    return a_l * a_r, a_r * b_l + b_r


def _linear_scan(a, b, h0):
    b = b.at[:, 0].add(a[:, 0] * h0)
    _, h = lax.associative_scan(_lin_combine, (a, b), axis=1)
    return h


def _rglru_dir(xc, p, d, h0):
    bsz, length, _ = xc.shape
    xb = xc.reshape(bsz, length, LRU_BLOCKS, LRU_BLOCK_W)
    r = jax.nn.sigmoid((jnp.einsum('blnk,nkj->blnj', xb, p['lru_w_a'][d]).reshape(bsz, length, BRANCH_W)
                        + p['lru_b_a'][d]).astype(jnp.float32))
    i = jax.nn.sigmoid((jnp.einsum('blnk,nkj->blnj', xb, p['lru_w_i'][d]).reshape(bsz, length, BRANCH_W)
                        + p['lru_b_i'][d]).astype(jnp.float32))
    log_a = -LRU_C * r * jax.nn.softplus(-p['lru_lambda'][d].astype(jnp.float32))
    a = jnp.exp(log_a)
    gated_x = jnp.sqrt(-jnp.expm1(2.0 * log_a)) * (i * xc.astype(jnp.float32))
    h = _linear_scan(a, gated_x, h0)
    return h, h[:, -1]


def _hyena_filters(length, p):
    pos = jnp.arange(length, dtype=jnp.float32)[:, None]
    t01 = jnp.linspace(0.0, 1.0, length, dtype=jnp.float32)[:, None]
    bands = jnp.linspace(1e-4, HY_BANDS - 1, HY_BANDS, dtype=jnp.float32)
    ang = (2.0 * math.pi / length) * pos * bands
    feats = jnp.concatenate([t01, jnp.cos(ang), jnp.sin(ang)], axis=-1)
    freq = p['hy_freq'].astype(jnp.float32)
    h = jnp.sin(freq[0] * (feats @ p['hy_w1'].astype(jnp.float32) + p['hy_b1'].astype(jnp.float32)))
    h = jnp.sin(freq[1] * (h @ p['hy_w2'].astype(jnp.float32) + p['hy_b2'].astype(jnp.float32)))
    h = (h @ p['hy_w3'].astype(jnp.float32)).reshape(length, HY_ORDER, 2, BRANCH_W)
    max_decay = math.log(HY_TARGET) / HY_FAST_DECAY
    min_decay = math.log(HY_TARGET) / HY_SLOW_DECAY
    deltas = jnp.abs(jnp.linspace(min_decay, max_decay, BRANCH_W, dtype=jnp.float32))
    window = jnp.exp(-t01 * deltas)
    return h * window[:, None, None, :]


def _bidir_long_conv(z, h_fwd, h_bwd, bias):
    length = z.shape[1]
    taps = jnp.concatenate([h_fwd, jnp.zeros_like(h_fwd[:1]), h_bwd[:0:-1]], axis=0)
    zf = jnp.fft.rfft(z.astype(jnp.float32), n=2 * length, axis=1)
    kf = jnp.fft.rfft(taps, n=2 * length, axis=0)
    y = jnp.fft.irfft(zf * kf[None], n=2 * length, axis=1)[:, :length]
    return (y + z.astype(jnp.float32) * bias.astype(jnp.float32)).astype(z.dtype)


def _hyena(u, p):
    uc = _dwconv1d(u, p['hy_conv_w'], p['hy_conv_b'], HY_PAD)
    v, x1, x2 = jnp.split(uc, 3, axis=-1)
    filt = _hyena_filters(u.shape[1], p)
    z = x1 * _bidir_long_conv(v, filt[:, 0, 0], filt[:, 0, 1], p['hy_bias'][0])
    return x2 * _bidir_long_conv(z, filt[:, 1, 0], filt[:, 1, 1], p['hy_bias'][1])


def _ssd(xs, dt, a_neg, bm, cm, h0, with_output):
    bsz, length = xs.shape[:2]
    nc, q, g, e = length // SSM_CHUNK, SSM_CHUNK, SSM_GROUPS, SSM_HEADS // SSM_GROUPS
    x = xs.reshape(bsz, nc, q, g, e, SSM_HEADDIM)
    dtc = dt.reshape(bsz, nc, q, g, e)
    xdt = x * dtc[..., None]
    bc = bm.reshape(bsz, nc, q, g, SSM_STATE)
    cc = cm.reshape(bsz, nc, q, g, SSM_STATE)
    cs = jnp.cumsum(dtc * a_neg.reshape(g, e), axis=2)
    decay_to_end = jnp.exp(cs[:, :, -1:] - cs)
    states = jnp.einsum('bcsgn,bcsge,bcsgep->bcgepn', bc, decay_to_end, xdt)
    chunk_decay = jnp.exp(cs[:, :, -1])[..., None, None]
    h0g = h0.reshape(bsz, g, e, SSM_HEADDIM, SSM_STATE)
    states = states.at[:, 0].add(chunk_decay[:, 0] * h0g)
    _, h_out = lax.associative_scan(_lin_combine, (chunk_decay, states), axis=1)
    final = h_out[:, -1].reshape(bsz, SSM_HEADS, SSM_HEADDIM, SSM_STATE)
    if not with_output:
        return None, final
    h_in = jnp.concatenate([h0g[:, None], h_out[:, :-1]], axis=1)
    seg = cs[:, :, :, None] - cs[:, :, None]
    lower = jnp.tril(jnp.ones((q, q), dtype=bool))[:, :, None, None]
    lmat = jnp.exp(jnp.where(lower, seg, -jnp.inf))
    cb = jnp.einsum('bcqgn,bcsgn->bcqsg', cc, bc)
    y_diag = jnp.einsum('bcqsg,bcqsge,bcsgep->bcqgep', cb, lmat, xdt)
    y_off = jnp.einsum('bcqgn,bcgepn,bcqge->bcqgep', cc, h_in, jnp.exp(cs))
    return (y_diag + y_off).reshape(bsz, length, SSM_HEADS, SSM_HEADDIM), final


def _token_mixer(hm, p, init, with_output):
    bsz, length, _ = hm.shape
    f32 = jnp.float32
    lru_f0, lru_b0, ssm_f0, ssm_b0 = init
    ps = hm @ p['w_in'][:, :STATE_COLS]
    xa = ps[..., :BRANCH_W]
    xbc = ps[..., BRANCH_W:BRANCH_W + SSM_XBC]
    dt_raw = ps[..., BRANCH_W + SSM_XBC:].reshape(bsz, length, 2, SSM_HEADS)

    xa_c = _dwconv1d(xa, p['lru_conv_w'], p['lru_conv_b'], LRU_PAD)
    ha_f, fin_a_f = _rglru_dir(xa_c, p, 0, lru_f0)
    ha_b, fin_a_b = _rglru_dir(jnp.flip(xa_c, 1), p, 1, lru_b0)

    xbc_c = jax.nn.silu(_dwconv1d(xbc, p['ssm_conv_w'], p['ssm_conv_b'], SSM_PAD)).astype(f32)
    xs = xbc_c[..., :BRANCH_W].reshape(bsz, length, SSM_HEADS, SSM_HEADDIM)
    bm = xbc_c[..., BRANCH_W:BRANCH_W + SSM_BC].reshape(bsz, length, SSM_GROUPS, SSM_STATE)
    cm = xbc_c[..., BRANCH_W + SSM_BC:].reshape(bsz, length, SSM_GROUPS, SSM_STATE)
    dt = jax.nn.softplus(dt_raw.astype(f32) + p['ssm_dt_bias'].astype(f32))
    a_neg = -jnp.exp(p['ssm_a_log'].astype(f32))
    ys_f, fin_s_f = _ssd(xs, dt[:, :, 0], a_neg[0], bm, cm, ssm_f0, with_output)
    ys_b, fin_s_b = _ssd(jnp.flip(xs, 1), jnp.flip(dt[:, :, 1], 1), a_neg[1],
                         jnp.flip(bm, 1), jnp.flip(cm, 1), ssm_b0, with_output)
    finals = (fin_a_f, fin_a_b, fin_s_f, fin_s_b)
    if not with_output:
        return None, finals

    pr = hm @ p['w_in'][:, STATE_COLS:]
    ga = pr[..., :BRANCH_W]
    hy_in = pr[..., BRANCH_W:BRANCH_W + HY_PROJ]
    z = pr[..., BRANCH_W + HY_PROJ:2 * BRANCH_W + HY_PROJ]
    gate_logits = pr[..., 2 * BRANCH_W + HY_PROJ:]

    y_a = (ha_f + jnp.flip(ha_b, 1)).astype(hm.dtype) * jax.nn.gelu(ga)
    y_b = _hyena(hy_in, p)
    ys = ys_f + jnp.flip(ys_b, 1) + p['ssm_d'].astype(f32)[:, None] * xs
    y_c = _rmsnorm(ys.reshape(bsz, length, BRANCH_W) * jax.nn.silu(z.astype(f32)), p['ssm_norm']).astype(hm.dtype)

    branch_outs = (y_a, y_b, y_c)
    merged = jnp.zeros((bsz, length, D_MODEL), f32)
    for k in range(N_BRANCH):
        gate = jax.nn.sigmoid(gate_logits[..., k * D_MODEL:(k + 1) * D_MODEL].astype(f32))
        merged = merged + gate * (branch_outs[k] @ p['w_branch'][k])
    return merged.astype(hm.dtype) @ p['w_out'], finals


def _conv_ffn(h, p, grid_h, grid_w):
    u = _dwconv_grid(h @ p['ffn_w_up'], p['ffn_conv_w'], p['ffn_conv_b'], grid_h, grid_w)
    val, gate = jnp.split(u, 2, axis=-1)
    return (jax.nn.gelu(gate) * val) @ p['ffn_w_down']


def setup_inputs(seed: int = 0) -> dict:
    key = jax.random.key(seed)
    ks = iter(jax.random.split(key, 48))
    f32 = jnp.float32

    def nrm(shape, scale):
        return scale * jax.random.normal(next(ks), shape, f32)

    def unif(shape, lo, hi):
        return jax.random.uniform(next(ks), shape, f32, lo, hi)

    x = nrm((BATCH, SEQ, D_MODEL), 1.0)
    c = nrm((BATCH, D_MODEL), 1.0)
    ctx = nrm((BATCH, CTX_LEN, D_MODEL), 1.0)
    c_ctx = nrm((D_MODEL,), 1.0)
    mod_w = nrm((DEPTH, D_MODEL, 6 * D_MODEL), 0.5 * D_MODEL ** -0.5)
    mod_b = nrm((DEPTH, 6 * D_MODEL), 0.02)
    norms = 1.0 + nrm((DEPTH, 4, D_MODEL), 0.05)
    w_in = nrm((DEPTH, D_MODEL, IN_COLS), D_MODEL ** -0.5)
    lru_conv_w = nrm((DEPTH, LRU_CONV_W, BRANCH_W), LRU_CONV_W ** -0.5)
    lru_conv_b = nrm((DEPTH, BRANCH_W), 0.02)
    lru_w_a = nrm((DEPTH, 2, LRU_BLOCKS, LRU_BLOCK_W, LRU_BLOCK_W), LRU_BLOCK_W ** -0.5)
    lru_b_a = nrm((DEPTH, 2, BRANCH_W), 0.02)
    lru_w_i = nrm((DEPTH, 2, LRU_BLOCKS, LRU_BLOCK_W, LRU_BLOCK_W), LRU_BLOCK_W ** -0.5)
    lru_b_i = nrm((DEPTH, 2, BRANCH_W), 0.02)
    a_pow = unif((DEPTH, 2, BRANCH_W), 0.9, 0.999)
    a_base = a_pow ** (1.0 / LRU_C)
    lru_lambda = jnp.log(a_base) - jnp.log1p(-a_base)
    hy_conv_w = nrm((DEPTH, HY_CONV_W, HY_PROJ), HY_CONV_W ** -0.5)
    hy_conv_b = nrm((DEPTH, HY_PROJ), 0.02)
    hy_w1 = nrm((DEPTH, HY_EMB, HY_FFN), HY_EMB ** -0.5)
    hy_b1 = nrm((DEPTH, HY_FFN), 0.02)
    hy_w2 = nrm((DEPTH, HY_FFN, HY_FFN), HY_FFN ** -0.5)
    hy_b2 = nrm((DEPTH, HY_FFN), 0.02)
    hy_w3 = nrm((DEPTH, HY_FFN, HY_ORDER * 2 * BRANCH_W), 0.1 * HY_FFN ** -0.5)
    hy_freq = 1.0 + nrm((DEPTH, 2, HY_FFN), 0.05)
    hy_bias = nrm((DEPTH, HY_ORDER, BRANCH_W), 0.5)
    ssm_conv_w = nrm((DEPTH, SSM_CONV_W, SSM_XBC), SSM_CONV_W ** -0.5)
    ssm_conv_b = nrm((DEPTH, SSM_XBC), 0.02)
    dt0 = jnp.exp(unif((DEPTH, 2, SSM_HEADS), math.log(1e-3), math.log(1e-1)))
    ssm_dt_bias = dt0 + jnp.log(-jnp.expm1(-dt0))
    ssm_a_log = jnp.log(unif((DEPTH, 2, SSM_HEADS), 1.0, 16.0))
    ssm_d = 1.0 + nrm((DEPTH, SSM_HEADS), 0.05)
    ssm_norm = 1.0 + nrm((DEPTH, BRANCH_W), 0.05)
    w_branch = nrm((DEPTH, N_BRANCH, BRANCH_W, D_MODEL), BRANCH_W ** -0.5)
    w_out = nrm((DEPTH, D_MODEL, D_MODEL), D_MODEL ** -0.5)
    ffn_w_up = nrm((DEPTH, D_MODEL, 2 * FFN_HIDDEN), D_MODEL ** -0.5)
    ffn_conv_w = nrm((DEPTH, FFN_CONV_W, FFN_CONV_W, 2 * FFN_HIDDEN), 1.0 / FFN_CONV_W)
    ffn_conv_b = nrm((DEPTH, 2 * FFN_HIDDEN), 0.02)
    ffn_w_down = nrm((DEPTH, FFN_HIDDEN, D_MODEL), FFN_HIDDEN ** -0.5)
    return {'x': x, 'c': c, 'ctx': ctx, 'c_ctx': c_ctx, 'mod_w': mod_w, 'mod_b': mod_b,
            'norms': norms, 'w_in': w_in, 'lru_conv_w': lru_conv_w, 'lru_conv_b': lru_conv_b,
            'lru_w_a': lru_w_a, 'lru_b_a': lru_b_a, 'lru_w_i': lru_w_i, 'lru_b_i': lru_b_i,
            'lru_lambda': lru_lambda, 'hy_conv_w': hy_conv_w, 'hy_conv_b': hy_conv_b,
            'hy_w1': hy_w1, 'hy_b1': hy_b1, 'hy_w2': hy_w2, 'hy_b2': hy_b2, 'hy_w3': hy_w3,
            'hy_freq': hy_freq, 'hy_bias': hy_bias, 'ssm_conv_w': ssm_conv_w,
            'ssm_conv_b': ssm_conv_b, 'ssm_dt_bias': ssm_dt_bias, 'ssm_a_log': ssm_a_log,
            'ssm_d': ssm_d, 'ssm_norm': ssm_norm, 'w_branch': w_branch, 'w_out': w_out,
            'ffn_w_up': ffn_w_up, 'ffn_conv_w': ffn_conv_w, 'ffn_conv_b': ffn_conv_b,
            'ffn_w_down': ffn_w_down}


def reference(x, c, ctx, c_ctx, mod_w, mod_b, norms, w_in, lru_conv_w, lru_conv_b,
              lru_w_a, lru_b_a, lru_w_i, lru_b_i, lru_lambda, hy_conv_w, hy_conv_b,
              hy_w1, hy_b1, hy_w2, hy_b2, hy_w3, hy_freq, hy_bias, ssm_conv_w, ssm_conv_b,
              ssm_dt_bias, ssm_a_log, ssm_d, ssm_norm, w_branch, w_out, ffn_w_up,
              ffn_conv_w, ffn_conv_b, ffn_w_down):
    bsz = x.shape[0]
    rows = x.shape[1] // GRID_W
    ctx_len = ctx.shape[1]
    f32 = jnp.float32
    zero_init = (jnp.zeros((bsz, BRANCH_W), f32), jnp.zeros((bsz, BRANCH_W), f32),
                 jnp.zeros((bsz, SSM_HEADS, SSM_HEADDIM, SSM_STATE), f32),
                 jnp.zeros((bsz, SSM_HEADS, SSM_HEADDIM, SSM_STATE), f32))
    for l in range(DEPTH):
        last = l == DEPTH - 1
        p = {'w_in': w_in[l], 'lru_conv_w': lru_conv_w[l], 'lru_conv_b': lru_conv_b[l],
             'lru_w_a': lru_w_a[l], 'lru_b_a': lru_b_a[l], 'lru_w_i': lru_w_i[l],
             'lru_b_i': lru_b_i[l], 'lru_lambda': lru_lambda[l], 'hy_conv_w': hy_conv_w[l],
             'hy_conv_b': hy_conv_b[l], 'hy_w1': hy_w1[l], 'hy_b1': hy_b1[l], 'hy_w2': hy_w2[l],
             'hy_b2': hy_b2[l], 'hy_w3': hy_w3[l], 'hy_freq': hy_freq[l], 'hy_bias': hy_bias[l],
             'ssm_conv_w': ssm_conv_w[l], 'ssm_conv_b': ssm_conv_b[l],
             'ssm_dt_bias': ssm_dt_bias[l], 'ssm_a_log': ssm_a_log[l], 'ssm_d': ssm_d[l],
             'ssm_norm': ssm_norm[l], 'w_branch': w_branch[l], 'w_out': w_out[l],
             'ffn_w_up': ffn_w_up[l], 'ffn_conv_w': ffn_conv_w[l], 'ffn_conv_b': ffn_conv_b[l],
             'ffn_w_down': ffn_w_down[l]}
        mod = jax.nn.silu(c) @ mod_w[l] + mod_b[l]
        sh1, sc1, g1, sh2, sc2, g2 = jnp.split(mod[:, None, :], 6, axis=-1)
        mod_c = jax.nn.silu(c_ctx) @ mod_w[l] + mod_b[l]
        csh1, csc1, cg1, csh2, csc2, cg2 = jnp.split(mod_c, 6)

        hc = _rmsnorm(ctx, norms[l, 0]) * (1.0 + csc1) + csh1
        ctx_out, ctx_states = _token_mixer(hc, p, zero_init, not last)
        hx = _rmsnorm(x, norms[l, 0]) * (1.0 + sc1) + sh1
        x_out, _ = _token_mixer(hx, p, ctx_states, True)
        x = x + g1 * _rmsnorm(x_out, norms[l, 1])

        hx2 = _rmsnorm(x, norms[l, 2]) * (1.0 + sc2) + sh2
        x = x + g2 * _rmsnorm(_conv_ffn(hx2, p, rows, GRID_W), norms[l, 3])

        if not last:
            ctx = ctx + cg1 * _rmsnorm(ctx_out, norms[l, 1])
            hc2 = _rmsnorm(ctx, norms[l, 2]) * (1.0 + csc2) + csh2
            ctx = ctx + cg2 * _rmsnorm(_conv_ffn(hc2, p, 1, ctx_len), norms[l, 3])
    return x
```

```python
import math
from contextlib import ExitStack
import numpy as np
import ml_dtypes
import concourse.bass as bass
import concourse.mybir as mybir
from concourse.bass_utils import run_bass_kernel_spmd

F32 = mybir.dt.float32
BF16 = mybir.dt.bfloat16
AF = mybir.ActivationFunctionType
ALU = mybir.AluOpType
AX = mybir.AxisListType

D = 1024
SEQ = 4096
CTX = 256
U = CTX + SEQ
NSEQ = 2
DEPTH = 2
BW = 512
IN_COLS = 7184
FFN_H = 2816
EPS = 1e-6

COMPUTE = ("pe", "act", "dve", "pool")
ENGINES = ("pe", "act", "dve", "pool", "sp")


class Res:
    __slots__ = ("name", "w", "r", "dkey")

    def __init__(self, name):
        self.name = name
        self.w = None
        self.r = {}
        self.dkey = None


class Prog:
    NDK = 96

    def __init__(self):
        self.items = {e: [] for e in ENGINES}
        self.cnt = {e: 0 for e in COMPUTE}
        self.known = {e: {} for e in ENGINES}
        self.dtot = {}
        self.ndk = 0

    def cur(self, key):
        return self.cnt[key] if key in self.cnt else self.dtot.get(key, 0)

    def _need(self, eng, key, val):
        if key not in self.cnt:
            val = self.dtot[key]
        if self.known[eng].get(key, 0) >= val:
            return
        self.known[eng][key] = val
        self.items[eng].append(("wait", key, val))

    def _deps(self, eng, reads, writes):
        for r in reads:
            if r.w is not None and not (eng == "pe" and r.w[0] == "pe"):
                self._need(eng, *r.w)
        for w in writes:
            if w.w is not None and not (eng == "pe" and w.w[0] == "pe"):
                self._need(eng, *w.w)
            for k, v in w.r.items():
                if not (eng == "pe" and k == "pe"):
                    self._need(eng, k, v)

    def op(self, eng, fn, reads=(), writes=(), inc=True):
        self._deps(eng, reads, writes)
        val = self.cnt[eng] + 1
        if inc:
            self.cnt[eng] = val
        self.items[eng].append(("op", fn, inc))
        for r in reads:
            r.r[eng] = max(r.r.get(eng, 0), val)
        for w in writes:
            w.w = (eng, val)
            w.r = {}

    def dkey(self, res):
        if res.dkey is None:
            res.dkey = "d%d" % (self.ndk % self.NDK)
            self.ndk += 1
            self.dtot.setdefault(res.dkey, 0)
        return res.dkey

    def dma(self, eng, out_ap, in_ap, reads, writes, semres, **kw):
        self._deps(eng, reads, writes)
        key = self.dkey(semres)
        self.dtot[key] += 16
        val = self.dtot[key]
        self.items[eng].append(("dma", out_ap, in_ap, key, kw))
        for r in reads:
            r.r[key] = val
        for w in writes:
            w.w = (key, val)
            w.r = {}

    def barrier(self):
        keys = list(self.cnt.keys()) + list(self.dtot.keys())
        for e in ENGINES:
            for k in keys:
                if self.cur(k) > 0:
                    self._need(e, k, self.cur(k))


class Arena:
    def __init__(self, ap_f32, ap_bf16, nbytes):
        self.f = ap_f32
        self.b = ap_bf16
        self.top = 0
        self.nbytes = nbytes

    def alloc(self, name, shape, dt):
        esz = 4 if dt == F32 else 2
        n = int(np.prod(shape))
        nb = (n * esz + 31) // 32 * 32
        off = self.top
        assert off + nb <= self.nbytes, ("SBUF arena overflow", name, off + nb)
        self.top += nb
        base = self.f if dt == F32 else self.b
        v = base[:, off // esz: off // esz + n]
        if len(shape) == 2:
            v = v.rearrange("p (a b) -> p a b", a=shape[0])
        elif len(shape) == 3:
            v = v.rearrange("p (a b c) -> p a b c", a=shape[0], b=shape[1])
        return v, Res(name)


def _bf(a):
    return np.asarray(a, np.float32).astype(ml_dtypes.bfloat16)


W_NAMES = ["mod_w", "mod_b", "norms", "w_in", "lru_conv_w", "lru_conv_b", "lru_w_a", "lru_b_a", "lru_w_i", "lru_b_i",
           "lru_lambda", "hy_conv_w", "hy_conv_b", "hy_w1", "hy_b1", "hy_w2", "hy_b2", "hy_w3", "hy_freq", "hy_bias",
           "ssm_conv_w", "ssm_conv_b", "ssm_dt_bias", "ssm_a_log", "ssm_d", "ssm_norm", "w_branch", "w_out",
           "ffn_w_up", "ffn_conv_w", "ffn_conv_b", "ffn_w_down"]

W_SHAPES = {"mod_w": [2, 1024, 6144], "mod_b": [2, 6144], "norms": [2, 4, 1024], "w_in": [2, 1024, 7184],
            "lru_conv_w": [2, 4, 512], "lru_conv_b": [2, 512], "lru_w_a": [2, 2, 8, 64, 64], "lru_b_a": [2, 2, 512],
            "lru_w_i": [2, 2, 8, 64, 64], "lru_b_i": [2, 2, 512], "lru_lambda": [2, 2, 512],
            "hy_conv_w": [2, 3, 1536], "hy_conv_b": [2, 1536], "hy_w1": [2, 33, 64], "hy_b1": [2, 64],
            "hy_w2": [2, 64, 64], "hy_b2": [2, 64], "hy_w3": [2, 64, 2048], "hy_freq": [2, 2, 64],
            "hy_bias": [2, 2, 512], "ssm_conv_w": [2, 4, 1024], "ssm_conv_b": [2, 1024], "ssm_dt_bias": [2, 2, 8],
            "ssm_a_log": [2, 2, 8], "ssm_d": [2, 8], "ssm_norm": [2, 512], "w_branch": [2, 3, 512, 1024],
            "w_out": [2, 1024, 1024], "ffn_w_up": [2, 1024, 5632], "ffn_conv_w": [2, 3, 3, 5632],
            "ffn_conv_b": [2, 5632], "ffn_w_down": [2, 2816, 1024]}


def host_consts():
    c = {}
    c["ident_bf"] = _bf(np.eye(128))
    c["ident_f"] = np.eye(128, dtype=np.float32)
    c["tri_f"] = np.triu(np.ones((128, 128), np.float32))
    c["tri_b"] = np.tril(np.ones((128, 128), np.float32))
    c["mneg_f"] = ((1.0 - c["tri_f"]) * -1e30).astype(np.float32)
    c["mneg_b"] = ((1.0 - c["tri_b"]) * -1e30).astype(np.float32)
    c["hy_fe0"] = hy_feats(True)
    c["hy_fe1"] = hy_feats(False)
    c["jmat"] = _bf(np.eye(128)[::-1])
    c["hy_ndl"] = hy_delta()
    return c


CONST_SHAPES = {"ident_bf": ([128, 128], BF16), "ident_f": ([128, 128], F32), "tri_f": ([128, 128], F32),
                "tri_b": ([128, 128], F32), "mneg_f": ([128, 128], F32), "mneg_b": ([128, 128], F32), "hy_fe0": ([33, 8704], F32), "hy_fe1": ([33, 8704], F32), "jmat": ([128, 128], BF16), "hy_ndl": ([128, 4], F32)}


class K:
    pass


def build(stop_after=None, dumps=()):
    nc = bass.Bass("TRN2", target_bir_lowering=False)
    k = K()
    k.nc = nc
    k.P = P = Prog()
    k.dumps = dumps
    k.s_per_h = 1
    dr = {}
    dr["x"] = nc.dram_tensor("x", [NSEQ, SEQ, D], F32, kind="ExternalInput").ap()
    dr["ctx"] = nc.dram_tensor("ctx", [NSEQ, CTX, D], F32, kind="ExternalInput").ap()
    dr["cT"] = nc.dram_tensor("cT", [128, 8, 3], F32, kind="ExternalInput").ap()
    for n in W_NAMES:
        dr[n] = nc.dram_tensor(n, W_SHAPES[n], F32, kind="ExternalInput").ap()
    for n, (shp, dt) in CONST_SHAPES.items():
        dr[n] = nc.dram_tensor(n, shp, dt, kind="ExternalInput").ap()
    dr["out"] = nc.dram_tensor("out", [NSEQ, SEQ, D], F32, kind="ExternalOutput").ap()
    k.dr = dr
    k.dres = {}

    def scratch(name, shape, dt):
        kind = "ExternalOutput" if name in dumps else "Internal"
        dr[name] = nc.dram_tensor(name, shape, dt, kind=kind).ap()
        k.dres[name] = Res(name)
        return dr[name]

    k.scratch = scratch
    scratch("MODR", [DEPTH, 3, 6, D], F32)
    scratch("XS", [2, NSEQ, U, D], F32)
    scratch("XA", [NSEQ, BW, U], F32)
    scratch("GA", [NSEQ, BW, U], BF16)
    scratch("XBC", [NSEQ, 1024, U], BF16)
    scratch("DT", [NSEQ, U, 16], F32)
    scratch("ZS", [NSEQ, U, BW], BF16)
    scratch("HYV", [NSEQ, BW, U], BF16)
    scratch("X12", [NSEQ, U, 1536], BF16)
    scratch("GATES", [NSEQ, 3 * D, U], BF16)
    scratch("YA", [NSEQ, BW, U], BF16)
    scratch("YS", [NSEQ, U, BW], F32)
    scratch("YB", [NSEQ, BW, U], BF16)
    scratch("TAPS", [2, BW, NP_], BF16)
    scratch("ACTS", [NSEQ, FFN_H, U], BF16)
    k.dres["out"] = Res("out")
    scratch("YC", [NSEQ, BW, U], BF16)
    for dn in ("DBG1", "DBG2", "DBG3", "DBG4"):
        if dn in dumps:
            scratch(dn, [NSEQ, BW, U], F32)
    if "HXT" in dumps:
        scratch("HXT", [NSEQ, D, U], BF16)

    with ExitStack() as es:
        ARENA_BYTES = 211968
        arena_t = es.enter_context(nc.sbuf_tensor("arena", [128, ARENA_BYTES // 4], F32))
        k.A = Arena(arena_t[:], arena_t[:].bitcast(BF16), ARENA_BYTES)
        k.ps = []
        for i in range(8):
            t = es.enter_context(nc.psum_tensor("ps%d" % i, [128, 512], F32))
            k.ps.append((t[:], t[:].bitcast(BF16), Res("ps%d" % i)))
        emit_all(k, stop_after)
        P.barrier()
        sems = {}
        for key in list(P.cnt.keys()) + list(P.dtot.keys()):
            sems[key] = es.enter_context(nc.semaphore("s_" + key))
        block = es.enter_context(nc.Block())

        def replay(eng_handle, items):
            for it in items:
                if it[0] == "wait":
                    eng_handle.wait_ge(sems[it[1]], it[2])
                elif it[0] == "op":
                    ins = it[1](eng_handle)
                    if it[2]:
                        ins.then_inc(sems[_ek[0]], 1)
                else:
                    eng_handle.dma_start(out=it[1], in_=it[2], **it[4]).then_inc(sems[it[3]], 16)

        _ek = [None]

        @block.tensor
        def _(e):
            _ek[0] = "pe"
            replay(e, P.items["pe"])

        @block.scalar
        def _(e):
            _ek[0] = "act"
            replay(e, P.items["act"])

        @block.vector
        def _(e):
            _ek[0] = "dve"
            replay(e, P.items["dve"])

        @block.gpsimd
        def _(e):
            _ek[0] = "pool"
            replay(e, P.items["pool"])

        @block.sync
        def _(e):
            _ek[0] = "sp"
            replay(e, P.items["sp"])
    return nc


def emit_all(k, stop_after):
    P = k.P
    stage_consts(k)
    stage_mod(k)
    P.barrier()
    if stop_after == "mod":
        return
    for l in range(DEPTH):
        stage_A(k, l)
        P.barrier()
        if stop_after == "A%d" % l:
            return
        if not getattr(k, "skipL", False):
            stage_L(k, l)
        P.barrier()
        if stop_after == "L%d" % l:
            return
        stage_HF(k, l)
        P.barrier()
        if getattr(k, "interleave", True):
            base_ = k.A.top
            gh = stage_HC(k, l, shared=True)
            next(gh)
            gs = stage_S(k, l, banks=(0, 1, 2, 3, 4))
            done_s = done_h = False
            while not (done_s and done_h):
                if not done_h:
                    try:
                        next(gh)
                    except StopIteration:
                        done_h = True
                for _ in range(k.s_per_h if not done_h else 64):
                    if done_s:
                        break
                    try:
                        next(gs)
                    except StopIteration:
                        done_s = True
            k.A.top = base_
        else:
            for _ in stage_S(k, l):
                pass
            P.barrier()
            for _ in stage_HC(k, l):
                pass
        P.barrier()
        if stop_after == "H%d" % l or stop_after == "S%d" % l:
            return
        stage_M(k, l)
        P.barrier()
        if stop_after == "M%d" % l:
            return
        stage_F(k, l)
        P.barrier()
        if stop_after == "F%d" % l:
            return


def stage_consts(k):
    P, A, dr = k.P, k.A, k.dr
    k.identb, k.identb_r = A.alloc("identb", [128], BF16)
    k.identf, k.identf_r = A.alloc("identf", [128], F32)
    P.dma("sp", k.identb, dr["ident_bf"], [], [k.identb_r], k.identb_r)
    P.dma("sp", k.identf, dr["ident_f"], [], [k.identf_r], k.identf_r)
    k.const_top = A.top


def stage_mod(k):
    P, A, dr = k.P, k.A, k.dr
    mark = A.top
    ct, ct_r = A.alloc("ct", [8, 3], F32)
    sg, sg_r = A.alloc("sg", [8, 3], F32)
    P.dma("sp", ct, dr["cT"], [], [ct_r], ct_r)
    P.op("act", lambda e: e.activation(out=sg, in_=ct, func=AF.Silu), [ct_r], [sg_r])
    wt = [A.alloc("modw%d" % i, [8, 512], F32) for i in range(2)]
    modrow, modrow_r = A.alloc("modrow", [6144], F32)
    modb, modb_r = A.alloc("modb", [6144], F32)
    nrm, nrm_r = A.alloc("nrm", [4, D], F32)
    der, der_r = A.alloc("der", [6, D], F32)
    for l in range(DEPTH):
        P.dma("sp", modb[0:3, :], dr["mod_b"][l:l + 1, :].partition_broadcast(3), [], [modb_r], modb_r)
        P.dma("sp", nrm[0:3], dr["norms"][l:l + 1].partition_broadcast(3), [], [nrm_r], nrm_r)
        for j in range(12):
            w, w_r = wt[j % 2]
            P.dma("sp", w, dr["mod_w"][l, :, j * 512:(j + 1) * 512].rearrange("(c p) n -> p c n", p=128),
                  [], [w_r], w_r)
            pst, _, ps_r = k.ps[j % 2]
            for c in range(8):
                P.op("pe", lambda e, c=c, w=w, pst=pst: e.matmul(pst[0:3, :], lhsT=sg[:, c, :], rhs=w[:, c, :],
                                                                   start=(c == 0), stop=(c == 7)),
                     [sg_r, w_r], [ps_r], inc=(c == 7))
            P.op("dve", lambda e, j=j, pst=pst: e.tensor_tensor(out=modrow[0:3, j * 512:(j + 1) * 512], in0=pst[0:3, :],
                                                                  in1=modb[0:3, j * 512:(j + 1) * 512], op=ALU.add),
                 [ps_r, modb_r], [modrow_r])
        sl = lambda i: modrow[0:3, i * D:(i + 1) * D]
        P.op("dve", lambda e: e.scalar_tensor_tensor(out=der[0:3, 0, :], in0=sl(1), scalar=1.0, in1=nrm[0:3, 0, :],
                                                      op0=ALU.add, op1=ALU.mult), [modrow_r, nrm_r], [der_r])
        P.op("dve", lambda e: e.tensor_copy(out=der[0:3, 1, :], in_=sl(0)), [modrow_r], [der_r])
        P.op("dve", lambda e: e.tensor_tensor(out=der[0:3, 2, :], in0=sl(2), in1=nrm[0:3, 1, :], op=ALU.mult),
             [modrow_r, nrm_r], [der_r])
        P.op("dve", lambda e: e.scalar_tensor_tensor(out=der[0:3, 3, :], in0=sl(4), scalar=1.0, in1=nrm[0:3, 2, :],
                                                      op0=ALU.add, op1=ALU.mult), [modrow_r, nrm_r], [der_r])
        P.op("dve", lambda e: e.tensor_copy(out=der[0:3, 4, :], in_=sl(3)), [modrow_r], [der_r])
        P.op("dve", lambda e: e.tensor_tensor(out=der[0:3, 5, :], in0=sl(5), in1=nrm[0:3, 3, :], op=ALU.mult),
             [modrow_r, nrm_r], [der_r])
        P.dma("sp", dr["MODR"][l], der[0:3], [der_r], [k.dres["MODR"]], der_r)
    A.top = mark


def load_modrow(k, l, v, j, name):
    t, r = k.A.alloc(name, [D], F32)
    k.P.dma("sp", t, k.dr["MODR"][l, v, j:j + 1, :].partition_broadcast(128), [k.dres["MODR"]], [r], r)
    return t, r


def seg_tiles():
    return [(0, CTX)] + [(CTX + i * 512, 512) for i in range(SEQ // 512)]


def stream_src(k, l, s, u0, n):
    if l == 0:
        if u0 < CTX:
            return k.dr["ctx"][s, u0:u0 + n, :], None
        return k.dr["x"][s, u0 - CTX:u0 - CTX + n, :], None
    return k.dr["XS"][1, s, u0:u0 + n, :], k.dres["XS"]


def norm_mod_T(k, l, s, hxT, hxT_r, j0, src_fn, name):
    P, A = k.P, k.A
    P.barrier()
    mark = A.top
    rows = {}
    for v in (s, 2):
        rows[v] = (load_modrow(k, l, v, j0, "Arow%d" % v), load_modrow(k, l, v, j0 + 1, "Brow%d" % v))
    xt = [A.alloc("xt%d" % i, [D], F32) for i in range(3)]
    sq, sq_r = A.alloc("sq", [D], F32)
    t1 = [A.alloc("t1_%d" % i, [D], F32) for i in range(2)]
    hb = [A.alloc("hb%d" % i, [D], BF16) for i in range(2)]
    st = [A.alloc("st%d" % i, [4], F32) for i in range(2)]
    nsub = U // 128

    def part_a1(i):
        u0 = i * 128
        x, x_r = xt[i % 3]
        src, sres = src_fn(k, l, s, u0, 128)
        P.dma("sp", x, src, [sres] if sres else [], [x_r], x_r)
        ss, ss_r = st[i % 2]
        P.op("act", lambda e, x=x, ss=ss: e.activation(out=sq, in_=x, func=AF.Square, accum_out=ss[:, 0:1]),
             [x_r], [sq_r, ss_r])

    def part_a2(i):
        ss, ss_r = st[i % 2]
        P.op("dve", lambda e, ss=ss: e.tensor_scalar(out=ss[:, 1:2], in0=ss[:, 0:1], scalar1=1.0 / D, scalar2=EPS,
                                                     op0=ALU.mult, op1=ALU.add), [ss_r], [ss_r])
        P.op("act", lambda e, ss=ss: e.activation(out=ss[:, 2:3], in_=ss[:, 1:2], func=AF.Sqrt), [ss_r], [ss_r])
        P.op("dve", lambda e, ss=ss: e.reciprocal(out=ss[:, 3:4], in_=ss[:, 2:3]), [ss_r], [ss_r])

    def part_b(i):
        u0 = i * 128
        v = 2 if u0 < CTX else s
        (Ar, Ar_r), (Br, Br_r) = rows[v]
        x, x_r = xt[i % 3]
        ss, ss_r = st[i % 2]
        tt, tt_r = t1[i % 2]
        P.op("dve", lambda e, x=x, ss=ss, tt=tt, Ar=Ar: e.scalar_tensor_tensor(
            out=tt, in0=x, scalar=ss[:, 3:4], in1=Ar, op0=ALU.mult, op1=ALU.mult), [x_r, ss_r, Ar_r], [tt_r])
        h, h_r = hb[i % 2]
        P.op("dve", lambda e, tt=tt, h=h, Br=Br: e.tensor_tensor(out=h, in0=tt, in1=Br, op=ALU.add),
             [tt_r, Br_r], [h_r])
        _, psb, ps_r = k.ps[i % 2]
        for c in range(8):
            P.op("pe", lambda e, c=c, h=h, psb=psb: e.transpose(out=psb[:, c * 128:(c + 1) * 128],
                                                                 in_=h[:, c * 128:(c + 1) * 128], identity=k.identb),
                 [h_r, k.identb_r], [ps_r], inc=(c == 7))
        P.op("act", lambda e, psb=psb, u0=u0: e.activation(
            out=hxT[:, :, u0:u0 + 128], in_=psb.rearrange("p (c t) -> p c t", c=8), func=AF.Identity),
            [ps_r], [hxT_r])

    part_a1(0)
    part_a2(0)
    for i in range(nsub):
        if i + 1 < nsub:
            part_a1(i + 1)
        part_b(i)
        if i + 1 < nsub:
            part_a2(i + 1)
    P.barrier()
    A.top = mark


PB_CTX = 2
PB_LAT = 264
PB_W = 4368


def pb_pos(u):
    return PB_CTX + u if u < CTX else PB_LAT + (u - CTX)


def stage_A(k, l):
    P, A, dr = k.P, k.A, k.dr
    base = A.top
    hxT, hxT_r = A.alloc("hxT", [8, U], BF16)
    cw_l, cw_l_r = A.alloc("cw_l", [4, 4], F32)
    cw_s, cw_s_r = A.alloc("cw_s", [8, 4], F32)
    cw_h, cw_h_r = A.alloc("cw_h", [12, 3], F32)
    cb_l, cb_l_r = A.alloc("cb_l", [4], F32)
    cb_s, cb_s_r = A.alloc("cb_s", [8], F32)
    cb_h, cb_h_r = A.alloc("cb_h", [12], F32)
    dtb, dtb_r = A.alloc("dtb", [16], F32)
    sl = dict(allow_slow_non_contiguous=True)
    for (cw_, cw_r_, nm_, nt_) in ((cw_l, cw_l_r, "lru_conv_w", 4), (cw_s, cw_s_r, "ssm_conv_w", 4), (cw_h, cw_h_r, "hy_conv_w", 3)):
        for kk in range(nt_):
            P.dma("sp", cw_[:, :, kk], dr[nm_][l, kk].rearrange("(m p) -> p m", p=128), [], [cw_r_], cw_r_, **sl)
    P.dma("sp", cb_l, dr["lru_conv_b"][l].rearrange("(m p) -> p m", p=128), [], [cb_l_r], cb_l_r, **sl)
    P.dma("sp", cb_s, dr["ssm_conv_b"][l].rearrange("(m p) -> p m", p=128), [], [cb_s_r], cb_s_r, **sl)
    P.dma("sp", cb_h, dr["hy_conv_b"][l].rearrange("(m p) -> p m", p=128), [], [cb_h_r], cb_h_r, **sl)
    P.dma("sp", dtb, dr["ssm_dt_bias"][l:l + 1].rearrange("o a b -> o (a b)").partition_broadcast(128), [], [dtb_r], dtb_r)
    wbuf = [A.alloc("wbuf%d" % i, [8, 512], BF16) for i in range(2)]
    pb = [A.alloc("pb%d" % i, [PB_W], BF16) for i in range(2)]
    for t, r in pb:
        P.op("pool", lambda e, t=t: e.memset(t, 0.0), [], [r])
    stg = [A.alloc("stg%d" % i, [U], F32) for i in range(2)]
    diag = [A.alloc("diag%d" % i, [4, 128], BF16) for i in range(2)]
    tmst = [A.alloc("tmst%d" % i, [U // 128, 128], BF16) for i in range(2)]
    zst = [A.alloc("zst%d" % i, [512], BF16) for i in range(2)]
    dtst, dtst_r = A.alloc("dtst", [U // 128, 16], F32)
    dtt, dtt_r = A.alloc("dtt", [16], F32)
    tiles = seg_tiles()
    cnt = {"w": 0, "c": 0, "ps": 0}

    def ps_next():
        cnt["ps"] += 1
        return k.ps[2 + cnt["ps"] % 4]

    groups = [(0, "conv", ("XA", 0, cw_l, cw_l_r, cb_l, cb_l_r, 0, 4, 2, AF.Identity, F32))]
    groups += [(512 + 512 * g, "conv", ("XBC", 512 * g, cw_s, cw_s_r, cb_s, cb_s_r, 4 * g, 4, 2, AF.Silu, BF16)) for g in range(2)]
    groups += [(1552, "plain", ("GA", 0, AF.Gelu))]
    groups += [(2064 + 512 * g, "x12", (512 * g, cw_h, cw_h_r, cb_h, cb_h_r, 4 * g, 3, 1)) for g in (0, 1, 2)]
    groups += [(3600, "z", None)]
    groups += [(4112 + 512 * g, "plain", ("GATES", 512 * g, AF.Sigmoid)) for g in range(6)]
    groups += [(1536, "dt", None)]

    for s in range(NSEQ):
        norm_mod_T(k, l, s, hxT, hxT_r, 0, stream_src, "A")
        if "HXT" in k.dumps:
            P.dma("sp", dr["HXT"][s].rearrange("(c p) u -> p c u", p=128), hxT, [hxT_r], [k.dres["HXT"]], hxT_r)
        for (c0, kind, prm) in groups:
            w, w_r = wbuf[cnt["w"] % 2]
            cnt["w"] += 1
            ncol = 16 if kind == "dt" else 512
            P.dma("pool", w[:, :, 0:ncol], dr["w_in"][l, :, c0:c0 + ncol].rearrange("(c p) n -> p c n", p=128),
                  [], [w_r], w_r)
            if kind in ("conv", "plain", "x12"):
                for m in range(4):
                    ci = cnt["c"]
                    cnt["c"] += 1
                    sg32, sg_r = stg[ci % 2]
                    if kind == "plain":
                        name, r0, func = prm
                        sgb = sg32.bitcast(BF16)[:, 0:U]
                        for (u0, n) in tiles:
                            pst, _, ps_r = ps_next()
                            for c in range(8):
                                P.op("pe", lambda e, c=c, w=w, pst=pst, u0=u0, n=n, m=m: e.matmul(
                                    pst[:, 0:n], lhsT=w[:, c, m * 128:(m + 1) * 128], rhs=hxT[:, c, u0:u0 + n],
                                    start=(c == 0), stop=(c == 7)), [w_r, hxT_r], [ps_r], inc=(c == 7))
                            P.op("act", lambda e, pst=pst, u0=u0, n=n, sgb=sgb, func=func: e.activation(
                                out=sgb[:, u0:u0 + n], in_=pst[:, 0:n], func=func), [ps_r], [sg_r])
                        P.dma("sp", dr[name][s, r0 + m * 128:r0 + (m + 1) * 128, :], sgb, [sg_r], [k.dres[name]], sg_r)
                        continue
                    if kind == "conv":
                        name, r0, cw, cw_r, cb, cb_r, mb, ntap, padl, func, odt = prm
                    else:
                        r0, cw, cw_r, cb, cb_r, mb, ntap, padl = prm
                        func, odt = AF.Identity, BF16
                    pbt, pb_r = pb[ci % 2]
                    dg, dg_r = diag[ci % 2]
                    for kk in range(ntap):
                        P.op("dve", lambda e, kk=kk, dg=dg, cw=cw, mm=mb + m: e.tensor_scalar(
                            out=dg[:, kk, :], in0=k.identf, scalar1=cw[:, mm, kk:kk + 1], scalar2=None, op0=ALU.mult),
                            [k.identf_r, cw_r], [dg_r])
                    for (u0, n) in tiles:
                        pst, _, ps_r = ps_next()
                        for c in range(8):
                            P.op("pe", lambda e, c=c, w=w, pst=pst, u0=u0, n=n, m=m: e.matmul(
                                pst[:, 0:n], lhsT=w[:, c, m * 128:(m + 1) * 128], rhs=hxT[:, c, u0:u0 + n],
                                start=(c == 0), stop=(c == 7)), [w_r, hxT_r], [ps_r], inc=(c == 7))
                        P.op("act", lambda e, pst=pst, u0=u0, n=n, pbt=pbt: e.activation(
                            out=pbt[:, pb_pos(u0):pb_pos(u0) + n], in_=pst[:, 0:n], func=AF.Identity), [ps_r], [pb_r])
                    sgo = sg32 if odt == F32 else sg32.bitcast(BF16)[:, 0:U]
                    for (u0, n) in tiles:
                        pst, _, ps_r = ps_next()
                        for kk in range(ntap):
                            P.op("pe", lambda e, kk=kk, dg=dg, pbt=pbt, pst=pst, u0=u0, n=n, padl=padl: e.matmul(
                                pst[:, 0:n], lhsT=dg[:, kk, :], rhs=pbt[:, pb_pos(u0) + kk - padl:pb_pos(u0) + kk - padl + n],
                                start=(kk == 0), stop=(kk == ntap - 1)), [dg_r, pb_r], [ps_r], inc=(kk == ntap - 1))
                        P.op("act", lambda e, pst=pst, u0=u0, n=n, sgo=sgo, func=func, cb=cb, mm=mb + m: e.activation(
                            out=sgo[:, u0:u0 + n], in_=pst[:, 0:n], func=func, bias=cb[:, mm:mm + 1]),
                            [ps_r, cb_r], [sg_r])
                    if kind == "conv":
                        P.dma("sp", dr[name][s, r0 + m * 128:r0 + (m + 1) * 128, :], sgo, [sg_r], [k.dres[name]], sg_r)
                    else:
                        tm, tm_r = tmst[ci % 2]
                        for i0 in range(0, U // 128, 4):
                            nn = min(4, U // 128 - i0)
                            _, psb, ps_r = ps_next()
                            for j in range(nn):
                                P.op("pe", lambda e, j=j, i0=i0, psb=psb, sgo=sgo: e.transpose(
                                    out=psb[:, j * 128:(j + 1) * 128], in_=sgo[:, (i0 + j) * 128:(i0 + j + 1) * 128],
                                    identity=k.identb), [sg_r, k.identb_r], [ps_r], inc=(j == nn - 1))
                            P.op("dve", lambda e, i0=i0, nn=nn, psb=psb, tm=tm: e.tensor_copy(
                                out=tm[:, i0:i0 + nn, :], in_=psb[:, 0:nn * 128].rearrange("p (a b) -> p a b", a=nn)),
                                [ps_r], [tm_r])
                        P.dma("sp", dr["X12"][s, :, r0 + m * 128:r0 + (m + 1) * 128].rearrange("(i p) c -> p i c", p=128),
                              tm, [tm_r], [k.dres["X12"]], tm_r)
            elif kind == "z":
                for i in range(U // 128):
                    pst, _, ps_r = ps_next()
                    for c in range(8):
                        P.op("pe", lambda e, c=c, w=w, pst=pst, i=i: e.matmul(
                            pst, lhsT=hxT[:, c, i * 128:(i + 1) * 128], rhs=w[:, c, :], start=(c == 0), stop=(c == 7)),
                            [w_r, hxT_r], [ps_r], inc=(c == 7))
                    zt, zt_r = zst[i % 2]
                    P.op("act", lambda e, pst=pst, zt=zt: e.activation(out=zt, in_=pst, func=AF.Silu), [ps_r], [zt_r])
                    P.dma("sp", dr["ZS"][s, i * 128:(i + 1) * 128, :], zt, [zt_r], [k.dres["ZS"]], zt_r)
            elif kind == "dt":
                for i in range(U // 128):
                    pst, _, ps_r = ps_next()
                    for c in range(8):
                        P.op("pe", lambda e, c=c, w=w, pst=pst, i=i: e.matmul(
                            pst[:, 0:16], lhsT=hxT[:, c, i * 128:(i + 1) * 128], rhs=w[:, c, 0:16], start=(c == 0),
                            stop=(c == 7)), [w_r, hxT_r], [ps_r], inc=(c == 7))
                    P.op("dve", lambda e, pst=pst: e.tensor_tensor(out=dtt, in0=pst[:, 0:16], in1=dtb, op=ALU.add),
                         [ps_r, dtb_r], [dtt_r])
                    P.op("act", lambda e: e.activation(out=dtt, in_=dtt, func=AF.Exp), [dtt_r], [dtt_r])
                    P.op("act", lambda e, i=i: e.activation(out=dtst[:, i, :], in_=dtt, func=AF.Ln, bias=1.0),
                         [dtt_r], [dtst_r])
                P.dma("sp", dr["DT"][s].rearrange("(i p) h -> p i h", p=128), dtst, [dtst_r], [k.dres["DT"]], dtst_r)
    A.top = base


def stage_L(k, l):
    P, A, dr = k.P, k.A, k.dr
    base = A.top
    sl = dict(allow_slow_non_contiguous=True)
    tiles = seg_tiles()
    T = [A.alloc("LT%d" % i, [U], F32) for i in range(6)]
    xab, xab_r = A.alloc("xab", [U], BF16)
    gab, gab_r = A.alloc("gab", [U], BF16)
    yab, yab_r = A.alloc("yab", [U], BF16)
    bd = [A.alloc("bd%d" % i, [128], BF16) for i in range(2)]
    cols, cols_r = A.alloc("lcols", [2, 4, 4], F32)
    for d in range(2):
        for j, nm in enumerate(("lru_b_a", "lru_b_i", "lru_lambda")):
            P.dma("sp", cols[:, d, :, j], dr[nm][l, d].rearrange("(m p) -> p m", p=128), [], [cols_r], cols_r, **sl)
    for d in range(2):
        P.op("act", lambda e, d=d: e.activation(out=cols[:, d, :, 3], in_=cols[:, d, :, 2], func=AF.Exp, scale=-1.0),
             [cols_r], [cols_r])
        P.op("act", lambda e, d=d: e.activation(out=cols[:, d, :, 3], in_=cols[:, d, :, 3], func=AF.Ln, bias=1.0),
             [cols_r], [cols_r])
        P.op("dve", lambda e, d=d: e.tensor_scalar(out=cols[:, d, :, 3], in0=cols[:, d, :, 3], scalar1=-8.0, scalar2=None,
                                                   op0=ALU.mult), [cols_r], [cols_r])
    cnt = {"ps": 0}

    def ps_next():
        cnt["ps"] += 1
        return k.ps[cnt["ps"] % 4]

    for s in range(NSEQ):
        for m in range(4):
            xa, xa_r = T[0]
            hs, hs_r = T[5]
            P.dma("sp", xa, dr["XA"][s, m * 128:(m + 1) * 128, :], [k.dres["XA"]], [xa_r], xa_r)
            P.dma("sp", gab, dr["GA"][s, m * 128:(m + 1) * 128, :], [k.dres["GA"]], [gab_r], gab_r)
            P.op("pool", lambda e, xa=xa: e.tensor_copy(out=xab, in_=xa), [xa_r], [xab_r])
            for d in range(2):
                gates = []
                for gi, (wn, bj) in enumerate((("lru_w_a", 0), ("lru_w_i", 1))):
                    b_, b_r = bd[gi]
                    P.op("pool", lambda e, b_=b_: e.memset(b_, 0.0), [], [b_r])
                    for h in range(2):
                        P.dma("pool", b_[h * 64:(h + 1) * 64, h * 64:(h + 1) * 64], dr[wn][l, d, 2 * m + h], [], [b_r], b_r)
                    g_, g_r = T[1 + gi]
                    for (u0, n) in tiles:
                        pst, _, ps_r = ps_next()
                        P.op("pe", lambda e, b_=b_, pst=pst, u0=u0, n=n: e.matmul(pst[:, 0:n], lhsT=b_, rhs=xab[:, u0:u0 + n],
                                                                                    start=True, stop=True), [b_r, xab_r], [ps_r])
                        P.op("act", lambda e, pst=pst, u0=u0, n=n, g_=g_, bj=bj, d=d, m=m: e.activation(
                            out=g_[:, u0:u0 + n], in_=pst[:, 0:n], func=AF.Sigmoid, bias=cols[:, d, m, bj:bj + 1]),
                            [ps_r, cols_r], [g_r])
                    gates.append((g_, g_r))
                (r_, r_r), (i_, i_r) = gates
                a_, a_r = T[3]
                q_, q_r = T[4]
                P.op("act", lambda e, d=d, m=m: e.activation(out=a_, in_=r_, func=AF.Exp, scale=cols[:, d, m, 3:4]),
                     [r_r, cols_r], [a_r])
                P.op("dve", lambda e: e.tensor_tensor(out=q_, in0=a_, in1=a_, op=ALU.mult), [a_r], [q_r])
                P.op("act", lambda e: e.activation(out=q_, in_=q_, func=AF.Sqrt, scale=-1.0, bias=1.0), [q_r], [q_r])
                P.op("pool", lambda e, xa=xa: e.tensor_tensor(out=i_, in0=i_, in1=xa, op=ALU.mult), [i_r, xa_r], [i_r])
                P.op("dve", lambda e: e.tensor_tensor(out=q_, in0=q_, in1=i_, op=ALU.mult), [q_r, i_r], [q_r])
                h_, h_r = r_, r_r
                if d == 0:
                    P.op("dve", lambda e: e.tensor_tensor_scan(out=hs, data0=a_, data1=q_, initial=0.0, op0=ALU.mult,
                                                               op1=ALU.add), [a_r, q_r], [hs_r])
                    if "DBG1" in k.dumps:
                        P.dma("sp", dr["DBG1"][s, m * 128:(m + 1) * 128, :], hs, [hs_r], [k.dres["DBG1"]], hs_r)
                        P.dma("sp", dr["DBG3"][s, m * 128:(m + 1) * 128, :], a_, [a_r], [k.dres["DBG3"]], a_r)
                        P.dma("sp", dr["DBG4"][s, m * 128:(m + 1) * 128, :], q_, [q_r], [k.dres["DBG4"]], q_r)
                else:
                    def rev(ap, lo, n):
                        return bass.AP(ap.tensor, ap.offset + lo + n - 1, [list(ap.ap[0]), [-1, n]])
                    P.op("dve", lambda e: e.tensor_tensor_scan(out=rev(h_, 0, CTX), data0=rev(a_, 0, CTX), data1=rev(q_, 0, CTX),
                                                               initial=0.0, op0=ALU.mult, op1=ALU.add), [a_r, q_r], [h_r])
                    P.op("dve", lambda e: e.tensor_tensor_scan(out=rev(h_, CTX, SEQ), data0=rev(a_, CTX, SEQ),
                                                               data1=rev(q_, CTX, SEQ), initial=h_[:, 0:1], op0=ALU.mult,
                                                               op1=ALU.add), [a_r, q_r, h_r], [h_r])
                    if "DBG2" in k.dumps:
                        P.dma("sp", dr["DBG2"][s, m * 128:(m + 1) * 128, :], h_, [h_r], [k.dres["DBG2"]], h_r)
                    P.op("pool", lambda e: e.tensor_tensor(out=hs, in0=hs, in1=h_, op=ALU.add), [hs_r, h_r], [hs_r])
            P.op("dve", lambda e: e.tensor_tensor(out=yab, in0=hs, in1=gab, op=ALU.mult), [hs_r, gab_r], [yab_r])
            P.dma("sp", dr["YA"][s, m * 128:(m + 1) * 128, :], yab, [yab_r], [k.dres["YA"]], yab_r)
    A.top = base


def bc(ap, dims):
    return bass.AP(ap.tensor, ap.offset, [list(ap.ap[0])] + [list(d) for d in dims])


def stage_S(k, l, banks=None):
    P, A, dr = k.P, k.A, k.dr
    base = A.top
    tri = []
    for nm in ("tri_f", "tri_b"):
        t, r = A.alloc(nm, [128], F32)
        P.dma("sp", t, dr[nm], [], [r], r)
        tri.append((t, r))
    mneg = []
    for nm in ("mneg_f", "mneg_b"):
        t, r = A.alloc(nm, [128], F32)
        P.dma("sp", t, dr[nm], [], [r], r)
        mneg.append((t, r))
    ones, ones_r = A.alloc("ones", [128], F32)
    P.op("pool", lambda e: e.memset(ones, 1.0), [], [ones_r])
    arow, arow_r = A.alloc("arow", [16], F32)
    drow, drow_r = A.alloc("drow", [8], F32)
    nrow, nrow_r = A.alloc("nrow", [BW], F32)
    P.dma("sp", arow, dr["ssm_a_log"][l:l + 1].rearrange("o a b -> o (a b)").partition_broadcast(128), [], [arow_r], arow_r)
    P.op("act", lambda e: e.activation(out=arow, in_=arow, func=AF.Exp), [arow_r], [arow_r])
    P.op("dve", lambda e: e.tensor_scalar(out=arow, in0=arow, scalar1=-1.0, scalar2=None, op0=ALU.mult), [arow_r], [arow_r])
    P.dma("sp", drow, dr["ssm_d"][l:l + 1].partition_broadcast(128), [], [drow_r], drow_r)
    P.dma("sp", nrow, dr["ssm_norm"][l:l + 1].partition_broadcast(128), [], [nrow_r], nrow_r)
    NCH = 2 if banks is None else 1
    Hs = [(A.alloc("H%d" % c, [8, 64], F32), A.alloc("Hb%d" % c, [8, 64], BF16)) for c in range(NCH)]
    NSL = 4 if banks is None else 2

    def many(name, shape, dt):
        return [A.alloc(name + str(i), shape, dt) for i in range(NSL)]
    T = {}
    for nm, shp, dt in (("xbc", [8, 128], BF16), ("dtc", [16], F32), ("xtok", [8, 64], BF16), ("btok", [2, 128], BF16),
                        ("acol", [8], F32), ("xdt", [8, 64], BF16), ("rhs2", [8, 128], F32), ("csc", [8], F32),
                        ("csr", [8, 128], F32), ("Dm", [8, 128], F32), ("M", [8, 128], BF16), ("Ecs", [8, 128], F32),
                        ("CTs", [8, 128], BF16), ("dec", [8], F32), ("xdd", [8, 64], BF16), ("ysum", [BW], F32),
                        ("zs", [BW], BF16), ("sst", [4], F32), ("yc", [BW], BF16), ("ycT", [4, 128], BF16)):
        T[nm] = many(nm, shp, dt)
    sq, sq_r = A.alloc("ssq", [BW], F32)
    PSC = []
    for ch_ in range(NCH):
        ia, ie, ic = (0 + ch_, 2 + ch_, 4 + ch_) if banks is None else banks[0:3]
        b0f, b0b, rA = k.ps[ia]
        b1f, b1b, rE = k.ps[ie]
        PSC.append(dict(psA=b0b, psA_r=rA, psB=b0f[:, 384:392], psB_r=rA, psE=b1f[:, 0:256], psE_r=rE,
                        psT=b1b[:, 512:1024], psT_r=rE, psC=k.ps[ic][0], psC_r=k.ps[ic][2]))
    iy, isb = (6, 7) if banks is None else banks[3:5]

    def p1(s, d, ci, sl):
        pc = PSC[s if NCH == 2 else 0]
        psA, psA_r, psB, psB_r, psE, psE_r = pc["psA"], pc["psA_r"], pc["psB"], pc["psB_r"], pc["psE"], pc["psE_r"]
        trd, trd_r = tri[d]
        qe = 127 if d == 0 else 0
        u0 = ci * 128
        g = {nm: T[nm][sl] for nm in T}
        xbc, xbc_r = g["xbc"]
        dtc, dtc_r = g["dtc"]
        P.dma("sp", xbc, dr["XBC"][s, :, u0:u0 + 128].rearrange("(c p) u -> p c u", p=128), [k.dres["XBC"]], [xbc_r], xbc_r)
        yield
        P.dma("sp", dtc, dr["DT"][s, u0:u0 + 128, :], [k.dres["DT"]], [dtc_r], dtc_r)
        yield
        for c in range(6):
            P.op("pe", lambda e, c=c: e.transpose(out=psA[:, c * 128:(c + 1) * 128], in_=xbc[:, c, :], identity=k.identb),
                 [xbc_r, k.identb_r], [psA_r], inc=(c == 5))
            yield
        xtok, xtok_r = g["xtok"]
        btok, btok_r = g["btok"]
        P.op("act", lambda e: e.activation(out=xtok, in_=psA[:, 0:512].rearrange("p (h q) -> p h q", h=8), func=AF.Identity),
             [psA_r], [xtok_r])
        yield
        P.op("act", lambda e: e.activation(out=btok, in_=psA[:, 512:768].rearrange("p (h q) -> p h q", h=2), func=AF.Identity),
             [psA_r], [btok_r])
        yield
        acol, acol_r = g["acol"]
        P.op("dve", lambda e: e.tensor_tensor(out=acol, in0=dtc[:, 8 * d:8 * d + 8], in1=arow[:, 8 * d:8 * d + 8], op=ALU.mult),
             [dtc_r, arow_r], [acol_r])
        yield
        xdt, xdt_r = g["xdt"]
        P.op("dve", lambda e: e.tensor_tensor(out=xdt, in0=xtok, in1=bc(dtc[:, 8 * d:8 * d + 8], [[1, 8], [0, 64]]), op=ALU.mult),
             [xtok_r, dtc_r], [xdt_r])
        yield
        rhs2, rhs2_r = g["rhs2"]
        P.op("dve", lambda e: e.tensor_tensor(out=rhs2, in0=bc(acol, [[1, 8], [0, 128]]), in1=bc(trd, [[0, 8], [1, 128]]), op=ALU.mult),
             [acol_r, trd_r], [rhs2_r])
        yield
        P.op("pe", lambda e: e.matmul(psB, lhsT=trd, rhs=acol, start=True, stop=True), [trd_r, acol_r], [psB_r])
        yield
        csc, csc_r = g["csc"]
        P.op("dve", lambda e: e.tensor_copy(out=csc, in_=psB), [psB_r], [csc_r])
        yield
        csr, csr_r = g["csr"]
        for hh in range(2):
            psC, psC_r = pc["psC"], pc["psC_r"]
            P.op("pe", lambda e, psC=psC, hh=hh: e.matmul(psC, lhsT=ones, rhs=rhs2[:, 4 * hh:4 * hh + 4, :], start=True, stop=True),
                 [ones_r, rhs2_r], [psC_r])
            yield
            P.op("act", lambda e, psC=psC, hh=hh: e.activation(out=csr[:, 4 * hh:4 * hh + 4, :], in_=psC.rearrange("p (h q) -> p h q", h=4),
                                                             func=AF.Identity), [psC_r], [csr_r])
            yield
        Dm, Dm_r = g["Dm"]
        P.op("dve", lambda e: e.tensor_tensor(out=Dm, in0=csr, in1=bc(csc, [[1, 8], [0, 128]]), op=ALU.subtract), [csr_r, csc_r], [Dm_r])
        yield
        P.op("dve", lambda e: e.tensor_tensor(out=Dm, in0=Dm, in1=bc(mneg[d][0], [[0, 8], [1, 128]]), op=ALU.add), [Dm_r, mneg[d][1]], [Dm_r])
        yield
        P.op("act", lambda e: e.activation(out=Dm, in_=Dm, func=AF.Exp), [Dm_r], [Dm_r])
        yield
        for gg in range(2):
            P.op("pe", lambda e, gg=gg: e.matmul(psE[:, gg * 128:(gg + 1) * 128], lhsT=xbc[:, 4 + gg, :], rhs=xbc[:, 6 + gg, :],
                                                 start=True, stop=True), [xbc_r], [psE_r], inc=(gg == 1))
            yield
        M, M_r = g["M"]
        for gg in range(2):
            P.op("dve", lambda e, gg=gg: e.tensor_tensor(out=M[:, 4 * gg:4 * gg + 4, :], in0=Dm[:, 4 * gg:4 * gg + 4, :],
                                                         in1=bc(psE[:, gg * 128:(gg + 1) * 128], [[0, 4], [1, 128]]), op=ALU.mult),
                 [Dm_r, psE_r], [M_r])
            yield
        Ecs, Ecs_r = g["Ecs"]
        P.op("act", lambda e: e.activation(out=Ecs, in_=csr, func=AF.Exp), [csr_r], [Ecs_r])
        yield
        CTs, CTs_r = g["CTs"]
        for gg in range(2):
            P.op("dve", lambda e, gg=gg: e.tensor_tensor(out=CTs[:, 4 * gg:4 * gg + 4, :], in0=Ecs[:, 4 * gg:4 * gg + 4, :],
                                                          in1=bc(xbc[:, 6 + gg, :], [[0, 4], [1, 128]]), op=ALU.mult),
                 [Ecs_r, xbc_r], [CTs_r])
            yield
        dec, dec_r = g["dec"]
        P.op("dve", lambda e: e.tensor_tensor(out=dec, in0=csr[:, :, qe], in1=csc, op=ALU.subtract), [csr_r, csc_r], [dec_r])
        yield
        P.op("act", lambda e: e.activation(out=dec, in_=dec, func=AF.Exp), [dec_r], [dec_r])
        yield
        xdd, xdd_r = g["xdd"]
        P.op("dve", lambda e: e.tensor_tensor(out=xdd, in0=xdt, in1=bc(dec, [[1, 8], [0, 64]]), op=ALU.mult), [xdt_r, dec_r], [xdd_r])
        yield
        if d == 1:
            ysum, ysum_r = g["ysum"]
            zs, zs_r = g["zs"]
            P.dma("sp", ysum, dr["YS"][s, u0:u0 + 128, :], [k.dres["YS"]], [ysum_r], ysum_r)
            yield
            P.dma("sp", zs, dr["ZS"][s, u0:u0 + 128, :], [k.dres["ZS"]], [zs_r], zs_r)
            yield

    def p2(s, d, ci, sl, chain):
        pc = PSC[chain]
        psT, psT_r = pc["psT"], pc["psT_r"]
        qe = 127 if d == 0 else 0
        u0 = ci * 128
        g = {nm: T[nm][sl] for nm in T}
        (H, H_r), (Hb, Hb_r) = Hs[chain]
        CTs, CTs_r = g["CTs"]
        Ecs, Ecs_r = g["Ecs"]
        xtok, xtok_r = g["xtok"]
        psY, _, psY_r = k.ps[iy]
        psS, _, psS_r = k.ps[isb]
        M, M_r = g["M"]
        xdt, xdt_r = g["xdt"]
        xdd, xdd_r = g["xdd"]
        btok, btok_r = g["btok"]
        for gg in range(2):
            P.op("pe", lambda e, gg=gg: e.matmul(psS[:, 256 * gg:256 * gg + 256], lhsT=btok[:, gg, :],
                                                 rhs=xdd[:, 4 * gg:4 * gg + 4, :], start=True, stop=True),
                 [btok_r, xdd_r], [psS_r], inc=(gg == 1))
            yield
        for h in range(8):
            P.op("pe", lambda e, h=h: e.matmul(psY[:, 64 * h:64 * h + 64], lhsT=M[:, h, :], rhs=xdt[:, h, :],
                                               start=True, stop=False), [M_r, xdt_r], [psY_r], inc=False)
            yield
            P.op("pe", lambda e, h=h: e.matmul(psY[:, 64 * h:64 * h + 64], lhsT=CTs[:, h, :], rhs=Hb[:, h, :], start=False,
                                               stop=True), [CTs_r, Hb_r], [psY_r], inc=(h == 7))
            yield
        P.op("dve", lambda e: e.tensor_tensor(out=H, in0=H, in1=bc(Ecs[:, :, qe], [[128, 8], [0, 64]]), op=ALU.mult), [H_r, Ecs_r], [H_r])
        yield
        P.op("dve", lambda e: e.tensor_tensor(out=H, in0=H, in1=psS.rearrange("p (h q) -> p h q", h=8), op=ALU.add), [H_r, psS_r], [H_r])
        yield
        P.op("act", lambda e: e.activation(out=Hb, in_=H, func=AF.Identity), [H_r], [Hb_r])
        yield
        ysum, ysum_r = g["ysum"]
        if d == 0:
            P.op("dve", lambda e: e.tensor_tensor(out=ysum.rearrange("p (h q) -> p h q", h=8), in0=xtok, in1=bc(drow, [[1, 8], [0, 64]]),
                                                   op=ALU.mult), [xtok_r, drow_r], [ysum_r])
            yield
            P.op("dve", lambda e: e.tensor_tensor(out=ysum, in0=ysum, in1=psY, op=ALU.add), [ysum_r, psY_r], [ysum_r])
            yield
            P.dma("sp", dr["YS"][s, u0:u0 + 128, :], ysum, [ysum_r], [k.dres["YS"]], ysum_r)
            yield
        else:
            zs, zs_r = g["zs"]
            P.op("dve", lambda e: e.tensor_tensor(out=ysum, in0=ysum, in1=psY, op=ALU.add), [ysum_r, psY_r], [ysum_r])
            yield
            P.op("dve", lambda e: e.tensor_tensor(out=ysum, in0=ysum, in1=zs, op=ALU.mult), [ysum_r, zs_r], [ysum_r])
            yield
            ss, ss_r = g["sst"]
            P.op("act", lambda e: e.activation(out=sq, in_=ysum, func=AF.Square, accum_out=ss[:, 0:1]), [ysum_r], [sq_r, ss_r])
            yield
            P.op("dve", lambda e: e.tensor_scalar(out=ss[:, 1:2], in0=ss[:, 0:1], scalar1=1.0 / BW, scalar2=EPS, op0=ALU.mult,
                                                  op1=ALU.add), [ss_r], [ss_r])
            yield
            P.op("act", lambda e: e.activation(out=ss[:, 2:3], in_=ss[:, 1:2], func=AF.Sqrt), [ss_r], [ss_r])
            yield
            P.op("dve", lambda e: e.reciprocal(out=ss[:, 3:4], in_=ss[:, 2:3]), [ss_r], [ss_r])
            yield
            yc, yc_r = g["yc"]
            P.op("dve", lambda e: e.scalar_tensor_tensor(out=yc, in0=ysum, scalar=ss[:, 3:4], in1=nrow, op0=ALU.mult, op1=ALU.mult),
                 [ysum_r, ss_r, nrow_r], [yc_r])
            yield
            for c in range(4):
                P.op("pe", lambda e, c=c: e.transpose(out=psT[:, c * 128:(c + 1) * 128], in_=yc[:, c * 128:(c + 1) * 128],
                                                      identity=k.identb), [yc_r, k.identb_r], [psT_r], inc=(c == 3))
                yield
            ycT, ycT_r = g["ycT"]
            P.op("act", lambda e: e.activation(out=ycT, in_=psT.rearrange("p (c q) -> p c q", c=4), func=AF.Identity), [psT_r], [ycT_r])
            yield
            P.dma("sp", dr["YC"][s, :, u0:u0 + 128].rearrange("(c p) u -> p c u", p=128), ycT, [ycT_r], [k.dres["YC"]], ycT_r)
            yield

    def lockstep(gens):
        gens = list(gens)
        while gens:
            for g_ in list(gens):
                try:
                    next(g_)
                except StopIteration:
                    gens.remove(g_)

    for d in range(2):
        order = list(range(U // 128)) if d == 0 else [1, 0] + list(range(U // 128 - 1, 1, -1))
        if banks is None:
            for c in range(NCH):
                (H, H_r), (Hb, Hb_r) = Hs[c]
                P.op("pool", lambda e, H=H: e.memset(H, 0.0), [], [H_r])
                P.op("pool", lambda e, Hb=Hb: e.memset(Hb, 0.0), [], [Hb_r])
            prev = None
            for idx, ci in enumerate(order):
                sls = [2 * s + idx % 2 for s in range(NSEQ)]
                lockstep([p1(s, d, ci, sls[s]) for s in range(NSEQ)])
                if prev is not None:
                    for pv in prev:
                        for _ in p2(*pv):
                            pass
                prev = [(s, d, ci, sls[s], s) for s in range(NSEQ)]
                yield
            for pv in prev:
                for _ in p2(*pv):
                    pass
        else:
            for s in range(NSEQ):
                (H, H_r), (Hb, Hb_r) = Hs[0]
                P.op("pool", lambda e, H=H: e.memset(H, 0.0), [], [H_r])
                P.op("pool", lambda e, Hb=Hb: e.memset(Hb, 0.0), [], [Hb_r])
                prev = None
                for idx, ci in enumerate(order):
                    for _ in p1(s, d, ci, idx % 2):
                        yield
                    if prev is not None:
                        for _ in p2(*prev):
                            yield
                    prev = (s, d, ci, idx % 2, 0)
                for _ in p2(*prev):
                    yield
    A.top = base


def xs_src(which):
    def f(k, l, s, u0, n):
        return k.dr["XS"][which, s, u0:u0 + n, :], k.dres["XS"]
    return f


def norm_residual(k, pss, xsrc, xsres, grow, grow_r, dsts, bufs, i):
    P = k.P
    (xt, xt_r), (ss, ss_r), (sq, sq_r), (o, o_r) = bufs[i % 2]
    P.dma("sp", xt, xsrc, [xsres] if xsres else [], [xt_r], xt_r)
    for h in range(2):
        P.op("act", lambda e, h=h, ss=ss, sq=sq: e.activation(out=sq, in_=pss[h][0], func=AF.Square, accum_out=ss[:, h:h + 1]),
             [pss[h][2]], [sq_r, ss_r])
    P.op("dve", lambda e, ss=ss: e.tensor_tensor(out=ss[:, 2:3], in0=ss[:, 0:1], in1=ss[:, 1:2], op=ALU.add), [ss_r], [ss_r])
    P.op("dve", lambda e, ss=ss: e.tensor_scalar(out=ss[:, 3:4], in0=ss[:, 2:3], scalar1=1.0 / D, scalar2=EPS, op0=ALU.mult,
                                                 op1=ALU.add), [ss_r], [ss_r])
    P.op("act", lambda e, ss=ss: e.activation(out=ss[:, 4:5], in_=ss[:, 3:4], func=AF.Sqrt), [ss_r], [ss_r])
    P.op("dve", lambda e, ss=ss: e.reciprocal(out=ss[:, 5:6], in_=ss[:, 4:5]), [ss_r], [ss_r])
    for h in range(2):
        P.op("dve", lambda e, h=h, ss=ss, o=o: e.scalar_tensor_tensor(
            out=o[:, h * 512:(h + 1) * 512], in0=pss[h][0], scalar=ss[:, 5:6], in1=grow[:, h * 512:(h + 1) * 512],
            op0=ALU.mult, op1=ALU.mult), [pss[h][2], ss_r, grow_r], [o_r])
    P.op("pool", lambda e, o=o, xt=xt: e.tensor_tensor(out=o, in0=o, in1=xt, op=ALU.add), [o_r, xt_r], [o_r])
    for (dst, dres) in dsts:
        P.dma("sp", dst, o, [o_r], [dres], o_r)


def nr_bufs(k):
    A = k.A
    return [(A.alloc("nrx%d" % i, [D], F32), A.alloc("nrs%d" % i, [8], F32), A.alloc("nrq%d" % i, [512], F32),
             A.alloc("nro%d" % i, [D], F32)) for i in range(2)]


def stage_M(k, l):
    P, A, dr = k.P, k.A, k.dr
    base = A.top
    wb, wb_r = A.alloc("wb", [12, D], BF16)
    wo, wo_r = A.alloc("wo", [8, D], BF16)
    for kb in range(3):
        P.dma("pool", wb[:, 4 * kb:4 * kb + 4, :], dr["w_branch"][l, kb].rearrange("(c p) n -> p c n", p=128), [], [wb_r], wb_r)
    P.dma("pool", wo, dr["w_out"][l].rearrange("(c p) n -> p c n", p=128), [], [wo_r], wo_r)
    grows = {v: load_modrow(k, l, v, 2, "G1row%d" % v) for v in range(3)}
    yb2 = [[A.alloc("my%d_%d" % (kb, i), [4, 512], BF16) for kb in range(3)] for i in range(2)]
    gt2 = [A.alloc("mg%d" % i, [24, 512], BF16) for i in range(2)]
    mg2 = [A.alloc("mm%d" % i, [8, 512], BF16) for i in range(2)]
    t1, t1_r = A.alloc("mt1", [512], F32)
    t2, t2_r = A.alloc("mt2", [512], F32)
    bufs = nr_bufs(k)
    names = ("YA", "YB", "YC")
    it = 0
    sub = 0
    last = (l == DEPTH - 1)
    subc = [0]

    def p1(s, u0, n, i2):
        ys = yb2[i2]
        for kb in range(3):
            P.dma("sp", ys[kb][0][:, :, 0:n], dr[names[kb]][s, :, u0:u0 + n].rearrange("(c p) u -> p c u", p=128),
                  [k.dres[names[kb]]], [ys[kb][1]], ys[kb][1])
        gt, gt_r = gt2[i2]
        for kb in range(3):
            P.dma("sp", gt[:, 8 * kb:8 * kb + 8, 0:n], dr["GATES"][s, kb * D:(kb + 1) * D, u0:u0 + n].rearrange("(c p) u -> p c u", p=128),
                  [k.dres["GATES"]], [gt_r], gt_r)
        mg, mg_r = mg2[i2]
        for j in range(8):
            pks = [k.ps[(3 * j + kb) % 4] for kb in range(3)]
            for kb in range(3):
                pst, _, ps_r = pks[kb]
                for c in range(4):
                    P.op("pe", lambda e, c=c, kb=kb, j=j, pst=pst, y=ys[kb][0], n=n: e.matmul(
                        pst[:, 0:n], lhsT=wb[:, 4 * kb + c, j * 128:(j + 1) * 128], rhs=y[:, c, 0:n], start=(c == 0),
                        stop=(c == 3)), [wb_r, ys[kb][1]], [ps_r], inc=(c == 3))
            P.op("dve", lambda e, j=j, gt=gt, p0=pks[0][0], n=n: e.tensor_tensor(out=t1[:, 0:n], in0=gt[:, j, 0:n], in1=p0[:, 0:n],
                                                                              op=ALU.mult), [gt_r, pks[0][2]], [t1_r])
            P.op("dve", lambda e, j=j, gt=gt, p1=pks[1][0], n=n: e.tensor_tensor(out=t2[:, 0:n], in0=gt[:, 8 + j, 0:n], in1=p1[:, 0:n],
                                                                              op=ALU.mult), [gt_r, pks[1][2]], [t2_r])
            P.op("dve", lambda e, n=n: e.tensor_tensor(out=t1[:, 0:n], in0=t1[:, 0:n], in1=t2[:, 0:n], op=ALU.add), [t1_r, t2_r], [t1_r])
            P.op("dve", lambda e, j=j, gt=gt, p2=pks[2][0], n=n: e.tensor_tensor(out=t2[:, 0:n], in0=gt[:, 16 + j, 0:n], in1=p2[:, 0:n],
                                                                              op=ALU.mult), [gt_r, pks[2][2]], [t2_r])
            P.op("dve", lambda e, j=j, mg=mg, n=n: e.tensor_tensor(out=mg[:, j, 0:n], in0=t1[:, 0:n], in1=t2[:, 0:n], op=ALU.add),
                 [t1_r, t2_r], [mg_r])

    def p2(s, u0, n, i2):
        mg, mg_r = mg2[i2]
        for i in range(n // 128):
            pss = [k.ps[4 + 2 * (subc[0] % 2)], k.ps[5 + 2 * (subc[0] % 2)]]
            for h in range(2):
                for c in range(8):
                    P.op("pe", lambda e, c=c, h=h, i=i, mg=mg, pst=pss[h][0]: e.matmul(
                        pst, lhsT=mg[:, c, i * 128:(i + 1) * 128], rhs=wo[:, c, h * 512:(h + 1) * 512], start=(c == 0),
                        stop=(c == 7)), [mg_r, wo_r], [pss[h][2]], inc=(c == 7))
            uu = u0 + i * 128
            v = 2 if uu < CTX else s
            xsrc, xsres = stream_src(k, l, s, uu, 128)
            norm_residual(k, pss, xsrc, xsres, grows[v][0], grows[v][1], [(dr["XS"][0, s, uu:uu + 128, :], k.dres["XS"])], bufs, subc[0])
            subc[0] += 1

    tl = [(s, u0, n) for s in range(NSEQ) for (u0, n) in seg_tiles() if not (last and u0 < CTX)]
    prev = None
    for it_, (s, u0, n) in enumerate(tl):
        p1(s, u0, n, it_ % 2)
        if prev is not None:
            p2(*prev)
        prev = (s, u0, n, it_ % 2)
    p2(*prev)
    A.top = base


FC_CTX = 1
FC_LAT = 258 + 65
FC_W = 258 + 65 + SEQ + 65


def fc_pos(u):
    return FC_CTX + u if u < CTX else FC_LAT + (u - CTX)


def stage_F(k, l):
    P, A, dr = k.P, k.A, k.dr
    last = (l == DEPTH - 1)
    base = A.top
    sl = dict(allow_slow_non_contiguous=True)
    hxT, hxT_r = A.alloc("hxT2", [8, U], BF16)
    cwf, cwf_r = A.alloc("cwf", [9, 44], F32)
    cbf, cbf_r = A.alloc("cbf", [44], F32)
    for t9 in range(9):
        P.dma("sp", cwf[:, t9, :], dr["ffn_conv_w"][l, t9 // 3, t9 % 3].rearrange("(m p) -> p m", p=128), [], [cwf_r], cwf_r, **sl)
    P.dma("sp", cbf, dr["ffn_conv_b"][l].rearrange("(m p) -> p m", p=128), [], [cbf_r], cbf_r, **sl)
    tiles = [t for t in seg_tiles() if not (last and t[0] < CTX)]
    for s in range(NSEQ):
        mark = A.top
        norm_mod_T(k, l, s, hxT, hxT_r, 3, xs_src(0), "F")
        wu2 = [A.alloc("wu%d" % i, [2, 8, 128], BF16) for i in range(2)]
        UBW = 258 + 66 * 66
        ub = [[A.alloc("ub%d_%d" % (vg, t), [UBW], BF16) for t in range(2)] for vg in range(2)]
        for vg in range(2):
            for t in range(2):
                P.op("pool", lambda e, b=ub[vg][t][0]: e.memset(b, 0.0), [], [ub[vg][t][1]])
        dg2 = [A.alloc("fdg%d" % i, [18, 128], BF16) for i in range(2)]
        gg2 = [A.alloc("fgg%d" % i, [512], F32) for i in range(2)]
        ac2 = [A.alloc("fac%d" % i, [U], BF16) for i in range(2)]
        pcount = [0]

        def ps_next():
            pcount[0] += 1
            return k.ps[pcount[0] % 6]
        for j in range(FFN_H // 128):
            wu, wu_r = wu2[j % 2]
            for vg in range(2):
                c0 = vg * FFN_H + j * 128
                P.dma("pool", wu[:, vg], dr["ffn_w_up"][l, :, c0:c0 + 128].rearrange("(c p) n -> p c n", p=128), [], [wu_r], wu_r)
            dg, dg_r = dg2[j % 2]
            for vg in range(2):
                mm = vg * 22 + j
                for t9 in range(9):
                    P.op("dve", lambda e, vg=vg, t9=t9, dg=dg, mm=mm: e.tensor_scalar(
                        out=dg[:, vg * 9 + t9, :], in0=k.identf, scalar1=cwf[:, t9, mm:mm + 1], scalar2=None, op0=ALU.mult),
                        [k.identf_r, cwf_r], [dg_r])
            for vg in range(2):
                uc, uc_r = ub[vg][j % 2]
                for (u0, n) in tiles:
                    pst, _, ps_r = ps_next()
                    for c in range(8):
                        P.op("pe", lambda e, c=c, vg=vg, wu=wu, pst=pst, u0=u0, n=n: e.matmul(
                            pst[:, 0:n], lhsT=wu[:, vg, c, :], rhs=hxT[:, c, u0:u0 + n], start=(c == 0), stop=(c == 7)),
                            [wu_r, hxT_r], [ps_r], inc=(c == 7))
                    if u0 < CTX:
                        P.op("act", lambda e, pst=pst, u0=u0, n=n, uc=uc: e.activation(out=uc[:, 1 + u0:1 + u0 + n], in_=pst[:, 0:n],
                                                                                  func=AF.Identity), [ps_r], [uc_r])
                    else:
                        r0 = (u0 - CTX) // 64
                        o0 = 258 + (r0 + 1) * 66 + 1
                        P.op("act", lambda e, pst=pst, o0=o0, uc=uc: e.activation(
                            out=bc(uc[:, o0:o0 + 1], [[66, 8], [1, 64]]), in_=pst.rearrange("p (r c) -> p r c", c=64),
                            func=AF.Identity), [ps_r], [uc_r])
            ac, ac_r = ac2[j % 2]
            for (u0, n) in tiles:
                pvg = []
                for vg in range(2):
                    uc, uc_r = ub[vg][j % 2]
                    pst, _, ps_r = ps_next()
                    if u0 < CTX:
                        taps = [(3 + kc, uc[:, 1 + u0 + kc - 1:1 + u0 + kc - 1 + n]) for kc in range(3)]
                    else:
                        r0 = (u0 - CTX) // 64
                        taps = []
                        for kr in range(3):
                            for kc in range(3):
                                o0 = 258 + (r0 + kr) * 66 + kc
                                taps.append((kr * 3 + kc, bc(uc[:, o0:o0 + 1], [[66, 8], [1, 64]])))
                    for ti, (t9, rhs_ap) in enumerate(taps):
                        P.op("pe", lambda e, vg=vg, t9=t9, dg=dg, rhs_ap=rhs_ap, pst=pst, n=n, ti=ti, nt=len(taps): e.matmul(
                            pst[:, 0:n], lhsT=dg[:, vg * 9 + t9, :], rhs=rhs_ap,
                            start=(ti == 0), stop=(ti == nt - 1)), [dg_r, uc_r], [ps_r], inc=(ti == len(taps) - 1))
                    pvg.append((pst, ps_r))
                gg, gg_r = gg2[(u0 // 512) % 2]
                P.op("act", lambda e, gg=gg, p=pvg[1][0], n=n, mm=22 + j: e.activation(out=gg[:, 0:n], in_=p[:, 0:n], func=AF.Gelu,
                                                                                   bias=cbf[:, mm:mm + 1]), [pvg[1][1], cbf_r], [gg_r])
                P.op("dve", lambda e, gg=gg, p=pvg[0][0], n=n, u0=u0, ac=ac, mm=j: e.scalar_tensor_tensor(
                    out=ac[:, u0:u0 + n], in0=p[:, 0:n], scalar=cbf[:, mm:mm + 1], in1=gg[:, 0:n], op0=ALU.add, op1=ALU.mult),
                    [pvg[0][1], cbf_r, gg_r], [ac_r])
            P.dma("sp", dr["ACTS"][s, j * 128:(j + 1) * 128, :], ac, [ac_r], [k.dres["ACTS"]], ac_r)
        A.top = mark
    P.barrier()
    A.top = base
    wd, wd_r = A.alloc("wd", [22, D], BF16)
    for q in range(2):
        P.dma("pool", wd[:, 11 * q:11 * q + 11, :], dr["ffn_w_down"][l, 1408 * q:1408 * (q + 1), :].rearrange("(c p) n -> p c n", p=128),
              [], [wd_r], wd_r)
    grows = {v: load_modrow(k, l, v, 5, "G2row%d" % v) for v in range(3)}
    at2 = [A.alloc("fat%d" % i, [22, 512], BF16) for i in range(2)]
    bufs = nr_bufs(k)
    it = 0
    sub = 0
    for s in range(NSEQ):
        for (u0, n) in seg_tiles():
            if last and u0 < CTX:
                continue
            at, at_r = at2[it % 2]
            it += 1
            for q in range(2):
                P.dma("sp", at[:, 11 * q:11 * q + 11, 0:n], dr["ACTS"][s, 1408 * q:1408 * (q + 1), u0:u0 + n].rearrange("(c p) u -> p c u", p=128),
                      [k.dres["ACTS"]], [at_r], at_r)
            for i in range(n // 128):
                pss = [k.ps[2 * (sub % 2)], k.ps[2 * (sub % 2) + 1]]
                for h in range(2):
                    for c in range(22):
                        P.op("pe", lambda e, c=c, h=h, i=i, at=at, pst=pss[h][0]: e.matmul(
                            pst, lhsT=at[:, c, i * 128:(i + 1) * 128], rhs=wd[:, c, h * 512:(h + 1) * 512], start=(c == 0),
                            stop=(c == 21)), [at_r, wd_r], [pss[h][2]], inc=(c == 21))
                uu = u0 + i * 128
                v = 2 if uu < CTX else s
                if last:
                    dsts = [(dr["out"][s, uu - CTX:uu - CTX + 128, :], k.dres["out"])]
                else:
                    dsts = [(dr["XS"][1, s, uu:uu + 128, :], k.dres["XS"])]
                norm_residual(k, pss, dr["XS"][0, s, uu:uu + 128, :], k.dres["XS"], grows[v][0], grows[v][1], dsts, bufs, sub)
                sub += 1
    A.top = base


NP_ = 8192 + 512
HY_TILES_R = [(i * 512, 512, 0) for i in range(8)] + [(4096 + i * 512, 512, 1) for i in range(8)] + [(8192, 256, 0), (8448, 256, 1)]
HY_TILES_N = [(i * 512, 512, 1) for i in range(8)] + [(4096 + i * 512, 512, 0) for i in range(8)] + [(8192, 256, 1), (8448, 256, 0)]


def hy_feats(rev):
    bands = np.linspace(1e-4, 15.0, 16).astype(np.float32)
    out = np.zeros((33, NP_), np.float32)
    for (j0, L, n) in ((0, 4096, 8192), (8192, 256, 512)):
        lag = ((n // 2 - 1) - np.arange(n)) if rev else (np.arange(n) - n // 2)
        pos = np.abs(lag).astype(np.float32)
        t01 = pos / np.float32(L - 1)
        ang = np.float32(2.0 * math.pi / L) * pos[None, :] * bands[:, None]
        out[0, j0:j0 + n] = t01
        out[1:17, j0:j0 + n] = np.cos(ang)
        out[17:33, j0:j0 + n] = np.sin(ang)
    return out


def hy_delta():
    dl = np.abs(np.linspace(math.log(1e-2) / 1.5, math.log(1e-2) / 0.3, 512)).astype(np.float32)
    return np.ascontiguousarray(-dl.reshape(4, 128).T)


def stage_HF(k, l):
    P, A, dr = k.P, k.A, k.dr
    base = A.top
    sl = dict(allow_slow_non_contiguous=True)
    fe, fe_r = A.alloc("fe", [NP_], F32)
    t01b, t01b_r = A.alloc("t01b", [NP_], F32)
    h2, h2_r = A.alloc("h2", [NP_], F32)
    ndl, ndl_r = A.alloc("ndl", [4], F32)
    P.dma("sp", ndl, dr["hy_ndl"], [], [ndl_r], ndl_r)
    w1, w1_r = A.alloc("hw1", [64], F32)
    w2, w2_r = A.alloc("hw2", [64], F32)
    w3, w3_r = A.alloc("hw3", [2048], F32)
    cl, cl_r = A.alloc("hcl", [8], F32)
    hb, hb_r = A.alloc("hbias", [2, 4], F32)
    P.dma("sp", w1[0:33, :], dr["hy_w1"][l], [], [w1_r], w1_r)
    P.dma("sp", w2[0:64, :], dr["hy_w2"][l], [], [w2_r], w2_r)
    P.dma("sp", w3[0:64, :], dr["hy_w3"][l], [], [w3_r], w3_r)
    P.dma("sp", cl[0:64, 0:1], dr["hy_b1"][l].rearrange("(p o) -> p o", o=1), [], [cl_r], cl_r, **sl)
    P.dma("sp", cl[0:64, 1:2], dr["hy_b2"][l].rearrange("(p o) -> p o", o=1), [], [cl_r], cl_r, **sl)
    for q in range(2):
        P.dma("sp", cl[0:64, 2 + q:3 + q], dr["hy_freq"][l, q].rearrange("(p o) -> p o", o=1), [], [cl_r], cl_r, **sl)
        P.dma("sp", hb[:, q, :], dr["hy_bias"][l, q].rearrange("(m p) -> p m", p=128), [], [hb_r], hb_r, **sl)
    tm = [A.alloc("hft%d" % i, [512], F32) for i in range(6)]
    h1, h1_r = tm[5]

    def sinf(psrc, ps_r, bcol, fcol, out_ap, out_r, n):
        (xs, xs_r), (s8, s8_r), (s4, s4_r), (t, t_r), (c, c_r) = tm[0:5]
        P.op("dve", lambda e: e.tensor_scalar(out=xs[0:64, 0:n], in0=psrc[0:64, 0:n], scalar1=cl[0:64, bcol:bcol + 1],
                                              scalar2=cl[0:64, fcol:fcol + 1], op0=ALU.add, op1=ALU.mult), [ps_r, cl_r], [xs_r])
        P.op("act", lambda e: e.activation(out=s8[0:64, 0:n], in_=xs[0:64, 0:n], func=AF.Sin, scale=0.125), [xs_r], [s8_r])
        P.op("act", lambda e: e.activation(out=s4[0:64, 0:n], in_=xs[0:64, 0:n], func=AF.Sin, scale=0.25), [xs_r], [s4_r])
        P.op("dve", lambda e: e.tensor_tensor(out=t[0:64, 0:n], in0=s8[0:64, 0:n], in1=s8[0:64, 0:n], op=ALU.mult), [s8_r], [t_r])
        P.op("dve", lambda e: e.tensor_scalar(out=c[0:64, 0:n], in0=t[0:64, 0:n], scalar1=-2.0, scalar2=1.0, op0=ALU.mult,
                                              op1=ALU.add), [t_r], [c_r])
        P.op("dve", lambda e: e.scalar_tensor_tensor(out=s8[0:64, 0:n], in0=s4[0:64, 0:n], scalar=2.0, in1=c[0:64, 0:n],
                                                      op0=ALU.mult, op1=ALU.mult), [s4_r, c_r], [s8_r])
        P.op("dve", lambda e: e.tensor_tensor(out=t[0:64, 0:n], in0=s4[0:64, 0:n], in1=s4[0:64, 0:n], op=ALU.mult), [s4_r], [t_r])
        P.op("dve", lambda e: e.tensor_scalar(out=c[0:64, 0:n], in0=t[0:64, 0:n], scalar1=-2.0, scalar2=1.0, op0=ALU.mult,
                                              op1=ALU.add), [t_r], [c_r])
        P.op("dve", lambda e: e.scalar_tensor_tensor(out=out_ap, in0=s8[0:64, 0:n], scalar=2.0, in1=c[0:64, 0:n],
                                                      op0=ALU.mult, op1=ALU.mult), [s8_r, c_r], [out_r])

    tp2 = [A.alloc("htp%d" % i, [NP_], BF16) for i in range(2)]
    wt2 = [A.alloc("hwt%d" % i, [512], F32) for i in range(2)]
    it = 0
    for o in range(2):
        tiles_o = HY_TILES_R if o == 0 else HY_TILES_N
        P.dma("sp", fe[0:33, :], dr["hy_fe%d" % o], [], [fe_r], fe_r)
        P.dma("sp", t01b, dr["hy_fe%d" % o][0:1, :].partition_broadcast(128), [], [t01b_r], t01b_r)
        for (j0, n, dd) in tiles_o:
            pst, _, ps_r = k.ps[0]
            P.op("pe", lambda e, j0=j0, n=n, pst=pst: e.matmul(pst[0:64, 0:n], lhsT=w1[0:33, :], rhs=fe[0:33, j0:j0 + n], start=True, stop=True),
                 [w1_r, fe_r], [ps_r])
            sinf(pst, ps_r, 0, 2, h1[0:64, 0:n], h1_r, n)
            pst2, _, ps2_r = k.ps[1]
            P.op("pe", lambda e, n=n, pst2=pst2: e.matmul(pst2[0:64, 0:n], lhsT=w2[0:64, :], rhs=h1[0:64, 0:n], start=True, stop=True),
                 [w2_r, h1_r], [ps2_r])
            sinf(pst2, ps2_r, 1, 3, h2[0:64, j0:j0 + n], h2_r, n)
        for m in range(4):
            tp, tp_r = tp2[(o * 4 + m) % 2]
            for (j0, n, dd) in tiles_o:
                pst, _, ps_r = k.ps[2 + it % 4]
                wt, wt_r = wt2[it % 2]
                it += 1
                col0 = o * 1024 + dd * 512 + m * 128
                P.op("pe", lambda e, j0=j0, n=n, pst=pst, col0=col0: e.matmul(pst[:, 0:n], lhsT=w3[0:64, col0:col0 + 128],
                                                                         rhs=h2[0:64, j0:j0 + n], start=True, stop=True),
                     [w3_r, h2_r], [ps_r])
                P.op("act", lambda e, wt=wt, j0=j0, n=n, m=m: e.activation(out=wt[:, 0:n], in_=t01b[:, j0:j0 + n], func=AF.Exp,
                                                                       scale=ndl[:, m:m + 1]), [t01b_r, ndl_r], [wt_r])
                P.op("dve", lambda e, wt=wt, pst=pst, tp=tp, j0=j0, n=n: e.tensor_tensor(out=tp[:, j0:j0 + n], in0=pst[:, 0:n], in1=wt[:, 0:n],
                                                                                      op=ALU.mult), [ps_r, wt_r], [tp_r])
            for jj in ((4095, 8192 + 255) if o == 0 else (4096, 8192 + 256)):
                P.op("dve", lambda e, tp=tp, jj=jj, o=o, m=m: e.tensor_tensor(out=tp[:, jj:jj + 1], in0=tp[:, jj:jj + 1], in1=hb[:, o, m:m + 1],
                                                                          op=ALU.add), [tp_r, hb_r], [tp_r])
            P.dma("sp", dr["TAPS"][o, m * 128:(m + 1) * 128, :], tp, [tp_r], [k.dres["TAPS"]], tp_r)
    A.top = base


def stage_HC(k, l, shared=False):
    P, A, dr = k.P, k.A, k.dr
    base = A.top
    NJ = U // 128
    NPASS = 2 if shared else 1
    CW = BW // NPASS
    cb0, cb1 = (5, 6) if shared else (0, 1)
    ab0, ab1 = (7, 7) if shared else (2, 3)
    VZ, VZ_r = A.alloc("VZ", [2, NJ, CW], BF16)
    X, X_r = A.alloc("HX", [2, NJ, CW], BF16)
    HW = NP_ - 127
    hc2 = [A.alloc("hc%d" % i, [HW], BF16) for i in range(2)]
    jm, jm_r = A.alloc("jmat", [128], BF16)
    P.dma("sp", jm, dr["jmat"], [], [jm_r], jm_r)
    yt2 = [A.alloc("hyt%d" % i, [CW // 128, 128], BF16) for i in range(2)]
    taps_t = dr["TAPS"].tensor
    for hp in range(NPASS):
        ch0 = hp * CW
        VZg_r = [Res("VZg%d" % g) for g in range(CW // 4)]
        for b in range(NSEQ):
            P.dma("sp", VZ[:, b], dr["X12"][b, :, ch0:ch0 + CW].rearrange("(j p) c -> p j c", p=128), [k.dres["X12"]],
                  [VZ_r] + VZg_r, VZ_r)
        yield
        for o in range(2):
            for b in range(NSEQ):
                P.dma("sp", X[:, b], dr["X12"][b, :, 512 * (o + 1) + ch0:512 * (o + 1) + ch0 + CW].rearrange("(j p) c -> p j c", p=128),
                      [k.dres["X12"]], [X_r], X_r)
            yield
            if o == 0:
                for b in range(NSEQ):
                    for J in range(NJ):
                        for h0 in range(0, CW, 512):
                            pr, _, pr_r = k.ps[ab0 if J % 2 == 0 else ab1]
                            w_ = min(512, CW - h0)
                            P.op("pe", lambda e, b=b, J=J, pr=pr, h0=h0, w_=w_: e.matmul(pr[:, 0:w_], lhsT=jm, rhs=X[:, b, J, h0:h0 + w_],
                                                                                      start=True, stop=True), [jm_r, X_r], [pr_r])
                            P.op("act", lambda e, b=b, J=J, pr=pr, h0=h0, w_=w_: e.activation(out=X[:, b, J, h0:h0 + w_], in_=pr[:, 0:w_],
                                                                                           func=AF.Identity), [pr_r], [X_r])
                        yield
            for cl in range(CW):
                c = ch0 + cl
                hc, hc_r = hc2[cl % 2]
                src = bass.AP(taps_t, (o * BW + c) * NP_, [[1, 128], [1, HW]])
                P.dma("sp", hc, src, [k.dres["TAPS"]], [hc_r], hc_r)
                pst, _, ps_r = k.ps[cb0 if (cl // 4) % 2 == 0 else cb1]
                cb = (cl % 4) * 68
                mms = []
                for d in [0] + [x for x in range(-31, 32) if x != 0]:
                    J0 = max(2, 2 - d)
                    J1 = min(NJ, NJ - d)
                    mms.append(((3968 - 128 * d) if o == 0 else (3969 + 128 * d), J0, J1 - J0, J0 + d))
                for d in ((0, -1, 1) if l < DEPTH - 1 else ()):
                    J0 = max(0, -d)
                    J1 = min(2, 2 - d)
                    mms.append(((8192 + 128 - 128 * d) if o == 0 else (8192 + 129 + 128 * d), J0, J1 - J0, J0 + d))
                for mi, (w0, J0, nJ, I0) in enumerate(mms):
                    P.op("pe", lambda e, hc=hc, w0=w0, J0=J0, nJ=nJ, I0=I0, pst=pst, cb=cb, cl=cl, mi=mi, nm=len(mms): e.matmul(
                        bc(pst[:, cb + I0:cb + I0 + 1], [[NJ, 2], [1, nJ]]), lhsT=hc[:, w0:w0 + 128],
                        rhs=bc(VZ[:, 0, J0, cl:cl + 1], [[NJ * CW, 2], [CW, nJ]]), start=(mi == 0), stop=(mi == nm - 1)),
                        [hc_r, VZg_r[cl // 4]], [ps_r], inc=(mi == len(mms) - 1))
                    if mi % 8 == 7:
                        yield
                if cl % 4 == 3:
                    c0 = cl - 3
                    for b in range(NSEQ):
                        P.op("dve", lambda e, b=b, c0=c0, pst=pst: e.tensor_tensor(
                            out=VZ[:, b, :, c0:c0 + 4], in0=bc(pst[:, b * NJ:b * NJ + 1], [[1, NJ], [68, 4]]),
                            in1=X[:, b, :, c0:c0 + 4], op=ALU.mult), [ps_r, X_r], [VZg_r[c0 // 4]])
                yield
        it = 0
        NM = CW // 128
        for b in range(NSEQ):
            for J in range(NJ):
                _, psb, ps_r = k.ps[ab0 if it % 2 == 0 else ab1]
                yt, yt_r = yt2[it % 2]
                it += 1
                for m in range(NM):
                    P.op("pe", lambda e, b=b, J=J, m=m, psb=psb: e.transpose(out=psb[:, m * 128:(m + 1) * 128], in_=VZ[:, b, J, m * 128:(m + 1) * 128],
                                                                         identity=k.identb), [VZ_r, k.identb_r] + VZg_r[32 * m:32 * m + 32], [ps_r],
                         inc=(m == NM - 1))
                P.op("act", lambda e, yt=yt, psb=psb, NM=NM: e.activation(out=yt, in_=psb[:, 0:NM * 128].rearrange("p (c q) -> p c q", c=NM),
                                                                       func=AF.Identity), [ps_r], [yt_r])
                P.dma("sp", dr["YB"][b, ch0:ch0 + CW, J * 128:(J + 1) * 128].rearrange("(c p) u -> p c u", p=128), yt, [yt_r], [k.dres["YB"]], yt_r)
                yield
    A.top = base


_CACHE = {}


def make_in_maps(inputs):
    consts = host_consts()
    maps = []
    x = np.asarray(inputs["x"], np.float32)
    ctx = np.asarray(inputs["ctx"], np.float32)
    c = np.asarray(inputs["c"], np.float32)
    c_ctx = np.asarray(inputs["c_ctx"], np.float32)
    for core in range(8):
        m = {}
        m["x"] = np.ascontiguousarray(x[2 * core:2 * core + 2])
        m["ctx"] = np.ascontiguousarray(ctx[2 * core:2 * core + 2])
        cv = np.stack([c[2 * core], c[2 * core + 1], c_ctx], axis=0)
        m["cT"] = np.ascontiguousarray(cv.T.reshape(8, 128, 3).transpose(1, 0, 2))
        for n in W_NAMES:
            m[n] = np.ascontiguousarray(np.asarray(inputs[n], np.float32))
        m.update(consts)
        maps.append(m)
    return maps


def kernel(**inputs):
    if "nc" not in _CACHE:
        _CACHE["nc"] = build()
    nc = _CACHE["nc"]
    maps = make_in_maps(inputs)
    res = run_bass_kernel_spmd(nc, maps, core_ids=list(range(8)))
    out = np.concatenate([np.asarray(r["out"], np.float32) for r in res.results], axis=0)
    return out
```

```python
import math
from contextlib import ExitStack
import numpy as np
import ml_dtypes
import concourse.bass as bass
import concourse.mybir as mybir
from concourse.bass_utils import run_bass_kernel_spmd

F32 = mybir.dt.float32
BF16 = mybir.dt.bfloat16
AF = mybir.ActivationFunctionType
ALU = mybir.AluOpType
AX = mybir.AxisListType

D = 1024
SEQ = 4096
CTX = 256
U = CTX + SEQ
NSEQ = 2
DEPTH = 2
BW = 512
IN_COLS = 7184
FFN_H = 2816
EPS = 1e-6

COMPUTE = ("pe", "act", "dve", "pool")
ENGINES = ("pe", "act", "dve", "pool", "sp")


class Res:
    __slots__ = ("name", "w", "r", "dkey")

    def __init__(self, name):
        self.name = name
        self.w = None
        self.r = {}
        self.dkey = None


class Prog:
    NDK = 96

    def __init__(self):
        self.items = {e: [] for e in ENGINES}
        self.cnt = {e: 0 for e in COMPUTE}
        self.known = {e: {} for e in ENGINES}
        self.dtot = {}
        self.ndk = 0

    def cur(self, key):
        return self.cnt[key] if key in self.cnt else self.dtot.get(key, 0)

    def _need(self, eng, key, val):
        if key not in self.cnt:
            val = self.dtot[key]
        if self.known[eng].get(key, 0) >= val:
            return
        self.known[eng][key] = val
        self.items[eng].append(("wait", key, val))

    def _deps(self, eng, reads, writes):
        for r in reads:
            if r.w is not None and not (eng == "pe" and r.w[0] == "pe"):
                self._need(eng, *r.w)
        for w in writes:
            if w.w is not None and not (eng == "pe" and w.w[0] == "pe"):
                self._need(eng, *w.w)
            for k, v in w.r.items():
                if not (eng == "pe" and k == "pe"):
                    self._need(eng, k, v)

    def op(self, eng, fn, reads=(), writes=(), inc=True):
        self._deps(eng, reads, writes)
        val = self.cnt[eng] + 1
        if inc:
            self.cnt[eng] = val
        self.items[eng].append(("op", fn, inc))
        for r in reads:
            r.r[eng] = max(r.r.get(eng, 0), val)
        for w in writes:
            w.w = (eng, val)
            w.r = {}

    def dkey(self, res):
        if res.dkey is None:
            res.dkey = "d%d" % (self.ndk % self.NDK)
            self.ndk += 1
            self.dtot.setdefault(res.dkey, 0)
        return res.dkey

    def dma(self, eng, out_ap, in_ap, reads, writes, semres, **kw):
        self._deps(eng, reads, writes)
        key = self.dkey(semres)
        self.dtot[key] += 16
        val = self.dtot[key]
        self.items[eng].append(("dma", out_ap, in_ap, key, kw))
        for r in reads:
            r.r[key] = val
        for w in writes:
            w.w = (key, val)
            w.r = {}

    def barrier(self):
        keys = list(self.cnt.keys()) + list(self.dtot.keys())
        for e in ENGINES:
            for k in keys:
                if self.cur(k) > 0:
                    self._need(e, k, self.cur(k))


class Arena:
    def __init__(self, ap_f32, ap_bf16, nbytes):
        self.f = ap_f32
        self.b = ap_bf16
        self.top = 0
        self.nbytes = nbytes

    def alloc(self, name, shape, dt):
        esz = 4 if dt == F32 else 2
        n = int(np.prod(shape))
        nb = (n * esz + 31) // 32 * 32
        off = self.top
        assert off + nb <= self.nbytes, ("SBUF arena overflow", name, off + nb)
        self.top += nb
        base = self.f if dt == F32 else self.b
        v = base[:, off // esz: off // esz + n]
        if len(shape) == 2:
            v = v.rearrange("p (a b) -> p a b", a=shape[0])
        elif len(shape) == 3:
            v = v.rearrange("p (a b c) -> p a b c", a=shape[0], b=shape[1])
        return v, Res(name)


def _bf(a):
    return np.asarray(a, np.float32).astype(ml_dtypes.bfloat16)


W_NAMES = ["mod_w", "mod_b", "norms", "w_in", "lru_conv_w", "lru_conv_b", "lru_w_a", "lru_b_a", "lru_w_i", "lru_b_i",
           "lru_lambda", "hy_conv_w", "hy_conv_b", "hy_w1", "hy_b1", "hy_w2", "hy_b2", "hy_w3", "hy_freq", "hy_bias",
           "ssm_conv_w", "ssm_conv_b", "ssm_dt_bias", "ssm_a_log", "ssm_d", "ssm_norm", "w_branch", "w_out",
           "ffn_w_up", "ffn_conv_w", "ffn_conv_b", "ffn_w_down"]

W_SHAPES = {"mod_w": [2, 1024, 6144], "mod_b": [2, 6144], "norms": [2, 4, 1024], "w_in": [2, 1024, 7184],
            "lru_conv_w": [2, 4, 512], "lru_conv_b": [2, 512], "lru_w_a": [2, 2, 8, 64, 64], "lru_b_a": [2, 2, 512],
            "lru_w_i": [2, 2, 8, 64, 64], "lru_b_i": [2, 2, 512], "lru_lambda": [2, 2, 512],
            "hy_conv_w": [2, 3, 1536], "hy_conv_b": [2, 1536], "hy_w1": [2, 33, 64], "hy_b1": [2, 64],
            "hy_w2": [2, 64, 64], "hy_b2": [2, 64], "hy_w3": [2, 64, 2048], "hy_freq": [2, 2, 64],
            "hy_bias": [2, 2, 512], "ssm_conv_w": [2, 4, 1024], "ssm_conv_b": [2, 1024], "ssm_dt_bias": [2, 2, 8],
            "ssm_a_log": [2, 2, 8], "ssm_d": [2, 8], "ssm_norm": [2, 512], "w_branch": [2, 3, 512, 1024],
            "w_out": [2, 1024, 1024], "ffn_w_up": [2, 1024, 5632], "ffn_conv_w": [2, 3, 3, 5632],
            "ffn_conv_b": [2, 5632], "ffn_w_down": [2, 2816, 1024]}


def host_consts():
    c = {}
    c["ident_bf"] = _bf(np.eye(128))
    c["ident_f"] = np.eye(128, dtype=np.float32)
    c["tri_f"] = np.triu(np.ones((128, 128), np.float32))
    c["tri_b"] = np.tril(np.ones((128, 128), np.float32))
    c["mneg_f"] = ((1.0 - c["tri_f"]) * -1e30).astype(np.float32)
    c["mneg_b"] = ((1.0 - c["tri_b"]) * -1e30).astype(np.float32)
    c["hy_fe0"] = hy_feats(True)
    c["hy_fe1"] = hy_feats(False)
    c["jmat"] = _bf(np.eye(128)[::-1])
    c["hy_ndl"] = hy_delta()
    return c


CONST_SHAPES = {"ident_bf": ([128, 128], BF16), "ident_f": ([128, 128], F32), "tri_f": ([128, 128], F32),
                "tri_b": ([128, 128], F32), "mneg_f": ([128, 128], F32), "mneg_b": ([128, 128], F32), "hy_fe0": ([33, 8704], F32), "hy_fe1": ([33, 8704], F32), "jmat": ([128, 128], BF16), "hy_ndl": ([128, 4], F32)}


class K:
    pass


def build(stop_after=None, dumps=()):
    nc = bass.Bass("TRN2", target_bir_lowering=False)
    k = K()
    k.nc = nc
    k.P = P = Prog()
    k.dumps = dumps
    k.s_per_h = 1
    import os
    k.l2_first = bool(os.environ.get("L2FIRST"))
    dr = {}
    dr["x"] = nc.dram_tensor("x", [NSEQ, SEQ, D], F32, kind="ExternalInput").ap()
    dr["ctx"] = nc.dram_tensor("ctx", [NSEQ, CTX, D], F32, kind="ExternalInput").ap()
    dr["cT"] = nc.dram_tensor("cT", [128, 8, 3], F32, kind="ExternalInput").ap()
    for n in W_NAMES:
        dr[n] = nc.dram_tensor(n, W_SHAPES[n], F32, kind="ExternalInput").ap()
    for n, (shp, dt) in CONST_SHAPES.items():
        dr[n] = nc.dram_tensor(n, shp, dt, kind="ExternalInput").ap()
    dr["out"] = nc.dram_tensor("out", [NSEQ, SEQ, D], F32, kind="ExternalOutput").ap()
    k.dr = dr
    k.dres = {}

    def scratch(name, shape, dt):
        kind = "ExternalOutput" if name in dumps else "Internal"
        dr[name] = nc.dram_tensor(name, shape, dt, kind=kind).ap()
        k.dres[name] = Res(name)
        return dr[name]

    k.scratch = scratch
    scratch("MODR", [DEPTH, 3, 6, D], F32)
    scratch("XS", [2, NSEQ, U, D], F32)
    scratch("XA", [NSEQ, BW, U], F32)
    scratch("GA", [NSEQ, BW, U], BF16)
    scratch("XBC", [NSEQ, 1024, U], BF16)
    scratch("DT", [NSEQ, U, 16], F32)
    scratch("ZS", [NSEQ, U, BW], BF16)
    scratch("HYV", [NSEQ, BW, U], BF16)
    scratch("X12", [NSEQ, U, 1536], BF16)
    scratch("GATES", [NSEQ, 3 * D, U], BF16)
    scratch("YA", [NSEQ, BW, U], BF16)
    scratch("HFWD", [NSEQ, BW, U], F32)
    scratch("YS", [NSEQ, U, BW], F32)
    scratch("YB", [NSEQ, BW, U], BF16)
    scratch("TAPS", [2, BW, NP_], BF16)
    scratch("ACTS", [NSEQ, FFN_H, U], BF16)
    k.dres["out"] = Res("out")
    scratch("YC", [NSEQ, BW, U], BF16)
    for dn in ("DBG1", "DBG2", "DBG3", "DBG4"):
        if dn in dumps:
            scratch(dn, [NSEQ, BW, U], F32)
    if "HXT" in dumps:
        scratch("HXT", [NSEQ, D, U], BF16)

    with ExitStack() as es:
        ARENA_BYTES = 211968
        arena_t = es.enter_context(nc.sbuf_tensor("arena", [128, ARENA_BYTES // 4], F32))
        k.A = Arena(arena_t[:], arena_t[:].bitcast(BF16), ARENA_BYTES)
        k.ps = []
        for i in range(8):
            t = es.enter_context(nc.psum_tensor("ps%d" % i, [128, 512], F32))
            k.ps.append((t[:], t[:].bitcast(BF16), Res("ps%d" % i)))
        emit_all(k, stop_after)
        P.barrier()
        sems = {}
        for key in list(P.cnt.keys()) + list(P.dtot.keys()):
            sems[key] = es.enter_context(nc.semaphore("s_" + key))
        block = es.enter_context(nc.Block())

        def replay(eng_handle, items):
            for it in items:
                if it[0] == "wait":
                    eng_handle.wait_ge(sems[it[1]], it[2])
                elif it[0] == "op":
                    ins = it[1](eng_handle)
                    if it[2]:
                        ins.then_inc(sems[_ek[0]], 1)
                else:
                    eng_handle.dma_start(out=it[1], in_=it[2], **it[4]).then_inc(sems[it[3]], 16)

        _ek = [None]

        @block.tensor
        def _(e):
            _ek[0] = "pe"
            replay(e, P.items["pe"])

        @block.scalar
        def _(e):
            _ek[0] = "act"
            replay(e, P.items["act"])

        @block.vector
        def _(e):
            _ek[0] = "dve"
            replay(e, P.items["dve"])

        @block.gpsimd
        def _(e):
            _ek[0] = "pool"
            replay(e, P.items["pool"])

        @block.sync
        def _(e):
            _ek[0] = "sp"
            replay(e, P.items["sp"])
    return nc


def emit_all(k, stop_after):
    P = k.P
    stage_consts(k)
    stage_mod(k)
    P.barrier()
    if stop_after == "mod":
        return
    for l in range(DEPTH):
        stage_A(k, l)
        P.barrier()
        if stop_after == "A%d" % l:
            return
        stage_L(k, l)
        P.barrier()
        if stop_after == "L%d" % l:
            return
        stage_HF(k, l)
        P.barrier()
        base_ = k.A.top
        gh = stage_HC(k, l, shared=True)
        next(gh)
        gs = stage_S(k, l, banks=(0, 1, 2, 3, 4))
        next(gs)
        others = [gs]
        done_h = False
        while others or not done_h:
            if not done_h:
                try:
                    next(gh)
                except StopIteration:
                    done_h = True
            for g_ in list(others):
                for _ in range(1 if not done_h else 32):
                    try:
                        next(g_)
                    except StopIteration:
                        others.remove(g_)
                        break
        k.A.top = base_
        P.barrier()
        if stop_after == "H%d" % l or stop_after == "S%d" % l:
            return
        stage_M(k, l)
        P.barrier()
        if stop_after == "M%d" % l:
            return
        stage_F(k, l)
        P.barrier()
        if stop_after == "F%d" % l:
            return


def stage_consts(k):
    P, A, dr = k.P, k.A, k.dr
    k.identb, k.identb_r = A.alloc("identb", [128], BF16)
    k.identf, k.identf_r = A.alloc("identf", [128], F32)
    P.dma("sp", k.identb, dr["ident_bf"], [], [k.identb_r], k.identb_r)
    P.dma("sp", k.identf, dr["ident_f"], [], [k.identf_r], k.identf_r)
    k.const_top = A.top


def stage_mod(k):
    P, A, dr = k.P, k.A, k.dr
    mark = A.top
    ct, ct_r = A.alloc("ct", [8, 3], F32)
    sg, sg_r = A.alloc("sg", [8, 3], F32)
    P.dma("sp", ct, dr["cT"], [], [ct_r], ct_r)
    P.op("act", lambda e: e.activation(out=sg, in_=ct, func=AF.Silu), [ct_r], [sg_r])
    wt = [A.alloc("modw%d" % i, [8, 512], F32) for i in range(2)]
    modrow, modrow_r = A.alloc("modrow", [6144], F32)
    modb, modb_r = A.alloc("modb", [6144], F32)
    nrm, nrm_r = A.alloc("nrm", [4, D], F32)
    der, der_r = A.alloc("der", [6, D], F32)
    for l in range(DEPTH):
        P.dma("sp", modb[0:3, :], dr["mod_b"][l:l + 1, :].partition_broadcast(3), [], [modb_r], modb_r)
        P.dma("sp", nrm[0:3], dr["norms"][l:l + 1].partition_broadcast(3), [], [nrm_r], nrm_r)
        for j in range(12):
            w, w_r = wt[j % 2]
            P.dma("sp", w, dr["mod_w"][l, :, j * 512:(j + 1) * 512].rearrange("(c p) n -> p c n", p=128),
                  [], [w_r], w_r)
            pst, _, ps_r = k.ps[j % 2]
            for c in range(8):
                P.op("pe", lambda e, c=c, w=w, pst=pst: e.matmul(pst[0:3, :], lhsT=sg[:, c, :], rhs=w[:, c, :],
                                                                   start=(c == 0), stop=(c == 7)),
                     [sg_r, w_r], [ps_r], inc=(c == 7))
            P.op("dve", lambda e, j=j, pst=pst: e.tensor_tensor(out=modrow[0:3, j * 512:(j + 1) * 512], in0=pst[0:3, :],
                                                                  in1=modb[0:3, j * 512:(j + 1) * 512], op=ALU.add),
                 [ps_r, modb_r], [modrow_r])
        sl = lambda i: modrow[0:3, i * D:(i + 1) * D]
        P.op("dve", lambda e: e.scalar_tensor_tensor(out=der[0:3, 0, :], in0=sl(1), scalar=1.0, in1=nrm[0:3, 0, :],
                                                      op0=ALU.add, op1=ALU.mult), [modrow_r, nrm_r], [der_r])
        P.op("dve", lambda e: e.tensor_copy(out=der[0:3, 1, :], in_=sl(0)), [modrow_r], [der_r])
        P.op("dve", lambda e: e.tensor_tensor(out=der[0:3, 2, :], in0=sl(2), in1=nrm[0:3, 1, :], op=ALU.mult),
             [modrow_r, nrm_r], [der_r])
        P.op("dve", lambda e: e.scalar_tensor_tensor(out=der[0:3, 3, :], in0=sl(4), scalar=1.0, in1=nrm[0:3, 2, :],
                                                      op0=ALU.add, op1=ALU.mult), [modrow_r, nrm_r], [der_r])
        P.op("dve", lambda e: e.tensor_copy(out=der[0:3, 4, :], in_=sl(3)), [modrow_r], [der_r])
        P.op("dve", lambda e: e.tensor_tensor(out=der[0:3, 5, :], in0=sl(5), in1=nrm[0:3, 3, :], op=ALU.mult),
             [modrow_r, nrm_r], [der_r])
        P.dma("sp", dr["MODR"][l], der[0:3], [der_r], [k.dres["MODR"]], der_r)
    A.top = mark


def load_modrow(k, l, v, j, name):
    t, r = k.A.alloc(name, [D], F32)
    k.P.dma("sp", t, k.dr["MODR"][l, v, j:j + 1, :].partition_broadcast(128), [k.dres["MODR"]], [r], r)
    return t, r


def seg_tiles():
    return [(0, CTX)] + [(CTX + i * 512, 512) for i in range(SEQ // 512)]


def stream_src(k, l, s, u0, n):
    if l == 0:
        if u0 < CTX:
            return k.dr["ctx"][s, u0:u0 + n, :], None
        return k.dr["x"][s, u0 - CTX:u0 - CTX + n, :], None
    return k.dr["XS"][1, s, u0:u0 + n, :], k.dres["XS"]


def norm_mod_T(k, l, s, hxT, hxT_r, j0, src_fn, name):
    P, A = k.P, k.A
    P.barrier()
    mark = A.top
    rows = {}
    for v in (s, 2):
        rows[v] = (load_modrow(k, l, v, j0, "Arow%d" % v), load_modrow(k, l, v, j0 + 1, "Brow%d" % v))
    xt = [A.alloc("xt%d" % i, [D], F32) for i in range(3)]
    sq, sq_r = A.alloc("sq", [D], F32)
    t1 = [A.alloc("t1_%d" % i, [D], F32) for i in range(2)]
    hb = [A.alloc("hb%d" % i, [D], BF16) for i in range(2)]
    st = [A.alloc("st%d" % i, [4], F32) for i in range(2)]
    nsub = U // 128

    def part_a1(i):
        u0 = i * 128
        x, x_r = xt[i % 3]
        src, sres = src_fn(k, l, s, u0, 128)
        P.dma("sp", x, src, [sres] if sres else [], [x_r], x_r)
        ss, ss_r = st[i % 2]
        P.op("act", lambda e, x=x, ss=ss: e.activation(out=sq, in_=x, func=AF.Square, accum_out=ss[:, 0:1]),
             [x_r], [sq_r, ss_r])

    def part_a2(i):
        ss, ss_r = st[i % 2]
        P.op("dve", lambda e, ss=ss: e.tensor_scalar(out=ss[:, 1:2], in0=ss[:, 0:1], scalar1=1.0 / D, scalar2=EPS,
                                                     op0=ALU.mult, op1=ALU.add), [ss_r], [ss_r])
        P.op("act", lambda e, ss=ss: e.activation(out=ss[:, 2:3], in_=ss[:, 1:2], func=AF.Sqrt), [ss_r], [ss_r])
        P.op("dve", lambda e, ss=ss: e.reciprocal(out=ss[:, 3:4], in_=ss[:, 2:3]), [ss_r], [ss_r])

    def part_b(i):
        u0 = i * 128
        v = 2 if u0 < CTX else s
        (Ar, Ar_r), (Br, Br_r) = rows[v]
        x, x_r = xt[i % 3]
        ss, ss_r = st[i % 2]
        tt, tt_r = t1[i % 2]
        P.op("dve", lambda e, x=x, ss=ss, tt=tt, Ar=Ar: e.scalar_tensor_tensor(
            out=tt, in0=x, scalar=ss[:, 3:4], in1=Ar, op0=ALU.mult, op1=ALU.mult), [x_r, ss_r, Ar_r], [tt_r])
        h, h_r = hb[i % 2]
        P.op("dve", lambda e, tt=tt, h=h, Br=Br: e.tensor_tensor(out=h, in0=tt, in1=Br, op=ALU.add),
             [tt_r, Br_r], [h_r])
        _, psb, ps_r = k.ps[i % 2]
        for c in range(8):
            P.op("pe", lambda e, c=c, h=h, psb=psb: e.transpose(out=psb[:, c * 128:(c + 1) * 128],
                                                                 in_=h[:, c * 128:(c + 1) * 128], identity=k.identb),
                 [h_r, k.identb_r], [ps_r], inc=(c == 7))
        P.op("act", lambda e, psb=psb, u0=u0: e.activation(
            out=hxT[:, :, u0:u0 + 128], in_=psb.rearrange("p (c t) -> p c t", c=8), func=AF.Identity),
            [ps_r], [hxT_r])

    part_a1(0)
    part_a2(0)
    for i in range(nsub):
        if i + 1 < nsub:
            part_a1(i + 1)
        part_b(i)
        if i + 1 < nsub:
            part_a2(i + 1)
    P.barrier()
    A.top = mark


PB_CTX = 2
PB_LAT = 264
PB_W = 4368


def pb_pos(u):
    return PB_CTX + u if u < CTX else PB_LAT + (u - CTX)


def stage_A(k, l):
    P, A, dr = k.P, k.A, k.dr
    base = A.top
    hxT, hxT_r = A.alloc("hxT", [8, U], BF16)
    cw_l, cw_l_r = A.alloc("cw_l", [4, 4], F32)
    cw_s, cw_s_r = A.alloc("cw_s", [8, 4], F32)
    cw_h, cw_h_r = A.alloc("cw_h", [12, 3], F32)
    cb_l, cb_l_r = A.alloc("cb_l", [4], F32)
    cb_s, cb_s_r = A.alloc("cb_s", [8], F32)
    cb_h, cb_h_r = A.alloc("cb_h", [12], F32)
    dtb, dtb_r = A.alloc("dtb", [16], F32)
    sl = dict(allow_slow_non_contiguous=True)
    for (cw_, cw_r_, nm_, nt_) in ((cw_l, cw_l_r, "lru_conv_w", 4), (cw_s, cw_s_r, "ssm_conv_w", 4), (cw_h, cw_h_r, "hy_conv_w", 3)):
        for kk in range(nt_):
            P.dma("sp", cw_[:, :, kk], dr[nm_][l, kk].rearrange("(m p) -> p m", p=128), [], [cw_r_], cw_r_, **sl)
    P.dma("sp", cb_l, dr["lru_conv_b"][l].rearrange("(m p) -> p m", p=128), [], [cb_l_r], cb_l_r, **sl)
    P.dma("sp", cb_s, dr["ssm_conv_b"][l].rearrange("(m p) -> p m", p=128), [], [cb_s_r], cb_s_r, **sl)
    P.dma("sp", cb_h, dr["hy_conv_b"][l].rearrange("(m p) -> p m", p=128), [], [cb_h_r], cb_h_r, **sl)
    P.dma("sp", dtb, dr["ssm_dt_bias"][l:l + 1].rearrange("o a b -> o (a b)").partition_broadcast(128), [], [dtb_r], dtb_r)
    wbuf = [A.alloc("wbuf%d" % i, [8, 512], BF16) for i in range(2)]
    pb = [A.alloc("pb%d" % i, [PB_W], BF16) for i in range(2)]
    for t, r in pb:
        P.op("pool", lambda e, t=t: e.memset(t, 0.0), [], [r])
    stg = [A.alloc("stg%d" % i, [U], F32) for i in range(2)]
    diag = [A.alloc("diag%d" % i, [4, 128], BF16) for i in range(2)]
    tmst = [A.alloc("tmst%d" % i, [U // 128, 128], BF16) for i in range(2)]
    zst = [A.alloc("zst%d" % i, [512], BF16) for i in range(2)]
    dtst, dtst_r = A.alloc("dtst", [U // 128, 16], F32)
    dtt, dtt_r = A.alloc("dtt", [16], F32)
    tiles = seg_tiles()
    cnt = {"w": 0, "c": 0, "ps": 0}

    def ps_next():
        cnt["ps"] += 1
        return k.ps[2 + cnt["ps"] % 4]

    groups = [(0, "conv", ("XA", 0, cw_l, cw_l_r, cb_l, cb_l_r, 0, 4, 2, AF.Identity, F32))]
    groups += [(512 + 512 * g, "conv", ("XBC", 512 * g, cw_s, cw_s_r, cb_s, cb_s_r, 4 * g, 4, 2, AF.Silu, BF16)) for g in range(2)]
    groups += [(1552, "plain", ("GA", 0, AF.Gelu))]
    groups += [(2064 + 512 * g, "x12", (512 * g, cw_h, cw_h_r, cb_h, cb_h_r, 4 * g, 3, 1)) for g in (0, 1, 2)]
    groups += [(3600, "z", None)]
    groups += [(4112 + 512 * g, "plain", ("GATES", 512 * g, AF.Sigmoid)) for g in range(6)]
    groups += [(1536, "dt", None)]

    for s in range(NSEQ):
        norm_mod_T(k, l, s, hxT, hxT_r, 0, stream_src, "A")
        if "HXT" in k.dumps:
            P.dma("sp", dr["HXT"][s].rearrange("(c p) u -> p c u", p=128), hxT, [hxT_r], [k.dres["HXT"]], hxT_r)
        for (c0, kind, prm) in groups:
            w, w_r = wbuf[cnt["w"] % 2]
            cnt["w"] += 1
            ncol = 16 if kind == "dt" else 512
            P.dma("pool", w[:, :, 0:ncol], dr["w_in"][l, :, c0:c0 + ncol].rearrange("(c p) n -> p c n", p=128),
                  [], [w_r], w_r)
            if kind in ("conv", "plain", "x12"):
                for m in range(4):
                    ci = cnt["c"]
                    cnt["c"] += 1
                    sg32, sg_r = stg[ci % 2]
                    if kind == "plain":
                        name, r0, func = prm
                        sgb = sg32.bitcast(BF16)[:, 0:U]
                        for (u0, n) in tiles:
                            pst, _, ps_r = ps_next()
                            for c in range(8):
                                P.op("pe", lambda e, c=c, w=w, pst=pst, u0=u0, n=n, m=m: e.matmul(
                                    pst[:, 0:n], lhsT=w[:, c, m * 128:(m + 1) * 128], rhs=hxT[:, c, u0:u0 + n],
                                    start=(c == 0), stop=(c == 7)), [w_r, hxT_r], [ps_r], inc=(c == 7))
                            P.op("act", lambda e, pst=pst, u0=u0, n=n, sgb=sgb, func=func: e.activation(
                                out=sgb[:, u0:u0 + n], in_=pst[:, 0:n], func=func), [ps_r], [sg_r])
                        P.dma("sp", dr[name][s, r0 + m * 128:r0 + (m + 1) * 128, :], sgb, [sg_r], [k.dres[name]], sg_r)
                        continue
                    if kind == "conv":
                        name, r0, cw, cw_r, cb, cb_r, mb, ntap, padl, func, odt = prm
                    else:
                        r0, cw, cw_r, cb, cb_r, mb, ntap, padl = prm
                        func, odt = AF.Identity, BF16
                    pbt, pb_r = pb[ci % 2]
                    dg, dg_r = diag[ci % 2]
                    for kk in range(ntap):
                        P.op("dve", lambda e, kk=kk, dg=dg, cw=cw, mm=mb + m: e.tensor_scalar(
                            out=dg[:, kk, :], in0=k.identf, scalar1=cw[:, mm, kk:kk + 1], scalar2=None, op0=ALU.mult),
                            [k.identf_r, cw_r], [dg_r])
                    for (u0, n) in tiles:
                        pst, _, ps_r = ps_next()
                        for c in range(8):
                            P.op("pe", lambda e, c=c, w=w, pst=pst, u0=u0, n=n, m=m: e.matmul(
                                pst[:, 0:n], lhsT=w[:, c, m * 128:(m + 1) * 128], rhs=hxT[:, c, u0:u0 + n],
                                start=(c == 0), stop=(c == 7)), [w_r, hxT_r], [ps_r], inc=(c == 7))
                        P.op("act", lambda e, pst=pst, u0=u0, n=n, pbt=pbt: e.activation(
                            out=pbt[:, pb_pos(u0):pb_pos(u0) + n], in_=pst[:, 0:n], func=AF.Identity), [ps_r], [pb_r])
                    sgo = sg32 if odt == F32 else sg32.bitcast(BF16)[:, 0:U]
                    for (u0, n) in tiles:
                        pst, _, ps_r = ps_next()
                        for kk in range(ntap):
                            P.op("pe", lambda e, kk=kk, dg=dg, pbt=pbt, pst=pst, u0=u0, n=n, padl=padl: e.matmul(
                                pst[:, 0:n], lhsT=dg[:, kk, :], rhs=pbt[:, pb_pos(u0) + kk - padl:pb_pos(u0) + kk - padl + n],
                                start=(kk == 0), stop=(kk == ntap - 1)), [dg_r, pb_r], [ps_r], inc=(kk == ntap - 1))
                        P.op("act", lambda e, pst=pst, u0=u0, n=n, sgo=sgo, func=func, cb=cb, mm=mb + m: e.activation(
                            out=sgo[:, u0:u0 + n], in_=pst[:, 0:n], func=func, bias=cb[:, mm:mm + 1]),
                            [ps_r, cb_r], [sg_r])
                    if kind == "conv":
                        P.dma("sp", dr[name][s, r0 + m * 128:r0 + (m + 1) * 128, :], sgo, [sg_r], [k.dres[name]], sg_r)
                    else:
                        tm, tm_r = tmst[ci % 2]
                        for i0 in range(0, U // 128, 4):
                            nn = min(4, U // 128 - i0)
                            _, psb, ps_r = ps_next()
                            for j in range(nn):
                                P.op("pe", lambda e, j=j, i0=i0, psb=psb, sgo=sgo: e.transpose(
                                    out=psb[:, j * 128:(j + 1) * 128], in_=sgo[:, (i0 + j) * 128:(i0 + j + 1) * 128],
                                    identity=k.identb), [sg_r, k.identb_r], [ps_r], inc=(j == nn - 1))
                            P.op("dve", lambda e, i0=i0, nn=nn, psb=psb, tm=tm: e.tensor_copy(
                                out=tm[:, i0:i0 + nn, :], in_=psb[:, 0:nn * 128].rearrange("p (a b) -> p a b", a=nn)),
                                [ps_r], [tm_r])
                        P.dma("sp", dr["X12"][s, :, r0 + m * 128:r0 + (m + 1) * 128].rearrange("(i p) c -> p i c", p=128),
                              tm, [tm_r], [k.dres["X12"]], tm_r)
            elif kind == "z":
                for i in range(U // 128):
                    pst, _, ps_r = ps_next()
                    for c in range(8):
                        P.op("pe", lambda e, c=c, w=w, pst=pst, i=i: e.matmul(
                            pst, lhsT=hxT[:, c, i * 128:(i + 1) * 128], rhs=w[:, c, :], start=(c == 0), stop=(c == 7)),
                            [w_r, hxT_r], [ps_r], inc=(c == 7))
                    zt, zt_r = zst[i % 2]
                    P.op("act", lambda e, pst=pst, zt=zt: e.activation(out=zt, in_=pst, func=AF.Silu), [ps_r], [zt_r])
                    P.dma("sp", dr["ZS"][s, i * 128:(i + 1) * 128, :], zt, [zt_r], [k.dres["ZS"]], zt_r)
            elif kind == "dt":
                for i in range(U // 128):
                    pst, _, ps_r = ps_next()
                    for c in range(8):
                        P.op("pe", lambda e, c=c, w=w, pst=pst, i=i: e.matmul(
                            pst[:, 0:16], lhsT=hxT[:, c, i * 128:(i + 1) * 128], rhs=w[:, c, 0:16], start=(c == 0),
                            stop=(c == 7)), [w_r, hxT_r], [ps_r], inc=(c == 7))
                    P.op("dve", lambda e, pst=pst: e.tensor_tensor(out=dtt, in0=pst[:, 0:16], in1=dtb, op=ALU.add),
                         [ps_r, dtb_r], [dtt_r])
                    P.op("act", lambda e: e.activation(out=dtt, in_=dtt, func=AF.Exp), [dtt_r], [dtt_r])
                    P.op("act", lambda e, i=i: e.activation(out=dtst[:, i, :], in_=dtt, func=AF.Ln, bias=1.0),
                         [dtt_r], [dtst_r])
                P.dma("sp", dr["DT"][s].rearrange("(i p) h -> p i h", p=128), dtst, [dtst_r], [k.dres["DT"]], dtst_r)
    A.top = base


def stage_L(k, l):
    P, A, dr = k.P, k.A, k.dr
    base = A.top
    sl = dict(allow_slow_non_contiguous=True)
    tiles = seg_tiles()
    T = [A.alloc("LT%d" % i, [U], F32) for i in range(6)]
    xab, xab_r = A.alloc("xab", [U], BF16)
    gab, gab_r = A.alloc("gab", [U], BF16)
    yab, yab_r = A.alloc("yab", [U], BF16)
    bd = [A.alloc("bd%d" % i, [128], BF16) for i in range(2)]
    cols, cols_r = A.alloc("lcols", [2, 4, 4], F32)
    for d in range(2):
        for j, nm in enumerate(("lru_b_a", "lru_b_i", "lru_lambda")):
            P.dma("sp", cols[:, d, :, j], dr[nm][l, d].rearrange("(m p) -> p m", p=128), [], [cols_r], cols_r, **sl)
    for d in range(2):
        P.op("act", lambda e, d=d: e.activation(out=cols[:, d, :, 3], in_=cols[:, d, :, 2], func=AF.Exp, scale=-1.0),
             [cols_r], [cols_r])
        P.op("act", lambda e, d=d: e.activation(out=cols[:, d, :, 3], in_=cols[:, d, :, 3], func=AF.Ln, bias=1.0),
             [cols_r], [cols_r])
        P.op("dve", lambda e, d=d: e.tensor_scalar(out=cols[:, d, :, 3], in0=cols[:, d, :, 3], scalar1=-8.0, scalar2=None,
                                                   op0=ALU.mult), [cols_r], [cols_r])
    cnt = {"ps": 0}

    def ps_next():
        cnt["ps"] += 1
        return k.ps[cnt["ps"] % 4]

    for s in range(NSEQ):
        for m in range(4):
            xa, xa_r = T[0]
            hs, hs_r = T[5]
            P.dma("sp", xa, dr["XA"][s, m * 128:(m + 1) * 128, :], [k.dres["XA"]], [xa_r], xa_r)
            P.dma("sp", gab, dr["GA"][s, m * 128:(m + 1) * 128, :], [k.dres["GA"]], [gab_r], gab_r)
            P.op("pool", lambda e, xa=xa: e.tensor_copy(out=xab, in_=xa), [xa_r], [xab_r])
            for d in range(2):
                gates = []
                for gi, (wn, bj) in enumerate((("lru_w_a", 0), ("lru_w_i", 1))):
                    b_, b_r = bd[gi]
                    P.op("pool", lambda e, b_=b_: e.memset(b_, 0.0), [], [b_r])
                    for h in range(2):
                        P.dma("pool", b_[h * 64:(h + 1) * 64, h * 64:(h + 1) * 64], dr[wn][l, d, 2 * m + h], [], [b_r], b_r)
                    g_, g_r = T[1 + gi]
                    for (u0, n) in tiles:
                        pst, _, ps_r = ps_next()
                        P.op("pe", lambda e, b_=b_, pst=pst, u0=u0, n=n: e.matmul(pst[:, 0:n], lhsT=b_, rhs=xab[:, u0:u0 + n],
                                                                                    start=True, stop=True), [b_r, xab_r], [ps_r])
                        P.op("act", lambda e, pst=pst, u0=u0, n=n, g_=g_, bj=bj, d=d, m=m: e.activation(
                            out=g_[:, u0:u0 + n], in_=pst[:, 0:n], func=AF.Sigmoid, bias=cols[:, d, m, bj:bj + 1]),
                            [ps_r, cols_r], [g_r])
                    gates.append((g_, g_r))
                (r_, r_r), (i_, i_r) = gates
                a_, a_r = T[3]
                q_, q_r = T[4]
                P.op("act", lambda e, d=d, m=m: e.activation(out=a_, in_=r_, func=AF.Exp, scale=cols[:, d, m, 3:4]),
                     [r_r, cols_r], [a_r])
                P.op("dve", lambda e: e.tensor_tensor(out=q_, in0=a_, in1=a_, op=ALU.mult), [a_r], [q_r])
                P.op("act", lambda e: e.activation(out=q_, in_=q_, func=AF.Sqrt, scale=-1.0, bias=1.0), [q_r], [q_r])
                P.op("pool", lambda e, xa=xa: e.tensor_tensor(out=i_, in0=i_, in1=xa, op=ALU.mult), [i_r, xa_r], [i_r])
                P.op("dve", lambda e: e.tensor_tensor(out=q_, in0=q_, in1=i_, op=ALU.mult), [q_r, i_r], [q_r])
                h_, h_r = r_, r_r
                if d == 0:
                    P.op("dve", lambda e: e.tensor_tensor_scan(out=hs, data0=a_, data1=q_, initial=0.0, op0=ALU.mult,
                                                               op1=ALU.add), [a_r, q_r], [hs_r])
                    if "DBG1" in k.dumps:
                        P.dma("sp", dr["DBG1"][s, m * 128:(m + 1) * 128, :], hs, [hs_r], [k.dres["DBG1"]], hs_r)
                        P.dma("sp", dr["DBG3"][s, m * 128:(m + 1) * 128, :], a_, [a_r], [k.dres["DBG3"]], a_r)
                        P.dma("sp", dr["DBG4"][s, m * 128:(m + 1) * 128, :], q_, [q_r], [k.dres["DBG4"]], q_r)
                else:
                    def rev(ap, lo, n):
                        return bass.AP(ap.tensor, ap.offset + lo + n - 1, [list(ap.ap[0]), [-1, n]])
                    P.op("dve", lambda e: e.tensor_tensor_scan(out=rev(h_, 0, CTX), data0=rev(a_, 0, CTX), data1=rev(q_, 0, CTX),
                                                               initial=0.0, op0=ALU.mult, op1=ALU.add), [a_r, q_r], [h_r])
                    P.op("dve", lambda e: e.tensor_tensor_scan(out=rev(h_, CTX, SEQ), data0=rev(a_, CTX, SEQ),
                                                               data1=rev(q_, CTX, SEQ), initial=h_[:, 0:1], op0=ALU.mult,
                                                               op1=ALU.add), [a_r, q_r, h_r], [h_r])
                    if "DBG2" in k.dumps:
                        P.dma("sp", dr["DBG2"][s, m * 128:(m + 1) * 128, :], h_, [h_r], [k.dres["DBG2"]], h_r)
                    P.op("pool", lambda e: e.tensor_tensor(out=hs, in0=hs, in1=h_, op=ALU.add), [hs_r, h_r], [hs_r])
            P.op("dve", lambda e: e.tensor_tensor(out=yab, in0=hs, in1=gab, op=ALU.mult), [hs_r, gab_r], [yab_r])
            P.dma("sp", dr["YA"][s, m * 128:(m + 1) * 128, :], yab, [yab_r], [k.dres["YA"]], yab_r)
    A.top = base


def stage_L2(k, l, bank):
    P, A, dr = k.P, k.A, k.dr
    sl = dict(allow_slow_non_contiguous=True)
    tiles = seg_tiles()
    W = 512

    def al(name, dt=F32, w=W):
        return A.alloc(name, [w], dt)
    xa, xa_r = al("l2xa")
    xab, xab_r = al("l2xab", BF16)
    r_, r_r = al("l2r")
    i_, i_r = al("l2i")
    a_, a_r = al("l2a")
    q_, q_r = al("l2q")
    h_, h_r = al("l2h")
    hf, hf_r = al("l2hf")
    gab, gab_r = al("l2ga", BF16)
    yab, yab_r = al("l2ya", BF16)
    car, car_r = A.alloc("l2car", [4], F32)
    bd = [A.alloc("l2bd%d" % i, [128], BF16) for i in range(2)]
    cols, cols_r = A.alloc("l2cols", [2, 4, 4], F32)
    for d in range(2):
        for j, nm in enumerate(("lru_b_a", "lru_b_i", "lru_lambda")):
            P.dma("sp", cols[:, d, :, j], dr[nm][l, d].rearrange("(m p) -> p m", p=128), [], [cols_r], cols_r, **sl)
    for d in range(2):
        P.op("act", lambda e, d=d: e.activation(out=cols[:, d, :, 3], in_=cols[:, d, :, 2], func=AF.Exp, scale=-1.0), [cols_r], [cols_r])
        P.op("act", lambda e, d=d: e.activation(out=cols[:, d, :, 3], in_=cols[:, d, :, 3], func=AF.Ln, bias=1.0), [cols_r], [cols_r])
        P.op("dve", lambda e, d=d: e.tensor_scalar(out=cols[:, d, :, 3], in0=cols[:, d, :, 3], scalar1=-8.0, scalar2=None, op0=ALU.mult),
             [cols_r], [cols_r])
    yield
    pst, _, ps_r = k.ps[bank]

    def rev(ap, n):
        return bass.AP(ap.tensor, ap.offset + n - 1, [list(ap.ap[0]), [-1, n]])

    for s in range(NSEQ):
        for m in range(4):
            for d in range(2):
                for gi, wn in enumerate(("lru_w_a", "lru_w_i")):
                    b_, b_r = bd[gi]
                    P.op("pool", lambda e, b_=b_: e.memset(b_, 0.0), [], [b_r])
                    for hh in range(2):
                        P.dma("pool", b_[hh * 64:(hh + 1) * 64, hh * 64:(hh + 1) * 64], dr[wn][l, d, 2 * m + hh], [], [b_r], b_r)
                yield
                order = tiles if d == 0 else [tiles[0]] + tiles[:0:-1]
                for ti, (u0, n) in enumerate(order):
                    P.dma("sp", xa[:, 0:n], dr["XA"][s, m * 128:(m + 1) * 128, u0:u0 + n], [k.dres["XA"]], [xa_r], xa_r)
                    if d == 1:
                        P.dma("sp", hf[:, 0:n], dr["HFWD"][s, m * 128:(m + 1) * 128, u0:u0 + n], [k.dres["HFWD"]], [hf_r], hf_r)
                        P.dma("sp", gab[:, 0:n], dr["GA"][s, m * 128:(m + 1) * 128, u0:u0 + n], [k.dres["GA"]], [gab_r], gab_r)
                    yield
                    P.op("act", lambda e, n=n: e.activation(out=xab[:, 0:n], in_=xa[:, 0:n], func=AF.Identity), [xa_r], [xab_r])
                    yield
                    for gi, (g_, g_r, bj) in enumerate(((r_, r_r, 0), (i_, i_r, 1))):
                        b_, b_r = bd[gi]
                        P.op("pe", lambda e, b_=b_, n=n: e.matmul(pst[:, 0:n], lhsT=b_, rhs=xab[:, 0:n], start=True, stop=True),
                             [b_r, xab_r], [ps_r])
                        P.op("act", lambda e, n=n, g_=g_, bj=bj, d=d, m=m: e.activation(
                            out=g_[:, 0:n], in_=pst[:, 0:n], func=AF.Sigmoid, bias=cols[:, d, m, bj:bj + 1]), [ps_r, cols_r], [g_r])
                        yield
                    P.op("act", lambda e, n=n, d=d, m=m: e.activation(out=a_[:, 0:n], in_=r_[:, 0:n], func=AF.Exp, scale=cols[:, d, m, 3:4]),
                         [r_r, cols_r], [a_r])
                    yield
                    P.op("dve", lambda e, n=n: e.tensor_tensor(out=q_[:, 0:n], in0=a_[:, 0:n], in1=a_[:, 0:n], op=ALU.mult), [a_r], [q_r])
                    yield
                    P.op("act", lambda e, n=n: e.activation(out=q_[:, 0:n], in_=q_[:, 0:n], func=AF.Sqrt, scale=-1.0, bias=1.0), [q_r], [q_r])
                    yield
                    P.op("dve", lambda e, n=n: e.tensor_tensor(out=i_[:, 0:n], in0=i_[:, 0:n], in1=xa[:, 0:n], op=ALU.mult), [i_r, xa_r], [i_r])
                    yield
                    P.op("dve", lambda e, n=n: e.tensor_tensor(out=q_[:, 0:n], in0=q_[:, 0:n], in1=i_[:, 0:n], op=ALU.mult), [q_r, i_r], [q_r])
                    yield
                    first = (ti == 0)
                    if d == 0:
                        P.op("dve", lambda e, n=n, first=first: e.tensor_tensor_scan(
                            out=h_[:, 0:n], data0=a_[:, 0:n], data1=q_[:, 0:n], initial=(0.0 if first else car[:, 0:1]), op0=ALU.mult,
                            op1=ALU.add), [a_r, q_r, car_r], [h_r])
                        yield
                        P.op("dve", lambda e, n=n: e.tensor_copy(out=car[:, 0:1], in_=h_[:, n - 1:n]), [h_r], [car_r])
                        yield
                        P.dma("sp", dr["HFWD"][s, m * 128:(m + 1) * 128, u0:u0 + n], h_[:, 0:n], [h_r], [k.dres["HFWD"]], h_r)
                        yield
                    else:
                        P.op("dve", lambda e, n=n, first=first: e.tensor_tensor_scan(
                            out=rev(h_, n), data0=rev(a_, n), data1=rev(q_, n), initial=(0.0 if first else car[:, 0:1]), op0=ALU.mult,
                            op1=ALU.add), [a_r, q_r, car_r], [h_r])
                        yield
                        P.op("dve", lambda e: e.tensor_copy(out=car[:, 0:1], in_=h_[:, 0:1]), [h_r], [car_r])
                        yield
                        P.op("dve", lambda e, n=n: e.tensor_tensor(out=h_[:, 0:n], in0=h_[:, 0:n], in1=hf[:, 0:n], op=ALU.add), [h_r, hf_r], [h_r])
                        yield
                        P.op("dve", lambda e, n=n: e.tensor_tensor(out=yab[:, 0:n], in0=h_[:, 0:n], in1=gab[:, 0:n], op=ALU.mult),
                             [h_r, gab_r], [yab_r])
                        yield
                        P.dma("sp", dr["YA"][s, m * 128:(m + 1) * 128, u0:u0 + n], yab[:, 0:n], [yab_r], [k.dres["YA"]], yab_r)
                        yield


def bc(ap, dims):
    return bass.AP(ap.tensor, ap.offset, [list(ap.ap[0])] + [list(d) for d in dims])


def stage_S(k, l, banks=None):
    P, A, dr = k.P, k.A, k.dr
    base = A.top
    tri = []
    for nm in ("tri_f", "tri_b"):
        t, r = A.alloc(nm, [128], F32)
        P.dma("sp", t, dr[nm], [], [r], r)
        tri.append((t, r))
    mneg = []
    for nm in ("mneg_f", "mneg_b"):
        t, r = A.alloc(nm, [128], F32)
        P.dma("sp", t, dr[nm], [], [r], r)
        mneg.append((t, r))
    ones, ones_r = A.alloc("ones", [128], F32)
    P.op("pool", lambda e: e.memset(ones, 1.0), [], [ones_r])
    arow, arow_r = A.alloc("arow", [16], F32)
    drow, drow_r = A.alloc("drow", [8], F32)
    nrow, nrow_r = A.alloc("nrow", [BW], F32)
    P.dma("sp", arow, dr["ssm_a_log"][l:l + 1].rearrange("o a b -> o (a b)").partition_broadcast(128), [], [arow_r], arow_r)
    P.op("act", lambda e: e.activation(out=arow, in_=arow, func=AF.Exp), [arow_r], [arow_r])
    P.op("dve", lambda e: e.tensor_scalar(out=arow, in0=arow, scalar1=-1.0, scalar2=None, op0=ALU.mult), [arow_r], [arow_r])
    P.dma("sp", drow, dr["ssm_d"][l:l + 1].partition_broadcast(128), [], [drow_r], drow_r)
    P.dma("sp", nrow, dr["ssm_norm"][l:l + 1].partition_broadcast(128), [], [nrow_r], nrow_r)
    NCH = 2 if banks is None else 1
    Hs = [(A.alloc("H%d" % c, [8, 64], F32), A.alloc("Hb%d" % c, [8, 64], BF16)) for c in range(NCH)]
    NSL = 4 if banks is None else 2

    def many(name, shape, dt):
        return [A.alloc(name + str(i), shape, dt) for i in range(NSL)]
    T = {}
    for nm, shp, dt in (("xbc", [8, 128], BF16), ("dtc", [16], F32), ("xtok", [8, 64], BF16), ("btok", [2, 128], BF16),
                        ("acol", [8], F32), ("xdt", [8, 64], BF16), ("rhs2", [8, 128], F32), ("csc", [8], F32),
                        ("csr", [8, 128], F32), ("Dm", [8, 128], F32), ("M", [8, 128], BF16), ("Ecs", [8, 128], F32),
                        ("CTs", [8, 128], BF16), ("dec", [8], F32), ("xdd", [8, 64], BF16), ("ysum", [BW], F32),
                        ("zs", [BW], BF16), ("sst", [4], F32), ("yc", [BW], BF16), ("ycT", [4, 128], BF16)):
        T[nm] = many(nm, shp, dt)
    sq, sq_r = A.alloc("ssq", [BW], F32)
    PSC = []
    for ch_ in range(NCH):
        ia, ie, ic = (0 + ch_, 2 + ch_, 4 + ch_) if banks is None else banks[0:3]
        b0f, b0b, rA = k.ps[ia]
        b1f, b1b, rE = k.ps[ie]
        PSC.append(dict(psA=b0b, psA_r=rA, psB=b0f[:, 384:392], psB_r=rA, psE=b1f[:, 0:256], psE_r=rE,
                        psT=b1b[:, 512:1024], psT_r=rE, psC=k.ps[ic][0], psC_r=k.ps[ic][2]))
    iy, isb = (6, 7) if banks is None else banks[3:5]

    def p1(s, d, ci, sl):
        pc = PSC[s if NCH == 2 else 0]
        psA, psA_r, psB, psB_r, psE, psE_r = pc["psA"], pc["psA_r"], pc["psB"], pc["psB_r"], pc["psE"], pc["psE_r"]
        trd, trd_r = tri[d]
        qe = 127 if d == 0 else 0
        u0 = ci * 128
        g = {nm: T[nm][sl] for nm in T}
        xbc, xbc_r = g["xbc"]
        dtc, dtc_r = g["dtc"]
        P.dma("sp", xbc, dr["XBC"][s, :, u0:u0 + 128].rearrange("(c p) u -> p c u", p=128), [k.dres["XBC"]], [xbc_r], xbc_r)
        yield
        P.dma("sp", dtc, dr["DT"][s, u0:u0 + 128, :], [k.dres["DT"]], [dtc_r], dtc_r)
        yield
        for c in range(6):
            P.op("pe", lambda e, c=c: e.transpose(out=psA[:, c * 128:(c + 1) * 128], in_=xbc[:, c, :], identity=k.identb),
                 [xbc_r, k.identb_r], [psA_r], inc=(c == 5))
            yield
        xtok, xtok_r = g["xtok"]
        btok, btok_r = g["btok"]
        P.op("act", lambda e: e.activation(out=xtok, in_=psA[:, 0:512].rearrange("p (h q) -> p h q", h=8), func=AF.Identity),
             [psA_r], [xtok_r])
        yield
        P.op("act", lambda e: e.activation(out=btok, in_=psA[:, 512:768].rearrange("p (h q) -> p h q", h=2), func=AF.Identity),
             [psA_r], [btok_r])
        yield
        acol, acol_r = g["acol"]
        P.op("dve", lambda e: e.tensor_tensor(out=acol, in0=dtc[:, 8 * d:8 * d + 8], in1=arow[:, 8 * d:8 * d + 8], op=ALU.mult),
             [dtc_r, arow_r], [acol_r])
        yield
        xdt, xdt_r = g["xdt"]
        P.op("dve", lambda e: e.tensor_tensor(out=xdt, in0=xtok, in1=bc(dtc[:, 8 * d:8 * d + 8], [[1, 8], [0, 64]]), op=ALU.mult),
             [xtok_r, dtc_r], [xdt_r])
        yield
        rhs2, rhs2_r = g["rhs2"]
        P.op("dve", lambda e: e.tensor_tensor(out=rhs2, in0=bc(acol, [[1, 8], [0, 128]]), in1=bc(trd, [[0, 8], [1, 128]]), op=ALU.mult),
             [acol_r, trd_r], [rhs2_r])
        yield
        P.op("pe", lambda e: e.matmul(psB, lhsT=trd, rhs=acol, start=True, stop=True), [trd_r, acol_r], [psB_r])
        yield
        csc, csc_r = g["csc"]
        P.op("dve", lambda e: e.tensor_copy(out=csc, in_=psB), [psB_r], [csc_r])
        yield
        csr, csr_r = g["csr"]
        for hh in range(2):
            psC, psC_r = pc["psC"], pc["psC_r"]
            P.op("pe", lambda e, psC=psC, hh=hh: e.matmul(psC, lhsT=ones, rhs=rhs2[:, 4 * hh:4 * hh + 4, :], start=True, stop=True),
                 [ones_r, rhs2_r], [psC_r])
            yield
            P.op("act", lambda e, psC=psC, hh=hh: e.activation(out=csr[:, 4 * hh:4 * hh + 4, :], in_=psC.rearrange("p (h q) -> p h q", h=4),
                                                             func=AF.Identity), [psC_r], [csr_r])
            yield
        Dm, Dm_r = g["Dm"]
        P.op("dve", lambda e: e.tensor_tensor(out=Dm, in0=csr, in1=bc(csc, [[1, 8], [0, 128]]), op=ALU.subtract), [csr_r, csc_r], [Dm_r])
        yield
        P.op("dve", lambda e: e.tensor_tensor(out=Dm, in0=Dm, in1=bc(mneg[d][0], [[0, 8], [1, 128]]), op=ALU.add), [Dm_r, mneg[d][1]], [Dm_r])
        yield
        P.op("act", lambda e: e.activation(out=Dm, in_=Dm, func=AF.Exp), [Dm_r], [Dm_r])
        yield
        for gg in range(2):
            P.op("pe", lambda e, gg=gg: e.matmul(psE[:, gg * 128:(gg + 1) * 128], lhsT=xbc[:, 4 + gg, :], rhs=xbc[:, 6 + gg, :],
                                                 start=True, stop=True), [xbc_r], [psE_r], inc=(gg == 1))
            yield
        M, M_r = g["M"]
        for gg in range(2):
            P.op("dve", lambda e, gg=gg: e.tensor_tensor(out=M[:, 4 * gg:4 * gg + 4, :], in0=Dm[:, 4 * gg:4 * gg + 4, :],
                                                         in1=bc(psE[:, gg * 128:(gg + 1) * 128], [[0, 4], [1, 128]]), op=ALU.mult),
                 [Dm_r, psE_r], [M_r])
            yield
        Ecs, Ecs_r = g["Ecs"]
        P.op("act", lambda e: e.activation(out=Ecs, in_=csr, func=AF.Exp), [csr_r], [Ecs_r])
        yield
        CTs, CTs_r = g["CTs"]
        for gg in range(2):
            P.op("dve", lambda e, gg=gg: e.tensor_tensor(out=CTs[:, 4 * gg:4 * gg + 4, :], in0=Ecs[:, 4 * gg:4 * gg + 4, :],
                                                          in1=bc(xbc[:, 6 + gg, :], [[0, 4], [1, 128]]), op=ALU.mult),
                 [Ecs_r, xbc_r], [CTs_r])
            yield
        dec, dec_r = g["dec"]
        P.op("dve", lambda e: e.tensor_tensor(out=dec, in0=csr[:, :, qe], in1=csc, op=ALU.subtract), [csr_r, csc_r], [dec_r])
        yield
        P.op("act", lambda e: e.activation(out=dec, in_=dec, func=AF.Exp), [dec_r], [dec_r])
        yield
        xdd, xdd_r = g["xdd"]
        P.op("dve", lambda e: e.tensor_tensor(out=xdd, in0=xdt, in1=bc(dec, [[1, 8], [0, 64]]), op=ALU.mult), [xdt_r, dec_r], [xdd_r])
        yield
        if d == 1:
            ysum, ysum_r = g["ysum"]
            zs, zs_r = g["zs"]
            P.dma("sp", ysum, dr["YS"][s, u0:u0 + 128, :], [k.dres["YS"]], [ysum_r], ysum_r)
            yield
            P.dma("sp", zs, dr["ZS"][s, u0:u0 + 128, :], [k.dres["ZS"]], [zs_r], zs_r)
            yield

    def p2(s, d, ci, sl, chain):
        pc = PSC[chain]
        psT, psT_r = pc["psT"], pc["psT_r"]
        qe = 127 if d == 0 else 0
        u0 = ci * 128
        g = {nm: T[nm][sl] for nm in T}
        (H, H_r), (Hb, Hb_r) = Hs[chain]
        CTs, CTs_r = g["CTs"]
        Ecs, Ecs_r = g["Ecs"]
        xtok, xtok_r = g["xtok"]
        psY, _, psY_r = k.ps[iy]
        psS, _, psS_r = k.ps[isb]
        M, M_r = g["M"]
        xdt, xdt_r = g["xdt"]
        xdd, xdd_r = g["xdd"]
        btok, btok_r = g["btok"]
        for gg in range(2):
            P.op("pe", lambda e, gg=gg: e.matmul(psS[:, 256 * gg:256 * gg + 256], lhsT=btok[:, gg, :],
                                                 rhs=xdd[:, 4 * gg:4 * gg + 4, :], start=True, stop=True),
                 [btok_r, xdd_r], [psS_r], inc=(gg == 1))
            yield
        for h in range(8):
            P.op("pe", lambda e, h=h: e.matmul(psY[:, 64 * h:64 * h + 64], lhsT=M[:, h, :], rhs=xdt[:, h, :],
                                               start=True, stop=False), [M_r, xdt_r], [psY_r], inc=False)
            yield
            P.op("pe", lambda e, h=h: e.matmul(psY[:, 64 * h:64 * h + 64], lhsT=CTs[:, h, :], rhs=Hb[:, h, :], start=False,
                                               stop=True), [CTs_r, Hb_r], [psY_r], inc=(h == 7))
            yield
        P.op("dve", lambda e: e.tensor_tensor(out=H, in0=H, in1=bc(Ecs[:, :, qe], [[128, 8], [0, 64]]), op=ALU.mult), [H_r, Ecs_r], [H_r])
        yield
        P.op("dve", lambda e: e.tensor_tensor(out=H, in0=H, in1=psS.rearrange("p (h q) -> p h q", h=8), op=ALU.add), [H_r, psS_r], [H_r])
        yield
        P.op("act", lambda e: e.activation(out=Hb, in_=H, func=AF.Identity), [H_r], [Hb_r])
        yield
        ysum, ysum_r = g["ysum"]
        if d == 0:
            P.op("dve", lambda e: e.tensor_tensor(out=ysum.rearrange("p (h q) -> p h q", h=8), in0=xtok, in1=bc(drow, [[1, 8], [0, 64]]),
                                                   op=ALU.mult), [xtok_r, drow_r], [ysum_r])
            yield
            P.op("dve", lambda e: e.tensor_tensor(out=ysum, in0=ysum, in1=psY, op=ALU.add), [ysum_r, psY_r], [ysum_r])
            yield
            P.dma("sp", dr["YS"][s, u0:u0 + 128, :], ysum, [ysum_r], [k.dres["YS"]], ysum_r)
            yield
        else:
            zs, zs_r = g["zs"]
            P.op("dve", lambda e: e.tensor_tensor(out=ysum, in0=ysum, in1=psY, op=ALU.add), [ysum_r, psY_r], [ysum_r])
            yield
            P.op("dve", lambda e: e.tensor_tensor(out=ysum, in0=ysum, in1=zs, op=ALU.mult), [ysum_r, zs_r], [ysum_r])
            yield
            ss, ss_r = g["sst"]
            P.op("act", lambda e: e.activation(out=sq, in_=ysum, func=AF.Square, accum_out=ss[:, 0:1]), [ysum_r], [sq_r, ss_r])
            yield
            P.op("dve", lambda e: e.tensor_scalar(out=ss[:, 1:2], in0=ss[:, 0:1], scalar1=1.0 / BW, scalar2=EPS, op0=ALU.mult,
                                                  op1=ALU.add), [ss_r], [ss_r])
            yield
            P.op("act", lambda e: e.activation(out=ss[:, 2:3], in_=ss[:, 1:2], func=AF.Sqrt), [ss_r], [ss_r])
            yield
            P.op("dve", lambda e: e.reciprocal(out=ss[:, 3:4], in_=ss[:, 2:3]), [ss_r], [ss_r])
            yield
            yc, yc_r = g["yc"]
            P.op("dve", lambda e: e.scalar_tensor_tensor(out=yc, in0=ysum, scalar=ss[:, 3:4], in1=nrow, op0=ALU.mult, op1=ALU.mult),
                 [ysum_r, ss_r, nrow_r], [yc_r])
            yield
            for c in range(4):
                P.op("pe", lambda e, c=c: e.transpose(out=psT[:, c * 128:(c + 1) * 128], in_=yc[:, c * 128:(c + 1) * 128],
                                                      identity=k.identb), [yc_r, k.identb_r], [psT_r], inc=(c == 3))
                yield
            ycT, ycT_r = g["ycT"]
            P.op("act", lambda e: e.activation(out=ycT, in_=psT.rearrange("p (c q) -> p c q", c=4), func=AF.Identity), [psT_r], [ycT_r])
            yield
            P.dma("sp", dr["YC"][s, :, u0:u0 + 128].rearrange("(c p) u -> p c u", p=128), ycT, [ycT_r], [k.dres["YC"]], ycT_r)
            yield

    def lockstep(gens):
        gens = list(gens)
        while gens:
            for g_ in list(gens):
                try:
                    next(g_)
                except StopIteration:
                    gens.remove(g_)

    for d in range(2):
        order = list(range(U // 128)) if d == 0 else [1, 0] + list(range(U // 128 - 1, 1, -1))
        if banks is None:
            for c in range(NCH):
                (H, H_r), (Hb, Hb_r) = Hs[c]
                P.op("pool", lambda e, H=H: e.memset(H, 0.0), [], [H_r])
                P.op("pool", lambda e, Hb=Hb: e.memset(Hb, 0.0), [], [Hb_r])
            prev = None
            for idx, ci in enumerate(order):
                sls = [2 * s + idx % 2 for s in range(NSEQ)]
                lockstep([p1(s, d, ci, sls[s]) for s in range(NSEQ)])
                if prev is not None:
                    for pv in prev:
                        for _ in p2(*pv):
                            pass
                prev = [(s, d, ci, sls[s], s) for s in range(NSEQ)]
                yield
            for pv in prev:
                for _ in p2(*pv):
                    pass
        else:
            for s in range(NSEQ):
                (H, H_r), (Hb, Hb_r) = Hs[0]
                P.op("pool", lambda e, H=H: e.memset(H, 0.0), [], [H_r])
                P.op("pool", lambda e, Hb=Hb: e.memset(Hb, 0.0), [], [Hb_r])
                prev = None
                for idx, ci in enumerate(order):
                    for _ in p1(s, d, ci, idx % 2):
                        yield
                    if prev is not None:
                        for _ in p2(*prev):
                            yield
                    prev = (s, d, ci, idx % 2, 0)
                for _ in p2(*prev):
                    yield
    A.top = base


def xs_src(which):
    def f(k, l, s, u0, n):
        return k.dr["XS"][which, s, u0:u0 + n, :], k.dres["XS"]
    return f


def norm_residual(k, pss, xsrc, xsres, grow, grow_r, dsts, bufs, i):
    P = k.P
    (xt, xt_r), (ss, ss_r), (sq, sq_r), (o, o_r) = bufs[i % 2]
    P.dma("sp", xt, xsrc, [xsres] if xsres else [], [xt_r], xt_r)
    for h in range(2):
        P.op("act", lambda e, h=h, ss=ss, sq=sq: e.activation(out=sq, in_=pss[h][0], func=AF.Square, accum_out=ss[:, h:h + 1]),
             [pss[h][2]], [sq_r, ss_r])
    P.op("dve", lambda e, ss=ss: e.tensor_tensor(out=ss[:, 2:3], in0=ss[:, 0:1], in1=ss[:, 1:2], op=ALU.add), [ss_r], [ss_r])
    P.op("dve", lambda e, ss=ss: e.tensor_scalar(out=ss[:, 3:4], in0=ss[:, 2:3], scalar1=1.0 / D, scalar2=EPS, op0=ALU.mult,
                                                 op1=ALU.add), [ss_r], [ss_r])
    P.op("act", lambda e, ss=ss: e.activation(out=ss[:, 4:5], in_=ss[:, 3:4], func=AF.Sqrt), [ss_r], [ss_r])
    P.op("dve", lambda e, ss=ss: e.reciprocal(out=ss[:, 5:6], in_=ss[:, 4:5]), [ss_r], [ss_r])
    for h in range(2):
        P.op("dve", lambda e, h=h, ss=ss, o=o: e.scalar_tensor_tensor(
            out=o[:, h * 512:(h + 1) * 512], in0=pss[h][0], scalar=ss[:, 5:6], in1=grow[:, h * 512:(h + 1) * 512],
            op0=ALU.mult, op1=ALU.mult), [pss[h][2], ss_r, grow_r], [o_r])
    P.op("pool", lambda e, o=o, xt=xt: e.tensor_tensor(out=o, in0=o, in1=xt, op=ALU.add), [o_r, xt_r], [o_r])
    for (dst, dres) in dsts:
        P.dma("sp", dst, o, [o_r], [dres], o_r)


def nr_bufs(k):
    A = k.A
    return [(A.alloc("nrx%d" % i, [D], F32), A.alloc("nrs%d" % i, [8], F32), A.alloc("nrq%d" % i, [512], F32),
             A.alloc("nro%d" % i, [D], F32)) for i in range(2)]


def stage_M(k, l):
    P, A, dr = k.P, k.A, k.dr
    base = A.top
    wb, wb_r = A.alloc("wb", [12, D], BF16)
    wo, wo_r = A.alloc("wo", [8, D], BF16)
    for kb in range(3):
        P.dma("pool", wb[:, 4 * kb:4 * kb + 4, :], dr["w_branch"][l, kb].rearrange("(c p) n -> p c n", p=128), [], [wb_r], wb_r)
    P.dma("pool", wo, dr["w_out"][l].rearrange("(c p) n -> p c n", p=128), [], [wo_r], wo_r)
    grows = {v: load_modrow(k, l, v, 2, "G1row%d" % v) for v in range(3)}
    yb2 = [[A.alloc("my%d_%d" % (kb, i), [4, 512], BF16) for kb in range(3)] for i in range(2)]
    gt2 = [A.alloc("mg%d" % i, [24, 512], BF16) for i in range(2)]
    mg2 = [A.alloc("mm%d" % i, [8, 512], BF16) for i in range(2)]
    t1, t1_r = A.alloc("mt1", [512], F32)
    t2, t2_r = A.alloc("mt2", [512], F32)
    bufs = nr_bufs(k)
    names = ("YA", "YB", "YC")
    it = 0
    sub = 0
    last = (l == DEPTH - 1)
    subc = [0]

    def p1(s, u0, n, i2):
        ys = yb2[i2]
        for kb in range(3):
            P.dma("sp", ys[kb][0][:, :, 0:n], dr[names[kb]][s, :, u0:u0 + n].rearrange("(c p) u -> p c u", p=128),
                  [k.dres[names[kb]]], [ys[kb][1]], ys[kb][1])
        gt, gt_r = gt2[i2]
        for kb in range(3):
            P.dma("sp", gt[:, 8 * kb:8 * kb + 8, 0:n], dr["GATES"][s, kb * D:(kb + 1) * D, u0:u0 + n].rearrange("(c p) u -> p c u", p=128),
                  [k.dres["GATES"]], [gt_r], gt_r)
        mg, mg_r = mg2[i2]
        for j in range(8):
            pks = [k.ps[(3 * j + kb) % 4] for kb in range(3)]
            for kb in range(3):
                pst, _, ps_r = pks[kb]
                for c in range(4):
                    P.op("pe", lambda e, c=c, kb=kb, j=j, pst=pst, y=ys[kb][0], n=n: e.matmul(
                        pst[:, 0:n], lhsT=wb[:, 4 * kb + c, j * 128:(j + 1) * 128], rhs=y[:, c, 0:n], start=(c == 0),
                        stop=(c == 3)), [wb_r, ys[kb][1]], [ps_r], inc=(c == 3))
            P.op("dve", lambda e, j=j, gt=gt, p0=pks[0][0], n=n: e.tensor_tensor(out=t1[:, 0:n], in0=gt[:, j, 0:n], in1=p0[:, 0:n],
                                                                              op=ALU.mult), [gt_r, pks[0][2]], [t1_r])
            P.op("dve", lambda e, j=j, gt=gt, p1=pks[1][0], n=n: e.tensor_tensor(out=t2[:, 0:n], in0=gt[:, 8 + j, 0:n], in1=p1[:, 0:n],
                                                                              op=ALU.mult), [gt_r, pks[1][2]], [t2_r])
            P.op("dve", lambda e, n=n: e.tensor_tensor(out=t1[:, 0:n], in0=t1[:, 0:n], in1=t2[:, 0:n], op=ALU.add), [t1_r, t2_r], [t1_r])
            P.op("dve", lambda e, j=j, gt=gt, p2=pks[2][0], n=n: e.tensor_tensor(out=t2[:, 0:n], in0=gt[:, 16 + j, 0:n], in1=p2[:, 0:n],
                                                                              op=ALU.mult), [gt_r, pks[2][2]], [t2_r])
            P.op("dve", lambda e, j=j, mg=mg, n=n: e.tensor_tensor(out=mg[:, j, 0:n], in0=t1[:, 0:n], in1=t2[:, 0:n], op=ALU.add),
                 [t1_r, t2_r], [mg_r])

    def p2(s, u0, n, i2):
        mg, mg_r = mg2[i2]
        for i in range(n // 128):
            pss = [k.ps[4 + 2 * (subc[0] % 2)], k.ps[5 + 2 * (subc[0] % 2)]]
            for h in range(2):
                for c in range(8):
                    P.op("pe", lambda e, c=c, h=h, i=i, mg=mg, pst=pss[h][0]: e.matmul(
                        pst, lhsT=mg[:, c, i * 128:(i + 1) * 128], rhs=wo[:, c, h * 512:(h + 1) * 512], start=(c == 0),
                        stop=(c == 7)), [mg_r, wo_r], [pss[h][2]], inc=(c == 7))
            uu = u0 + i * 128
            v = 2 if uu < CTX else s
            xsrc, xsres = stream_src(k, l, s, uu, 128)
            norm_residual(k, pss, xsrc, xsres, grows[v][0], grows[v][1], [(dr["XS"][0, s, uu:uu + 128, :], k.dres["XS"])], bufs, subc[0])
            subc[0] += 1

    tl = [(s, u0, n) for s in range(NSEQ) for (u0, n) in seg_tiles() if not (last and u0 < CTX)]
    prev = None
    for it_, (s, u0, n) in enumerate(tl):
        p1(s, u0, n, it_ % 2)
        if prev is not None:
            p2(*prev)
        prev = (s, u0, n, it_ % 2)
    p2(*prev)
    A.top = base


FC_CTX = 1
FC_LAT = 258 + 65
FC_W = 258 + 65 + SEQ + 65


def fc_pos(u):
    return FC_CTX + u if u < CTX else FC_LAT + (u - CTX)


def stage_F(k, l):
    P, A, dr = k.P, k.A, k.dr
    last = (l == DEPTH - 1)
    base = A.top
    sl = dict(allow_slow_non_contiguous=True)
    hxT, hxT_r = A.alloc("hxT2", [8, U], BF16)
    cwf, cwf_r = A.alloc("cwf", [9, 44], F32)
    cbf, cbf_r = A.alloc("cbf", [44], F32)
    for t9 in range(9):
        P.dma("sp", cwf[:, t9, :], dr["ffn_conv_w"][l, t9 // 3, t9 % 3].rearrange("(m p) -> p m", p=128), [], [cwf_r], cwf_r, **sl)
    P.dma("sp", cbf, dr["ffn_conv_b"][l].rearrange("(m p) -> p m", p=128), [], [cbf_r], cbf_r, **sl)
    tiles = [t for t in seg_tiles() if not (last and t[0] < CTX)]
    for s in range(NSEQ):
        mark = A.top
        norm_mod_T(k, l, s, hxT, hxT_r, 3, xs_src(0), "F")
        wu2 = [A.alloc("wu%d" % i, [2, 8, 128], BF16) for i in range(2)]
        UBW = 258 + 66 * 66
        ub = [[A.alloc("ub%d_%d" % (vg, t), [UBW], BF16) for t in range(2)] for vg in range(2)]
        for vg in range(2):
            for t in range(2):
                P.op("pool", lambda e, b=ub[vg][t][0]: e.memset(b, 0.0), [], [ub[vg][t][1]])
        dg2 = [A.alloc("fdg%d" % i, [18, 128], BF16) for i in range(2)]
        gg2 = [A.alloc("fgg%d" % i, [512], F32) for i in range(2)]
        ac2 = [A.alloc("fac%d" % i, [U], BF16) for i in range(2)]
        pcount = [0]

        def ps_next():
            pcount[0] += 1
            return k.ps[pcount[0] % 6]
        def f_up(j):
            wu, wu_r = wu2[j % 2]
            for vg in range(2):
                c0 = vg * FFN_H + j * 128
                P.dma("pool", wu[:, vg], dr["ffn_w_up"][l, :, c0:c0 + 128].rearrange("(c p) n -> p c n", p=128), [], [wu_r], wu_r)
            dg, dg_r = dg2[j % 2]
            for vg in range(2):
                mm = vg * 22 + j
                for t9 in range(9):
                    P.op("dve", lambda e, vg=vg, t9=t9, dg=dg, mm=mm: e.tensor_scalar(
                        out=dg[:, vg * 9 + t9, :], in0=k.identf, scalar1=cwf[:, t9, mm:mm + 1], scalar2=None, op0=ALU.mult),
                        [k.identf_r, cwf_r], [dg_r])
            for vg in range(2):
                uc, uc_r = ub[vg][j % 2]
                for (u0, n) in tiles:
                    pst, _, ps_r = ps_next()
                    for c in range(8):
                        P.op("pe", lambda e, c=c, vg=vg, wu=wu, pst=pst, u0=u0, n=n: e.matmul(
                            pst[:, 0:n], lhsT=wu[:, vg, c, :], rhs=hxT[:, c, u0:u0 + n], start=(c == 0), stop=(c == 7)),
                            [wu_r, hxT_r], [ps_r], inc=(c == 7))
                    if u0 < CTX:
                        P.op("act", lambda e, pst=pst, u0=u0, n=n, uc=uc: e.activation(out=uc[:, 1 + u0:1 + u0 + n], in_=pst[:, 0:n],
                                                                                  func=AF.Identity), [ps_r], [uc_r])
                    else:
                        r0 = (u0 - CTX) // 64
                        o0 = 258 + (r0 + 1) * 66 + 1
                        P.op("act", lambda e, pst=pst, o0=o0, uc=uc: e.activation(
                            out=bc(uc[:, o0:o0 + 1], [[66, 8], [1, 64]]), in_=pst.rearrange("p (r c) -> p r c", c=64),
                            func=AF.Identity), [ps_r], [uc_r])

        def f_conv(j):
            dg, dg_r = dg2[j % 2]
            ac, ac_r = ac2[j % 2]
            for (u0, n) in tiles:
                pvg = []
                for vg in range(2):
                    uc, uc_r = ub[vg][j % 2]
                    pst, _, ps_r = ps_next()
                    if u0 < CTX:
                        taps = [(3 + kc, uc[:, 1 + u0 + kc - 1:1 + u0 + kc - 1 + n]) for kc in range(3)]
                    else:
                        r0 = (u0 - CTX) // 64
                        taps = []
                        for kr in range(3):
                            for kc in range(3):
                                o0 = 258 + (r0 + kr) * 66 + kc
                                taps.append((kr * 3 + kc, bc(uc[:, o0:o0 + 1], [[66, 8], [1, 64]])))
                    for ti, (t9, rhs_ap) in enumerate(taps):
                        P.op("pe", lambda e, vg=vg, t9=t9, dg=dg, rhs_ap=rhs_ap, pst=pst, n=n, ti=ti, nt=len(taps): e.matmul(
                            pst[:, 0:n], lhsT=dg[:, vg * 9 + t9, :], rhs=rhs_ap,
                            start=(ti == 0), stop=(ti == nt - 1)), [dg_r, uc_r], [ps_r], inc=(ti == len(taps) - 1))
                    pvg.append((pst, ps_r))
                gg, gg_r = gg2[(u0 // 512) % 2]
                P.op("act", lambda e, gg=gg, p=pvg[1][0], n=n, mm=22 + j: e.activation(out=gg[:, 0:n], in_=p[:, 0:n], func=AF.Gelu,
                                                                                   bias=cbf[:, mm:mm + 1]), [pvg[1][1], cbf_r], [gg_r])
                P.op("dve", lambda e, gg=gg, p=pvg[0][0], n=n, u0=u0, ac=ac, mm=j: e.scalar_tensor_tensor(
                    out=ac[:, u0:u0 + n], in0=p[:, 0:n], scalar=cbf[:, mm:mm + 1], in1=gg[:, 0:n], op0=ALU.add, op1=ALU.mult),
                    [pvg[0][1], cbf_r, gg_r], [ac_r])
            P.dma("sp", dr["ACTS"][s, j * 128:(j + 1) * 128, :], ac, [ac_r], [k.dres["ACTS"]], ac_r)

        NJF = FFN_H // 128
        f_up(0)
        for j in range(NJF):
            if j + 1 < NJF:
                f_up(j + 1)
            f_conv(j)
        A.top = mark
    P.barrier()
    A.top = base
    wd, wd_r = A.alloc("wd", [22, D], BF16)
    for q in range(2):
        P.dma("pool", wd[:, 11 * q:11 * q + 11, :], dr["ffn_w_down"][l, 1408 * q:1408 * (q + 1), :].rearrange("(c p) n -> p c n", p=128),
              [], [wd_r], wd_r)
    grows = {v: load_modrow(k, l, v, 5, "G2row%d" % v) for v in range(3)}
    at2 = [A.alloc("fat%d" % i, [22, 512], BF16) for i in range(2)]
    bufs = nr_bufs(k)
    it = 0
    sub = 0
    for s in range(NSEQ):
        for (u0, n) in seg_tiles():
            if last and u0 < CTX:
                continue
            at, at_r = at2[it % 2]
            it += 1
            for q in range(2):
                P.dma("sp", at[:, 11 * q:11 * q + 11, 0:n], dr["ACTS"][s, 1408 * q:1408 * (q + 1), u0:u0 + n].rearrange("(c p) u -> p c u", p=128),
                      [k.dres["ACTS"]], [at_r], at_r)
            for i in range(n // 128):
                pss = [k.ps[2 * (sub % 2)], k.ps[2 * (sub % 2) + 1]]
                for h in range(2):
                    for c in range(22):
                        P.op("pe", lambda e, c=c, h=h, i=i, at=at, pst=pss[h][0]: e.matmul(
                            pst, lhsT=at[:, c, i * 128:(i + 1) * 128], rhs=wd[:, c, h * 512:(h + 1) * 512], start=(c == 0),
                            stop=(c == 21)), [at_r, wd_r], [pss[h][2]], inc=(c == 21))
                uu = u0 + i * 128
                v = 2 if uu < CTX else s
                if last:
                    dsts = [(dr["out"][s, uu - CTX:uu - CTX + 128, :], k.dres["out"])]
                else:
                    dsts = [(dr["XS"][1, s, uu:uu + 128, :], k.dres["XS"])]
                norm_residual(k, pss, dr["XS"][0, s, uu:uu + 128, :], k.dres["XS"], grows[v][0], grows[v][1], dsts, bufs, sub)
                sub += 1
    A.top = base


NP_ = 8192 + 512
HY_TILES_R = [(i * 512, 512, 0) for i in range(8)] + [(4096 + i * 512, 512, 1) for i in range(8)] + [(8192, 256, 0), (8448, 256, 1)]
HY_TILES_N = [(i * 512, 512, 1) for i in range(8)] + [(4096 + i * 512, 512, 0) for i in range(8)] + [(8192, 256, 1), (8448, 256, 0)]


def hy_feats(rev):
    bands = np.linspace(1e-4, 15.0, 16).astype(np.float32)
    out = np.zeros((33, NP_), np.float32)
    for (j0, L, n) in ((0, 4096, 8192), (8192, 256, 512)):
        lag = ((n // 2 - 1) - np.arange(n)) if rev else (np.arange(n) - n // 2)
        pos = np.abs(lag).astype(np.float32)
        t01 = pos / np.float32(L - 1)
        ang = np.float32(2.0 * math.pi / L) * pos[None, :] * bands[:, None]
        out[0, j0:j0 + n] = t01
        out[1:17, j0:j0 + n] = np.cos(ang)
        out[17:33, j0:j0 + n] = np.sin(ang)
    return out


def hy_delta():
    dl = np.abs(np.linspace(math.log(1e-2) / 1.5, math.log(1e-2) / 0.3, 512)).astype(np.float32)
    return np.ascontiguousarray(-dl.reshape(4, 128).T)


def stage_HF(k, l):
    P, A, dr = k.P, k.A, k.dr
    base = A.top
    sl = dict(allow_slow_non_contiguous=True)
    fe, fe_r = A.alloc("fe", [NP_], F32)
    t01b, t01b_r = A.alloc("t01b", [NP_], F32)
    h2, h2_r = A.alloc("h2", [NP_], F32)
    ndl, ndl_r = A.alloc("ndl", [4], F32)
    P.dma("sp", ndl, dr["hy_ndl"], [], [ndl_r], ndl_r)
    w1, w1_r = A.alloc("hw1", [64], F32)
    w2, w2_r = A.alloc("hw2", [64], F32)
    w3, w3_r = A.alloc("hw3", [2048], F32)
    cl, cl_r = A.alloc("hcl", [8], F32)
    hb, hb_r = A.alloc("hbias", [2, 4], F32)
    P.dma("sp", w1[0:33, :], dr["hy_w1"][l], [], [w1_r], w1_r)
    P.dma("sp", w2[0:64, :], dr["hy_w2"][l], [], [w2_r], w2_r)
    P.dma("sp", w3[0:64, :], dr["hy_w3"][l], [], [w3_r], w3_r)
    P.dma("sp", cl[0:64, 0:1], dr["hy_b1"][l].rearrange("(p o) -> p o", o=1), [], [cl_r], cl_r, **sl)
    P.dma("sp", cl[0:64, 1:2], dr["hy_b2"][l].rearrange("(p o) -> p o", o=1), [], [cl_r], cl_r, **sl)
    for q in range(2):
        P.dma("sp", cl[0:64, 2 + q:3 + q], dr["hy_freq"][l, q].rearrange("(p o) -> p o", o=1), [], [cl_r], cl_r, **sl)
        P.dma("sp", hb[:, q, :], dr["hy_bias"][l, q].rearrange("(m p) -> p m", p=128), [], [hb_r], hb_r, **sl)
    tm = [A.alloc("hft%d" % i, [512], F32) for i in range(6)]
    h1, h1_r = tm[5]

    def sinf(psrc, ps_r, bcol, fcol, out_ap, out_r, n):
        (xs, xs_r), (s8, s8_r), (s4, s4_r), (t, t_r), (c, c_r) = tm[0:5]
        P.op("dve", lambda e: e.tensor_scalar(out=xs[0:64, 0:n], in0=psrc[0:64, 0:n], scalar1=cl[0:64, bcol:bcol + 1],
                                              scalar2=cl[0:64, fcol:fcol + 1], op0=ALU.add, op1=ALU.mult), [ps_r, cl_r], [xs_r])
        P.op("act", lambda e: e.activation(out=s8[0:64, 0:n], in_=xs[0:64, 0:n], func=AF.Sin, scale=0.125), [xs_r], [s8_r])
        P.op("act", lambda e: e.activation(out=s4[0:64, 0:n], in_=xs[0:64, 0:n], func=AF.Sin, scale=0.25), [xs_r], [s4_r])
        P.op("dve", lambda e: e.tensor_tensor(out=t[0:64, 0:n], in0=s8[0:64, 0:n], in1=s8[0:64, 0:n], op=ALU.mult), [s8_r], [t_r])
        P.op("dve", lambda e: e.tensor_scalar(out=c[0:64, 0:n], in0=t[0:64, 0:n], scalar1=-2.0, scalar2=1.0, op0=ALU.mult,
                                              op1=ALU.add), [t_r], [c_r])
        P.op("dve", lambda e: e.scalar_tensor_tensor(out=s8[0:64, 0:n], in0=s4[0:64, 0:n], scalar=2.0, in1=c[0:64, 0:n],
                                                      op0=ALU.mult, op1=ALU.mult), [s4_r, c_r], [s8_r])
        P.op("dve", lambda e: e.tensor_tensor(out=t[0:64, 0:n], in0=s4[0:64, 0:n], in1=s4[0:64, 0:n], op=ALU.mult), [s4_r], [t_r])
        P.op("dve", lambda e: e.tensor_scalar(out=c[0:64, 0:n], in0=t[0:64, 0:n], scalar1=-2.0, scalar2=1.0, op0=ALU.mult,
                                              op1=ALU.add), [t_r], [c_r])
        P.op("dve", lambda e: e.scalar_tensor_tensor(out=out_ap, in0=s8[0:64, 0:n], scalar=2.0, in1=c[0:64, 0:n],
                                                      op0=ALU.mult, op1=ALU.mult), [s8_r, c_r], [out_r])

    tp2 = [A.alloc("htp%d" % i, [NP_], BF16) for i in range(2)]
    wt2 = [A.alloc("hwt%d" % i, [512], F32) for i in range(2)]
    it = 0
    for o in range(2):
        tiles_o = HY_TILES_R if o == 0 else HY_TILES_N
        P.dma("sp", fe[0:33, :], dr["hy_fe%d" % o], [], [fe_r], fe_r)
        P.dma("sp", t01b, dr["hy_fe%d" % o][0:1, :].partition_broadcast(128), [], [t01b_r], t01b_r)
        for (j0, n, dd) in tiles_o:
            pst, _, ps_r = k.ps[0]
            P.op("pe", lambda e, j0=j0, n=n, pst=pst: e.matmul(pst[0:64, 0:n], lhsT=w1[0:33, :], rhs=fe[0:33, j0:j0 + n], start=True, stop=True),
                 [w1_r, fe_r], [ps_r])
            sinf(pst, ps_r, 0, 2, h1[0:64, 0:n], h1_r, n)
            pst2, _, ps2_r = k.ps[1]
            P.op("pe", lambda e, n=n, pst2=pst2: e.matmul(pst2[0:64, 0:n], lhsT=w2[0:64, :], rhs=h1[0:64, 0:n], start=True, stop=True),
                 [w2_r, h1_r], [ps2_r])
            sinf(pst2, ps2_r, 1, 3, h2[0:64, j0:j0 + n], h2_r, n)
        for m in range(4):
            tp, tp_r = tp2[(o * 4 + m) % 2]
            for (j0, n, dd) in tiles_o:
                pst, _, ps_r = k.ps[2 + it % 4]
                wt, wt_r = wt2[it % 2]
                it += 1
                col0 = o * 1024 + dd * 512 + m * 128
                P.op("pe", lambda e, j0=j0, n=n, pst=pst, col0=col0: e.matmul(pst[:, 0:n], lhsT=w3[0:64, col0:col0 + 128],
                                                                         rhs=h2[0:64, j0:j0 + n], start=True, stop=True),
                     [w3_r, h2_r], [ps_r])
                P.op("act", lambda e, wt=wt, j0=j0, n=n, m=m: e.activation(out=wt[:, 0:n], in_=t01b[:, j0:j0 + n], func=AF.Exp,
                                                                       scale=ndl[:, m:m + 1]), [t01b_r, ndl_r], [wt_r])
                P.op("dve", lambda e, wt=wt, pst=pst, tp=tp, j0=j0, n=n: e.tensor_tensor(out=tp[:, j0:j0 + n], in0=pst[:, 0:n], in1=wt[:, 0:n],
                                                                                      op=ALU.mult), [ps_r, wt_r], [tp_r])
            for jj in ((4095, 8192 + 255) if o == 0 else (4096, 8192 + 256)):
                P.op("dve", lambda e, tp=tp, jj=jj, o=o, m=m: e.tensor_tensor(out=tp[:, jj:jj + 1], in0=tp[:, jj:jj + 1], in1=hb[:, o, m:m + 1],
                                                                          op=ALU.add), [tp_r, hb_r], [tp_r])
            P.dma("sp", dr["TAPS"][o, m * 128:(m + 1) * 128, :], tp, [tp_r], [k.dres["TAPS"]], tp_r)
    A.top = base


def stage_HC(k, l, shared=False):
    P, A, dr = k.P, k.A, k.dr
    base = A.top
    NJ = U // 128
    NPASS = 2 if shared else 1
    CW = BW // NPASS
    cb0, cb1 = (5, 6) if shared else (0, 1)
    ab0, ab1 = (7, 7) if shared else (2, 3)
    VZ, VZ_r = A.alloc("VZ", [2, NJ, CW], BF16)
    X, X_r = A.alloc("HX", [2, NJ, CW], BF16)
    HW = NP_ - 127
    hc2 = [A.alloc("hc%d" % i, [HW], BF16) for i in range(2)]
    jm, jm_r = A.alloc("jmat", [128], BF16)
    P.dma("sp", jm, dr["jmat"], [], [jm_r], jm_r)
    yt2 = [A.alloc("hyt%d" % i, [CW // 128, 128], BF16) for i in range(2)]
    taps_t = dr["TAPS"].tensor
    for hp in range(NPASS):
        ch0 = hp * CW
        VZg_r = [Res("VZg%d" % g) for g in range(CW // 4)]
        for b in range(NSEQ):
            P.dma("sp", VZ[:, b], dr["X12"][b, :, ch0:ch0 + CW].rearrange("(j p) c -> p j c", p=128), [k.dres["X12"]],
                  [VZ_r] + VZg_r, VZ_r)
        yield
        for o in range(2):
            for b in range(NSEQ):
                P.dma("sp", X[:, b], dr["X12"][b, :, 512 * (o + 1) + ch0:512 * (o + 1) + ch0 + CW].rearrange("(j p) c -> p j c", p=128),
                      [k.dres["X12"]], [X_r], X_r)
            yield
            if o == 0:
                for b in range(NSEQ):
                    for J in range(NJ):
                        for h0 in range(0, CW, 512):
                            pr, _, pr_r = k.ps[ab0 if J % 2 == 0 else ab1]
                            w_ = min(512, CW - h0)
                            P.op("pe", lambda e, b=b, J=J, pr=pr, h0=h0, w_=w_: e.matmul(pr[:, 0:w_], lhsT=jm, rhs=X[:, b, J, h0:h0 + w_],
                                                                                      start=True, stop=True), [jm_r, X_r], [pr_r])
                            P.op("act", lambda e, b=b, J=J, pr=pr, h0=h0, w_=w_: e.activation(out=X[:, b, J, h0:h0 + w_], in_=pr[:, 0:w_],
                                                                                           func=AF.Identity), [pr_r], [X_r])
                        yield
            for cl in range(CW):
                c = ch0 + cl
                hc, hc_r = hc2[cl % 2]
                HWl = HW if l < DEPTH - 1 else 8065
                src = bass.AP(taps_t, (o * BW + c) * NP_, [[1, 128], [1, HWl]])
                P.dma("sp", hc[:, 0:HWl], src, [k.dres["TAPS"]], [hc_r], hc_r)
                pst, _, ps_r = k.ps[cb0 if (cl // 4) % 2 == 0 else cb1]
                cb = (cl % 4) * 68
                mms = []
                for d in [0] + [x for x in range(-31, 32) if x != 0]:
                    J0 = max(2, 2 - d)
                    J1 = min(NJ, NJ - d)
                    mms.append(((3968 - 128 * d) if o == 0 else (3969 + 128 * d), J0, J1 - J0, J0 + d))
                for d in ((0, -1, 1) if l < DEPTH - 1 else ()):
                    J0 = max(0, -d)
                    J1 = min(2, 2 - d)
                    mms.append(((8192 + 128 - 128 * d) if o == 0 else (8192 + 129 + 128 * d), J0, J1 - J0, J0 + d))
                for mi, (w0, J0, nJ, I0) in enumerate(mms):
                    P.op("pe", lambda e, hc=hc, w0=w0, J0=J0, nJ=nJ, I0=I0, pst=pst, cb=cb, cl=cl, mi=mi, nm=len(mms): e.matmul(
                        bc(pst[:, cb + I0:cb + I0 + 1], [[NJ, 2], [1, nJ]]), lhsT=hc[:, w0:w0 + 128],
                        rhs=bc(VZ[:, 0, J0, cl:cl + 1], [[NJ * CW, 2], [CW, nJ]]), start=(mi == 0), stop=(mi == nm - 1)),
                        [hc_r, VZg_r[cl // 4]], [ps_r], inc=(mi == len(mms) - 1))
                    if mi % 8 == 7:
                        yield
                if cl % 4 == 3:
                    c0 = cl - 3
                    for b in range(NSEQ):
                        P.op("dve", lambda e, b=b, c0=c0, pst=pst: e.tensor_tensor(
                            out=VZ[:, b, :, c0:c0 + 4], in0=bc(pst[:, b * NJ:b * NJ + 1], [[1, NJ], [68, 4]]),
                            in1=X[:, b, :, c0:c0 + 4], op=ALU.mult), [ps_r, X_r], [VZg_r[c0 // 4]])
                yield
        it = 0
        NM = CW // 128
        for b in range(NSEQ):
            for J in range(NJ):
                _, psb, ps_r = k.ps[ab0 if it % 2 == 0 else ab1]
                yt, yt_r = yt2[it % 2]
                it += 1
                for m in range(NM):
                    P.op("pe", lambda e, b=b, J=J, m=m, psb=psb: e.transpose(out=psb[:, m * 128:(m + 1) * 128], in_=VZ[:, b, J, m * 128:(m + 1) * 128],
                                                                         identity=k.identb), [VZ_r, k.identb_r] + VZg_r[32 * m:32 * m + 32], [ps_r],
                         inc=(m == NM - 1))
                P.op("act", lambda e, yt=yt, psb=psb, NM=NM: e.activation(out=yt, in_=psb[:, 0:NM * 128].rearrange("p (c q) -> p c q", c=NM),
                                                                       func=AF.Identity), [ps_r], [yt_r])
                P.dma("sp", dr["YB"][b, ch0:ch0 + CW, J * 128:(J + 1) * 128].rearrange("(c p) u -> p c u", p=128), yt, [yt_r], [k.dres["YB"]], yt_r)
                yield
    A.top = base


_CACHE = {}


def make_in_maps(inputs):
    consts = host_consts()
    maps = []
    x = np.asarray(inputs["x"], np.float32)
    ctx = np.asarray(inputs["ctx"], np.float32)
    c = np.asarray(inputs["c"], np.float32)
    c_ctx = np.asarray(inputs["c_ctx"], np.float32)
    for core in range(8):
        m = {}
        m["x"] = np.ascontiguousarray(x[2 * core:2 * core + 2])
        m["ctx"] = np.ascontiguousarray(ctx[2 * core:2 * core + 2])
        cv = np.stack([c[2 * core], c[2 * core + 1], c_ctx], axis=0)
        m["cT"] = np.ascontiguousarray(cv.T.reshape(8, 128, 3).transpose(1, 0, 2))
        for n in W_NAMES:
            m[n] = np.ascontiguousarray(np.asarray(inputs[n], np.float32))
        m.update(consts)
        maps.append(m)
    return maps


def kernel(**inputs):
    if "nc" not in _CACHE:
        _CACHE["nc"] = build()
    nc = _CACHE["nc"]
    maps = make_in_maps(inputs)
    res = run_bass_kernel_spmd(nc, maps, core_ids=list(range(8)))
    out = np.concatenate([np.asarray(r["out"], np.float32) for r in res.results], axis=0)
    return out
```

```python
import math
from contextlib import ExitStack
import numpy as np
import ml_dtypes
import concourse.bass as bass
import concourse.mybir as mybir
from concourse.bass_utils import run_bass_kernel_spmd

F32 = mybir.dt.float32
BF16 = mybir.dt.bfloat16
AF = mybir.ActivationFunctionType
ALU = mybir.AluOpType
AX = mybir.AxisListType

D = 1024
SEQ = 4096
CTX = 256
U = CTX + SEQ
NSEQ = 2
DEPTH = 2
BW = 512
IN_COLS = 7184
FFN_H = 2816
EPS = 1e-6

COMPUTE = ("pe", "act", "dve", "pool")
ENGINES = ("pe", "act", "dve", "pool", "sp")


class Res:
    __slots__ = ("name", "w", "r", "dkey")

    def __init__(self, name):
        self.name = name
        self.w = None
        self.r = {}
        self.dkey = None


class Prog:
    NDK = 96

    def __init__(self):
        self.items = {e: [] for e in ENGINES}
        self.cnt = {e: 0 for e in COMPUTE}
        self.known = {e: {} for e in ENGINES}
        self.dtot = {}
        self.ndk = 0

    def cur(self, key):
        return self.cnt[key] if key in self.cnt else self.dtot.get(key, 0)

    def _need(self, eng, key, val):
        if key not in self.cnt:
            val = self.dtot[key]
        if self.known[eng].get(key, 0) >= val:
            return
        self.known[eng][key] = val
        self.items[eng].append(("wait", key, val))

    def _deps(self, eng, reads, writes):
        for r in reads:
            if r.w is not None and not (eng == "pe" and r.w[0] == "pe"):
                self._need(eng, *r.w)
        for w in writes:
            if w.w is not None and not (eng == "pe" and w.w[0] == "pe"):
                self._need(eng, *w.w)
            for k, v in w.r.items():
                if not (eng == "pe" and k == "pe"):
                    self._need(eng, k, v)

    def op(self, eng, fn, reads=(), writes=(), inc=True):
        self._deps(eng, reads, writes)
        val = self.cnt[eng] + 1
        if inc:
            self.cnt[eng] = val
        self.items[eng].append(("op", fn, inc))
        for r in reads:
            r.r[eng] = max(r.r.get(eng, 0), val)
        for w in writes:
            w.w = (eng, val)
            w.r = {}

    def dkey(self, res):
        if res.dkey is None:
            res.dkey = "d%d" % (self.ndk % self.NDK)
            self.ndk += 1
            self.dtot.setdefault(res.dkey, 0)
        return res.dkey

    def dma(self, eng, out_ap, in_ap, reads, writes, semres, **kw):
        self._deps(eng, reads, writes)
        key = self.dkey(semres)
        self.dtot[key] += 16
        val = self.dtot[key]
        self.items[eng].append(("dma", out_ap, in_ap, key, kw))
        for r in reads:
            r.r[key] = val
        for w in writes:
            w.w = (key, val)
            w.r = {}

    def barrier(self):
        keys = list(self.cnt.keys()) + list(self.dtot.keys())
        for e in ENGINES:
            for k in keys:
                if self.cur(k) > 0:
                    self._need(e, k, self.cur(k))


class Arena:
    def __init__(self, ap_f32, ap_bf16, nbytes):
        self.f = ap_f32
        self.b = ap_bf16
        self.top = 0
        self.nbytes = nbytes

    def alloc(self, name, shape, dt):
        esz = 4 if dt == F32 else 2
        n = int(np.prod(shape))
        nb = (n * esz + 31) // 32 * 32
        off = self.top
        assert off + nb <= self.nbytes, ("SBUF arena overflow", name, off + nb)
        self.top += nb
        base = self.f if dt == F32 else self.b
        v = base[:, off // esz: off // esz + n]
        if len(shape) == 2:
            v = v.rearrange("p (a b) -> p a b", a=shape[0])
        elif len(shape) == 3:
            v = v.rearrange("p (a b c) -> p a b c", a=shape[0], b=shape[1])
        return v, Res(name)


def _bf(a):
    return np.asarray(a, np.float32).astype(ml_dtypes.bfloat16)


W_NAMES = ["mod_w", "mod_b", "norms", "w_in", "lru_conv_w", "lru_conv_b", "lru_w_a", "lru_b_a", "lru_w_i", "lru_b_i",
           "lru_lambda", "hy_conv_w", "hy_conv_b", "hy_w1", "hy_b1", "hy_w2", "hy_b2", "hy_w3", "hy_freq", "hy_bias",
           "ssm_conv_w", "ssm_conv_b", "ssm_dt_bias", "ssm_a_log", "ssm_d", "ssm_norm", "w_branch", "w_out",
           "ffn_w_up", "ffn_conv_w", "ffn_conv_b", "ffn_w_down"]

W_SHAPES = {"mod_w": [2, 1024, 6144], "mod_b": [2, 6144], "norms": [2, 4, 1024], "w_in": [2, 1024, 7184],
            "lru_conv_w": [2, 4, 512], "lru_conv_b": [2, 512], "lru_w_a": [2, 2, 8, 64, 64], "lru_b_a": [2, 2, 512],
            "lru_w_i": [2, 2, 8, 64, 64], "lru_b_i": [2, 2, 512], "lru_lambda": [2, 2, 512],
            "hy_conv_w": [2, 3, 1536], "hy_conv_b": [2, 1536], "hy_w1": [2, 33, 64], "hy_b1": [2, 64],
            "hy_w2": [2, 64, 64], "hy_b2": [2, 64], "hy_w3": [2, 64, 2048], "hy_freq": [2, 2, 64],
            "hy_bias": [2, 2, 512], "ssm_conv_w": [2, 4, 1024], "ssm_conv_b": [2, 1024], "ssm_dt_bias": [2, 2, 8],
            "ssm_a_log": [2, 2, 8], "ssm_d": [2, 8], "ssm_norm": [2, 512], "w_branch": [2, 3, 512, 1024],
            "w_out": [2, 1024, 1024], "ffn_w_up": [2, 1024, 5632], "ffn_conv_w": [2, 3, 3, 5632],
            "ffn_conv_b": [2, 5632], "ffn_w_down": [2, 2816, 1024]}


def host_consts():
    c = {}
    c["ident_bf"] = _bf(np.eye(128))
    c["ident_f"] = np.eye(128, dtype=np.float32)
    c["tri_f"] = np.triu(np.ones((128, 128), np.float32))
    c["tri_b"] = np.tril(np.ones((128, 128), np.float32))
    c["mneg_f"] = ((1.0 - c["tri_f"]) * -1e30).astype(np.float32)
    c["mneg_b"] = ((1.0 - c["tri_b"]) * -1e30).astype(np.float32)
    c["hy_fe0"] = hy_feats(True)
    c["hy_fe1"] = hy_feats(False)
    c["jmat"] = _bf(np.eye(128)[::-1])
    c["hy_ndl"] = hy_delta()
    return c


CONST_SHAPES = {"ident_bf": ([128, 128], BF16), "ident_f": ([128, 128], F32), "tri_f": ([128, 128], F32),
                "tri_b": ([128, 128], F32), "mneg_f": ([128, 128], F32), "mneg_b": ([128, 128], F32), "hy_fe0": ([33, 8704], F32), "hy_fe1": ([33, 8704], F32), "jmat": ([128, 128], BF16), "hy_ndl": ([128, 4], F32)}


class K:
    pass


def build(stop_after=None, dumps=()):
    nc = bass.Bass("TRN2", target_bir_lowering=False)
    k = K()
    k.nc = nc
    k.P = P = Prog()
    k.dumps = dumps
    k.s_per_h = 1
    import os
    k.l2_first = bool(os.environ.get("L2FIRST"))
    dr = {}
    dr["x"] = nc.dram_tensor("x", [NSEQ, SEQ, D], F32, kind="ExternalInput").ap()
    dr["ctx"] = nc.dram_tensor("ctx", [NSEQ, CTX, D], F32, kind="ExternalInput").ap()
    dr["cT"] = nc.dram_tensor("cT", [128, 8, 3], F32, kind="ExternalInput").ap()
    for n in W_NAMES:
        dr[n] = nc.dram_tensor(n, W_SHAPES[n], F32, kind="ExternalInput").ap()
    for n, (shp, dt) in CONST_SHAPES.items():
        dr[n] = nc.dram_tensor(n, shp, dt, kind="ExternalInput").ap()
    dr["out"] = nc.dram_tensor("out", [NSEQ, SEQ, D], F32, kind="ExternalOutput").ap()
    k.dr = dr
    k.dres = {}

    def scratch(name, shape, dt):
        kind = "ExternalOutput" if name in dumps else "Internal"
        dr[name] = nc.dram_tensor(name, shape, dt, kind=kind).ap()
        k.dres[name] = Res(name)
        return dr[name]

    k.scratch = scratch
    scratch("MODR", [DEPTH, 3, 6, D], F32)
    scratch("XS", [2, NSEQ, U, D], F32)
    scratch("XA", [NSEQ, BW, U], F32)
    scratch("GA", [NSEQ, BW, U], BF16)
    scratch("XBC", [NSEQ, 1024, U], BF16)
    scratch("DT", [NSEQ, U, 16], F32)
    scratch("ZS", [NSEQ, U, BW], BF16)
    scratch("HYV", [NSEQ, BW, U], BF16)
    scratch("X12", [NSEQ, U, 1536], BF16)
    scratch("GATES", [NSEQ, 3 * D, U], BF16)
    scratch("YA", [NSEQ, BW, U], BF16)
    scratch("HFWD", [NSEQ, BW, U], F32)
    scratch("YS", [NSEQ, U, BW], F32)
    scratch("YB", [NSEQ, BW, U], BF16)
    scratch("TAPS", [2, BW, NP_], BF16)
    scratch("ACTS", [NSEQ, FFN_H, U], BF16)
    k.dres["out"] = Res("out")
    scratch("YC", [NSEQ, BW, U], BF16)
    for dn in ("DBG1", "DBG2", "DBG3", "DBG4"):
        if dn in dumps:
            scratch(dn, [NSEQ, BW, U], F32)
    if "HXT" in dumps:
        scratch("HXT", [NSEQ, D, U], BF16)

    with ExitStack() as es:
        ARENA_BYTES = 211968
        arena_t = es.enter_context(nc.sbuf_tensor("arena", [128, ARENA_BYTES // 4], F32))
        k.A = Arena(arena_t[:], arena_t[:].bitcast(BF16), ARENA_BYTES)
        k.ps = []
        for i in range(8):
            t = es.enter_context(nc.psum_tensor("ps%d" % i, [128, 512], F32))
            k.ps.append((t[:], t[:].bitcast(BF16), Res("ps%d" % i)))
        emit_all(k, stop_after)
        P.barrier()
        sems = {}
        for key in list(P.cnt.keys()) + list(P.dtot.keys()):
            sems[key] = es.enter_context(nc.semaphore("s_" + key))
        block = es.enter_context(nc.Block())

        def replay(eng_handle, items):
            for it in items:
                if it[0] == "wait":
                    eng_handle.wait_ge(sems[it[1]], it[2])
                elif it[0] == "op":
                    ins = it[1](eng_handle)
                    if it[2]:
                        ins.then_inc(sems[_ek[0]], 1)
                else:
                    eng_handle.dma_start(out=it[1], in_=it[2], **it[4]).then_inc(sems[it[3]], 16)

        _ek = [None]

        @block.tensor
        def _(e):
            _ek[0] = "pe"
            replay(e, P.items["pe"])

        @block.scalar
        def _(e):
            _ek[0] = "act"
            replay(e, P.items["act"])

        @block.vector
        def _(e):
            _ek[0] = "dve"
            replay(e, P.items["dve"])

        @block.gpsimd
        def _(e):
            _ek[0] = "pool"
            replay(e, P.items["pool"])

        @block.sync
        def _(e):
            _ek[0] = "sp"
            replay(e, P.items["sp"])
    return nc


def emit_all(k, stop_after):
    P = k.P
    stage_consts(k)
    stage_mod(k)
    P.barrier()
    if stop_after == "mod":
        return
    for l in range(DEPTH):
        stage_A(k, l)
        P.barrier()
        if stop_after == "A%d" % l:
            return
        stage_L(k, l)
        P.barrier()
        if stop_after == "L%d" % l:
            return
        stage_HF(k, l)
        P.barrier()
        base_ = k.A.top
        gh = stage_HC(k, l, shared=True)
        next(gh)
        gs = stage_S(k, l, banks=(0, 1, 2, 3, 4))
        next(gs)
        others = [gs]
        done_h = False
        while others or not done_h:
            if not done_h:
                try:
                    next(gh)
                except StopIteration:
                    done_h = True
            for g_ in list(others):
                for _ in range(1 if not done_h else 32):
                    try:
                        next(g_)
                    except StopIteration:
                        others.remove(g_)
                        break
        k.A.top = base_
        P.barrier()
        if stop_after == "H%d" % l or stop_after == "S%d" % l:
            return
        stage_M(k, l)
        P.barrier()
        if stop_after == "M%d" % l:
            return
        stage_F(k, l)
        P.barrier()
        if stop_after == "F%d" % l:
            return


def stage_consts(k):
    P, A, dr = k.P, k.A, k.dr
    k.identb, k.identb_r = A.alloc("identb", [128], BF16)
    k.identf, k.identf_r = A.alloc("identf", [128], F32)
    P.dma("sp", k.identb, dr["ident_bf"], [], [k.identb_r], k.identb_r)
    P.dma("sp", k.identf, dr["ident_f"], [], [k.identf_r], k.identf_r)
    k.const_top = A.top


def stage_mod(k):
    P, A, dr = k.P, k.A, k.dr
    mark = A.top
    ct, ct_r = A.alloc("ct", [8, 3], F32)
    sg, sg_r = A.alloc("sg", [8, 3], F32)
    P.dma("sp", ct, dr["cT"], [], [ct_r], ct_r)
    P.op("act", lambda e: e.activation(out=sg, in_=ct, func=AF.Silu), [ct_r], [sg_r])
    wt = [A.alloc("modw%d" % i, [8, 512], F32) for i in range(2)]
    modrow, modrow_r = A.alloc("modrow", [6144], F32)
    modb, modb_r = A.alloc("modb", [6144], F32)
    nrm, nrm_r = A.alloc("nrm", [4, D], F32)
    der, der_r = A.alloc("der", [6, D], F32)
    for l in range(DEPTH):
        P.dma("sp", modb[0:3, :], dr["mod_b"][l:l + 1, :].partition_broadcast(3), [], [modb_r], modb_r)
        P.dma("sp", nrm[0:3], dr["norms"][l:l + 1].partition_broadcast(3), [], [nrm_r], nrm_r)
        for j in range(12):
            w, w_r = wt[j % 2]
            P.dma("sp", w, dr["mod_w"][l, :, j * 512:(j + 1) * 512].rearrange("(c p) n -> p c n", p=128),
                  [], [w_r], w_r)
            pst, _, ps_r = k.ps[j % 2]
            for c in range(8):
                P.op("pe", lambda e, c=c, w=w, pst=pst: e.matmul(pst[0:3, :], lhsT=sg[:, c, :], rhs=w[:, c, :],
                                                                   start=(c == 0), stop=(c == 7)),
                     [sg_r, w_r], [ps_r], inc=(c == 7))
            P.op("dve", lambda e, j=j, pst=pst: e.tensor_tensor(out=modrow[0:3, j * 512:(j + 1) * 512], in0=pst[0:3, :],
                                                                  in1=modb[0:3, j * 512:(j + 1) * 512], op=ALU.add),
                 [ps_r, modb_r], [modrow_r])
        sl = lambda i: modrow[0:3, i * D:(i + 1) * D]
        P.op("dve", lambda e: e.scalar_tensor_tensor(out=der[0:3, 0, :], in0=sl(1), scalar=1.0, in1=nrm[0:3, 0, :],
                                                      op0=ALU.add, op1=ALU.mult), [modrow_r, nrm_r], [der_r])
        P.op("dve", lambda e: e.tensor_copy(out=der[0:3, 1, :], in_=sl(0)), [modrow_r], [der_r])
        P.op("dve", lambda e: e.tensor_tensor(out=der[0:3, 2, :], in0=sl(2), in1=nrm[0:3, 1, :], op=ALU.mult),
             [modrow_r, nrm_r], [der_r])
        P.op("dve", lambda e: e.scalar_tensor_tensor(out=der[0:3, 3, :], in0=sl(4), scalar=1.0, in1=nrm[0:3, 2, :],
                                                      op0=ALU.add, op1=ALU.mult), [modrow_r, nrm_r], [der_r])
        P.op("dve", lambda e: e.tensor_copy(out=der[0:3, 4, :], in_=sl(3)), [modrow_r], [der_r])
        P.op("dve", lambda e: e.tensor_tensor(out=der[0:3, 5, :], in0=sl(5), in1=nrm[0:3, 3, :], op=ALU.mult),
             [modrow_r, nrm_r], [der_r])
        P.dma("sp", dr["MODR"][l], der[0:3], [der_r], [k.dres["MODR"]], der_r)
    A.top = mark


def load_modrow(k, l, v, j, name):
    t, r = k.A.alloc(name, [D], F32)
    k.P.dma("sp", t, k.dr["MODR"][l, v, j:j + 1, :].partition_broadcast(128), [k.dres["MODR"]], [r], r)
    return t, r


def seg_tiles():
    return [(0, CTX)] + [(CTX + i * 512, 512) for i in range(SEQ // 512)]


def stream_src(k, l, s, u0, n):
    if l == 0:
        if u0 < CTX:
            return k.dr["ctx"][s, u0:u0 + n, :], None
        return k.dr["x"][s, u0 - CTX:u0 - CTX + n, :], None
    return k.dr["XS"][1, s, u0:u0 + n, :], k.dres["XS"]


def norm_mod_T(k, l, s, hxT, hxT_r, j0, src_fn, name):
    P, A = k.P, k.A
    P.barrier()
    mark = A.top
    rows = {}
    for v in (s, 2):
        rows[v] = (load_modrow(k, l, v, j0, "Arow%d" % v), load_modrow(k, l, v, j0 + 1, "Brow%d" % v))
    xt = [A.alloc("xt%d" % i, [D], F32) for i in range(3)]
    sq, sq_r = A.alloc("sq", [D], F32)
    t1 = [A.alloc("t1_%d" % i, [D], F32) for i in range(2)]
    hb = [A.alloc("hb%d" % i, [D], BF16) for i in range(2)]
    st = [A.alloc("st%d" % i, [4], F32) for i in range(2)]
    nsub = U // 128

    def part_a1(i):
        u0 = i * 128
        x, x_r = xt[i % 3]
        src, sres = src_fn(k, l, s, u0, 128)
        P.dma("sp", x, src, [sres] if sres else [], [x_r], x_r)
        ss, ss_r = st[i % 2]
        P.op("act", lambda e, x=x, ss=ss: e.activation(out=sq, in_=x, func=AF.Square, accum_out=ss[:, 0:1]),
             [x_r], [sq_r, ss_r])

    def part_a2(i):
        ss, ss_r = st[i % 2]
        P.op("dve", lambda e, ss=ss: e.tensor_scalar(out=ss[:, 1:2], in0=ss[:, 0:1], scalar1=1.0 / D, scalar2=EPS,
                                                     op0=ALU.mult, op1=ALU.add), [ss_r], [ss_r])
        P.op("act", lambda e, ss=ss: e.activation(out=ss[:, 2:3], in_=ss[:, 1:2], func=AF.Sqrt), [ss_r], [ss_r])
        P.op("dve", lambda e, ss=ss: e.reciprocal(out=ss[:, 3:4], in_=ss[:, 2:3]), [ss_r], [ss_r])

    def part_b(i):
        u0 = i * 128
        v = 2 if u0 < CTX else s
        (Ar, Ar_r), (Br, Br_r) = rows[v]
        x, x_r = xt[i % 3]
        ss, ss_r = st[i % 2]
        tt, tt_r = t1[i % 2]
        P.op("dve", lambda e, x=x, ss=ss, tt=tt, Ar=Ar: e.scalar_tensor_tensor(
            out=tt, in0=x, scalar=ss[:, 3:4], in1=Ar, op0=ALU.mult, op1=ALU.mult), [x_r, ss_r, Ar_r], [tt_r])
        h, h_r = hb[i % 2]
        P.op("dve", lambda e, tt=tt, h=h, Br=Br: e.tensor_tensor(out=h, in0=tt, in1=Br, op=ALU.add),
             [tt_r, Br_r], [h_r])
        _, psb, ps_r = k.ps[i % 2]
        for c in range(8):
            P.op("pe", lambda e, c=c, h=h, psb=psb: e.transpose(out=psb[:, c * 128:(c + 1) * 128],
                                                                 in_=h[:, c * 128:(c + 1) * 128], identity=k.identb),
                 [h_r, k.identb_r], [ps_r], inc=(c == 7))
        P.op("act", lambda e, psb=psb, u0=u0: e.activation(
            out=hxT[:, :, u0:u0 + 128], in_=psb.rearrange("p (c t) -> p c t", c=8), func=AF.Identity),
            [ps_r], [hxT_r])

    part_a1(0)
    part_a2(0)
    for i in range(nsub):
        if i + 1 < nsub:
            part_a1(i + 1)
        part_b(i)
        if i + 1 < nsub:
            part_a2(i + 1)
    P.barrier()
    A.top = mark


PB_CTX = 2
PB_LAT = 264
PB_W = 4368


def pb_pos(u):
    return PB_CTX + u if u < CTX else PB_LAT + (u - CTX)


def stage_A(k, l):
    P, A, dr = k.P, k.A, k.dr
    base = A.top
    hxT, hxT_r = A.alloc("hxT", [8, U], BF16)
    cw_l, cw_l_r = A.alloc("cw_l", [4, 4], F32)
    cw_s, cw_s_r = A.alloc("cw_s", [8, 4], F32)
    cw_h, cw_h_r = A.alloc("cw_h", [12, 3], F32)
    cb_l, cb_l_r = A.alloc("cb_l", [4], F32)
    cb_s, cb_s_r = A.alloc("cb_s", [8], F32)
    cb_h, cb_h_r = A.alloc("cb_h", [12], F32)
    dtb, dtb_r = A.alloc("dtb", [16], F32)
    sl = dict(allow_slow_non_contiguous=True)
    for (cw_, cw_r_, nm_, nt_) in ((cw_l, cw_l_r, "lru_conv_w", 4), (cw_s, cw_s_r, "ssm_conv_w", 4), (cw_h, cw_h_r, "hy_conv_w", 3)):
        for kk in range(nt_):
            P.dma("sp", cw_[:, :, kk], dr[nm_][l, kk].rearrange("(m p) -> p m", p=128), [], [cw_r_], cw_r_, **sl)
    P.dma("sp", cb_l, dr["lru_conv_b"][l].rearrange("(m p) -> p m", p=128), [], [cb_l_r], cb_l_r, **sl)
    P.dma("sp", cb_s, dr["ssm_conv_b"][l].rearrange("(m p) -> p m", p=128), [], [cb_s_r], cb_s_r, **sl)
    P.dma("sp", cb_h, dr["hy_conv_b"][l].rearrange("(m p) -> p m", p=128), [], [cb_h_r], cb_h_r, **sl)
    P.dma("sp", dtb, dr["ssm_dt_bias"][l:l + 1].rearrange("o a b -> o (a b)").partition_broadcast(128), [], [dtb_r], dtb_r)
    wbuf = [A.alloc("wbuf%d" % i, [8, 512], BF16) for i in range(2)]
    pb = [A.alloc("pb%d" % i, [PB_W], BF16) for i in range(2)]
    for t, r in pb:
        P.op("pool", lambda e, t=t: e.memset(t, 0.0), [], [r])
    stg = [A.alloc("stg%d" % i, [U], F32) for i in range(2)]
    diag = [A.alloc("diag%d" % i, [4, 128], BF16) for i in range(2)]
    tmst = [A.alloc("tmst%d" % i, [U // 128, 128], BF16) for i in range(2)]
    zst = [A.alloc("zst%d" % i, [512], BF16) for i in range(2)]
    dtst, dtst_r = A.alloc("dtst", [U // 128, 16], F32)
    dtt, dtt_r = A.alloc("dtt", [16], F32)
    tiles = seg_tiles()
    cnt = {"w": 0, "c": 0, "ps": 0}

    def ps_next():
        cnt["ps"] += 1
        return k.ps[2 + cnt["ps"] % 4]

    groups = [(0, "conv", ("XA", 0, cw_l, cw_l_r, cb_l, cb_l_r, 0, 4, 2, AF.Identity, F32))]
    groups += [(512 + 512 * g, "conv", ("XBC", 512 * g, cw_s, cw_s_r, cb_s, cb_s_r, 4 * g, 4, 2, AF.Silu, BF16)) for g in range(2)]
    groups += [(1552, "plain", ("GA", 0, AF.Gelu))]
    groups += [(2064 + 512 * g, "x12", (512 * g, cw_h, cw_h_r, cb_h, cb_h_r, 4 * g, 3, 1)) for g in (0, 1, 2)]
    groups += [(3600, "z", None)]
    groups += [(4112 + 512 * g, "plain", ("GATES", 512 * g, AF.Sigmoid)) for g in range(6)]
    groups += [(1536, "dt", None)]

    for s in range(NSEQ):
        norm_mod_T(k, l, s, hxT, hxT_r, 0, stream_src, "A")
        if "HXT" in k.dumps:
            P.dma("sp", dr["HXT"][s].rearrange("(c p) u -> p c u", p=128), hxT, [hxT_r], [k.dres["HXT"]], hxT_r)
        for (c0, kind, prm) in groups:
            w, w_r = wbuf[cnt["w"] % 2]
            cnt["w"] += 1
            ncol = 16 if kind == "dt" else 512
            P.dma("pool", w[:, :, 0:ncol], dr["w_in"][l, :, c0:c0 + ncol].rearrange("(c p) n -> p c n", p=128),
                  [], [w_r], w_r)
            if kind in ("conv", "plain", "x12"):
                for m in range(4):
                    ci = cnt["c"]
                    cnt["c"] += 1
                    sg32, sg_r = stg[ci % 2]
                    if kind == "plain":
                        name, r0, func = prm
                        sgb = sg32.bitcast(BF16)[:, 0:U]
                        for (u0, n) in tiles:
                            pst, _, ps_r = ps_next()
                            for c in range(8):
                                P.op("pe", lambda e, c=c, w=w, pst=pst, u0=u0, n=n, m=m: e.matmul(
                                    pst[:, 0:n], lhsT=w[:, c, m * 128:(m + 1) * 128], rhs=hxT[:, c, u0:u0 + n],
                                    start=(c == 0), stop=(c == 7)), [w_r, hxT_r], [ps_r], inc=(c == 7))
                            P.op("act", lambda e, pst=pst, u0=u0, n=n, sgb=sgb, func=func: e.activation(
                                out=sgb[:, u0:u0 + n], in_=pst[:, 0:n], func=func), [ps_r], [sg_r])
                        P.dma("sp", dr[name][s, r0 + m * 128:r0 + (m + 1) * 128, :], sgb, [sg_r], [k.dres[name]], sg_r)
                        continue
                    if kind == "conv":
                        name, r0, cw, cw_r, cb, cb_r, mb, ntap, padl, func, odt = prm
                    else:
                        r0, cw, cw_r, cb, cb_r, mb, ntap, padl = prm
                        func, odt = AF.Identity, BF16
                    pbt, pb_r = pb[ci % 2]
                    dg, dg_r = diag[ci % 2]
                    for kk in range(ntap):
                        P.op("dve", lambda e, kk=kk, dg=dg, cw=cw, mm=mb + m: e.tensor_scalar(
                            out=dg[:, kk, :], in0=k.identf, scalar1=cw[:, mm, kk:kk + 1], scalar2=None, op0=ALU.mult),
                            [k.identf_r, cw_r], [dg_r])
                    for (u0, n) in tiles:
                        pst, _, ps_r = ps_next()
                        for c in range(8):
                            P.op("pe", lambda e, c=c, w=w, pst=pst, u0=u0, n=n, m=m: e.matmul(
                                pst[:, 0:n], lhsT=w[:, c, m * 128:(m + 1) * 128], rhs=hxT[:, c, u0:u0 + n],
                                start=(c == 0), stop=(c == 7)), [w_r, hxT_r], [ps_r], inc=(c == 7))
                        P.op("act", lambda e, pst=pst, u0=u0, n=n, pbt=pbt: e.activation(
                            out=pbt[:, pb_pos(u0):pb_pos(u0) + n], in_=pst[:, 0:n], func=AF.Identity), [ps_r], [pb_r])
                    sgo = sg32 if odt == F32 else sg32.bitcast(BF16)[:, 0:U]
                    for (u0, n) in tiles:
                        pst, _, ps_r = ps_next()
                        for kk in range(ntap):
                            P.op("pe", lambda e, kk=kk, dg=dg, pbt=pbt, pst=pst, u0=u0, n=n, padl=padl: e.matmul(
                                pst[:, 0:n], lhsT=dg[:, kk, :], rhs=pbt[:, pb_pos(u0) + kk - padl:pb_pos(u0) + kk - padl + n],
                                start=(kk == 0), stop=(kk == ntap - 1)), [dg_r, pb_r], [ps_r], inc=(kk == ntap - 1))
                        P.op("act", lambda e, pst=pst, u0=u0, n=n, sgo=sgo, func=func, cb=cb, mm=mb + m: e.activation(
                            out=sgo[:, u0:u0 + n], in_=pst[:, 0:n], func=func, bias=cb[:, mm:mm + 1]),
                            [ps_r, cb_r], [sg_r])
                    if kind == "conv":
                        P.dma("sp", dr[name][s, r0 + m * 128:r0 + (m + 1) * 128, :], sgo, [sg_r], [k.dres[name]], sg_r)
                    else:
                        tm, tm_r = tmst[ci % 2]
                        for i0 in range(0, U // 128, 4):
                            nn = min(4, U // 128 - i0)
                            _, psb, ps_r = ps_next()
                            for j in range(nn):
                                P.op("pe", lambda e, j=j, i0=i0, psb=psb, sgo=sgo: e.transpose(
                                    out=psb[:, j * 128:(j + 1) * 128], in_=sgo[:, (i0 + j) * 128:(i0 + j + 1) * 128],
                                    identity=k.identb), [sg_r, k.identb_r], [ps_r], inc=(j == nn - 1))
                            P.op("dve", lambda e, i0=i0, nn=nn, psb=psb, tm=tm: e.tensor_copy(
                                out=tm[:, i0:i0 + nn, :], in_=psb[:, 0:nn * 128].rearrange("p (a b) -> p a b", a=nn)),
                                [ps_r], [tm_r])
                        P.dma("sp", dr["X12"][s, :, r0 + m * 128:r0 + (m + 1) * 128].rearrange("(i p) c -> p i c", p=128),
                              tm, [tm_r], [k.dres["X12"]], tm_r)
            elif kind == "z":
                for i in range(U // 128):
                    pst, _, ps_r = ps_next()
                    for c in range(8):
                        P.op("pe", lambda e, c=c, w=w, pst=pst, i=i: e.matmul(
                            pst, lhsT=hxT[:, c, i * 128:(i + 1) * 128], rhs=w[:, c, :], start=(c == 0), stop=(c == 7)),
                            [w_r, hxT_r], [ps_r], inc=(c == 7))
                    zt, zt_r = zst[i % 2]
                    P.op("act", lambda e, pst=pst, zt=zt: e.activation(out=zt, in_=pst, func=AF.Silu), [ps_r], [zt_r])
                    P.dma("sp", dr["ZS"][s, i * 128:(i + 1) * 128, :], zt, [zt_r], [k.dres["ZS"]], zt_r)
            elif kind == "dt":
                for i in range(U // 128):
                    pst, _, ps_r = ps_next()
                    for c in range(8):
                        P.op("pe", lambda e, c=c, w=w, pst=pst, i=i: e.matmul(
                            pst[:, 0:16], lhsT=hxT[:, c, i * 128:(i + 1) * 128], rhs=w[:, c, 0:16], start=(c == 0),
                            stop=(c == 7)), [w_r, hxT_r], [ps_r], inc=(c == 7))
                    P.op("dve", lambda e, pst=pst: e.tensor_tensor(out=dtt, in0=pst[:, 0:16], in1=dtb, op=ALU.add),
                         [ps_r, dtb_r], [dtt_r])
                    P.op("act", lambda e: e.activation(out=dtt, in_=dtt, func=AF.Exp), [dtt_r], [dtt_r])
                    P.op("act", lambda e, i=i: e.activation(out=dtst[:, i, :], in_=dtt, func=AF.Ln, bias=1.0),
                         [dtt_r], [dtst_r])
                P.dma("sp", dr["DT"][s].rearrange("(i p) h -> p i h", p=128), dtst, [dtst_r], [k.dres["DT"]], dtst_r)
    A.top = base


def stage_L(k, l):
    P, A, dr = k.P, k.A, k.dr
    base = A.top
    sl = dict(allow_slow_non_contiguous=True)
    tiles = seg_tiles()
    T = [A.alloc("LT%d" % i, [U], F32) for i in range(6)]
    xab, xab_r = A.alloc("xab", [U], BF16)
    gab, gab_r = A.alloc("gab", [U], BF16)
    yab, yab_r = A.alloc("yab", [U], BF16)
    bd = [A.alloc("bd%d" % i, [128], BF16) for i in range(2)]
    cols, cols_r = A.alloc("lcols", [2, 4, 4], F32)
    for d in range(2):
        for j, nm in enumerate(("lru_b_a", "lru_b_i", "lru_lambda")):
            P.dma("sp", cols[:, d, :, j], dr[nm][l, d].rearrange("(m p) -> p m", p=128), [], [cols_r], cols_r, **sl)
    for d in range(2):
        P.op("act", lambda e, d=d: e.activation(out=cols[:, d, :, 3], in_=cols[:, d, :, 2], func=AF.Exp, scale=-1.0),
             [cols_r], [cols_r])
        P.op("act", lambda e, d=d: e.activation(out=cols[:, d, :, 3], in_=cols[:, d, :, 3], func=AF.Ln, bias=1.0),
             [cols_r], [cols_r])
        P.op("dve", lambda e, d=d: e.tensor_scalar(out=cols[:, d, :, 3], in0=cols[:, d, :, 3], scalar1=-8.0, scalar2=None,
                                                   op0=ALU.mult), [cols_r], [cols_r])
    cnt = {"ps": 0}

    def ps_next():
        cnt["ps"] += 1
        return k.ps[cnt["ps"] % 4]

    for s in range(NSEQ):
        for m in range(4):
            xa, xa_r = T[0]
            hs, hs_r = T[5]
            P.dma("sp", xa, dr["XA"][s, m * 128:(m + 1) * 128, :], [k.dres["XA"]], [xa_r], xa_r)
            P.dma("sp", gab, dr["GA"][s, m * 128:(m + 1) * 128, :], [k.dres["GA"]], [gab_r], gab_r)
            P.op("pool", lambda e, xa=xa: e.tensor_copy(out=xab, in_=xa), [xa_r], [xab_r])
            for d in range(2):
                gates = []
                for gi, (wn, bj) in enumerate((("lru_w_a", 0), ("lru_w_i", 1))):
                    b_, b_r = bd[gi]
                    P.op("pool", lambda e, b_=b_: e.memset(b_, 0.0), [], [b_r])
                    for h in range(2):
                        P.dma("pool", b_[h * 64:(h + 1) * 64, h * 64:(h + 1) * 64], dr[wn][l, d, 2 * m + h], [], [b_r], b_r)
                    g_, g_r = T[1 + gi]
                    for (u0, n) in tiles:
                        pst, _, ps_r = ps_next()
                        P.op("pe", lambda e, b_=b_, pst=pst, u0=u0, n=n: e.matmul(pst[:, 0:n], lhsT=b_, rhs=xab[:, u0:u0 + n],
                                                                                    start=True, stop=True), [b_r, xab_r], [ps_r])
                        P.op("act", lambda e, pst=pst, u0=u0, n=n, g_=g_, bj=bj, d=d, m=m: e.activation(
                            out=g_[:, u0:u0 + n], in_=pst[:, 0:n], func=AF.Sigmoid, bias=cols[:, d, m, bj:bj + 1]),
                            [ps_r, cols_r], [g_r])
                    gates.append((g_, g_r))
                (r_, r_r), (i_, i_r) = gates
                a_, a_r = T[3]
                q_, q_r = T[4]
                P.op("act", lambda e, d=d, m=m: e.activation(out=a_, in_=r_, func=AF.Exp, scale=cols[:, d, m, 3:4]),
                     [r_r, cols_r], [a_r])
                P.op("dve", lambda e: e.tensor_tensor(out=q_, in0=a_, in1=a_, op=ALU.mult), [a_r], [q_r])
                P.op("act", lambda e: e.activation(out=q_, in_=q_, func=AF.Sqrt, scale=-1.0, bias=1.0), [q_r], [q_r])
                P.op("pool", lambda e, xa=xa: e.tensor_tensor(out=i_, in0=i_, in1=xa, op=ALU.mult), [i_r, xa_r], [i_r])
                P.op("dve", lambda e: e.tensor_tensor(out=q_, in0=q_, in1=i_, op=ALU.mult), [q_r, i_r], [q_r])
                h_, h_r = r_, r_r
                if d == 0:
                    P.op("dve", lambda e: e.tensor_tensor_scan(out=hs, data0=a_, data1=q_, initial=0.0, op0=ALU.mult,
                                                               op1=ALU.add), [a_r, q_r], [hs_r])
                    if "DBG1" in k.dumps:
                        P.dma("sp", dr["DBG1"][s, m * 128:(m + 1) * 128, :], hs, [hs_r], [k.dres["DBG1"]], hs_r)
                        P.dma("sp", dr["DBG3"][s, m * 128:(m + 1) * 128, :], a_, [a_r], [k.dres["DBG3"]], a_r)
                        P.dma("sp", dr["DBG4"][s, m * 128:(m + 1) * 128, :], q_, [q_r], [k.dres["DBG4"]], q_r)
                else:
                    def rev(ap, lo, n):
                        return bass.AP(ap.tensor, ap.offset + lo + n - 1, [list(ap.ap[0]), [-1, n]])
                    P.op("dve", lambda e: e.tensor_tensor_scan(out=rev(h_, 0, CTX), data0=rev(a_, 0, CTX), data1=rev(q_, 0, CTX),
                                                               initial=0.0, op0=ALU.mult, op1=ALU.add), [a_r, q_r], [h_r])
                    P.op("dve", lambda e: e.tensor_tensor_scan(out=rev(h_, CTX, SEQ), data0=rev(a_, CTX, SEQ),
                                                               data1=rev(q_, CTX, SEQ), initial=h_[:, 0:1], op0=ALU.mult,
                                                               op1=ALU.add), [a_r, q_r, h_r], [h_r])
                    if "DBG2" in k.dumps:
                        P.dma("sp", dr["DBG2"][s, m * 128:(m + 1) * 128, :], h_, [h_r], [k.dres["DBG2"]], h_r)
                    P.op("pool", lambda e: e.tensor_tensor(out=hs, in0=hs, in1=h_, op=ALU.add), [hs_r, h_r], [hs_r])
            P.op("dve", lambda e: e.tensor_tensor(out=yab, in0=hs, in1=gab, op=ALU.mult), [hs_r, gab_r], [yab_r])
            P.dma("sp", dr["YA"][s, m * 128:(m + 1) * 128, :], yab, [yab_r], [k.dres["YA"]], yab_r)
    A.top = base


def stage_L2(k, l, bank):
    P, A, dr = k.P, k.A, k.dr
    sl = dict(allow_slow_non_contiguous=True)
    tiles = seg_tiles()
    W = 512

    def al(name, dt=F32, w=W):
        return A.alloc(name, [w], dt)
    xa, xa_r = al("l2xa")
    xab, xab_r = al("l2xab", BF16)
    r_, r_r = al("l2r")
    i_, i_r = al("l2i")
    a_, a_r = al("l2a")
    q_, q_r = al("l2q")
    h_, h_r = al("l2h")
    hf, hf_r = al("l2hf")
    gab, gab_r = al("l2ga", BF16)
    yab, yab_r = al("l2ya", BF16)
    car, car_r = A.alloc("l2car", [4], F32)
    bd = [A.alloc("l2bd%d" % i, [128], BF16) for i in range(2)]
    cols, cols_r = A.alloc("l2cols", [2, 4, 4], F32)
    for d in range(2):
        for j, nm in enumerate(("lru_b_a", "lru_b_i", "lru_lambda")):
            P.dma("sp", cols[:, d, :, j], dr[nm][l, d].rearrange("(m p) -> p m", p=128), [], [cols_r], cols_r, **sl)
    for d in range(2):
        P.op("act", lambda e, d=d: e.activation(out=cols[:, d, :, 3], in_=cols[:, d, :, 2], func=AF.Exp, scale=-1.0), [cols_r], [cols_r])
        P.op("act", lambda e, d=d: e.activation(out=cols[:, d, :, 3], in_=cols[:, d, :, 3], func=AF.Ln, bias=1.0), [cols_r], [cols_r])
        P.op("dve", lambda e, d=d: e.tensor_scalar(out=cols[:, d, :, 3], in0=cols[:, d, :, 3], scalar1=-8.0, scalar2=None, op0=ALU.mult),
             [cols_r], [cols_r])
    yield
    pst, _, ps_r = k.ps[bank]

    def rev(ap, n):
        return bass.AP(ap.tensor, ap.offset + n - 1, [list(ap.ap[0]), [-1, n]])

    for s in range(NSEQ):
        for m in range(4):
            for d in range(2):
                for gi, wn in enumerate(("lru_w_a", "lru_w_i")):
                    b_, b_r = bd[gi]
                    P.op("pool", lambda e, b_=b_: e.memset(b_, 0.0), [], [b_r])
                    for hh in range(2):
                        P.dma("pool", b_[hh * 64:(hh + 1) * 64, hh * 64:(hh + 1) * 64], dr[wn][l, d, 2 * m + hh], [], [b_r], b_r)
                yield
                order = tiles if d == 0 else [tiles[0]] + tiles[:0:-1]
                for ti, (u0, n) in enumerate(order):
                    P.dma("sp", xa[:, 0:n], dr["XA"][s, m * 128:(m + 1) * 128, u0:u0 + n], [k.dres["XA"]], [xa_r], xa_r)
                    if d == 1:
                        P.dma("sp", hf[:, 0:n], dr["HFWD"][s, m * 128:(m + 1) * 128, u0:u0 + n], [k.dres["HFWD"]], [hf_r], hf_r)
                        P.dma("sp", gab[:, 0:n], dr["GA"][s, m * 128:(m + 1) * 128, u0:u0 + n], [k.dres["GA"]], [gab_r], gab_r)
                    yield
                    P.op("act", lambda e, n=n: e.activation(out=xab[:, 0:n], in_=xa[:, 0:n], func=AF.Identity), [xa_r], [xab_r])
                    yield
                    for gi, (g_, g_r, bj) in enumerate(((r_, r_r, 0), (i_, i_r, 1))):
                        b_, b_r = bd[gi]
                        P.op("pe", lambda e, b_=b_, n=n: e.matmul(pst[:, 0:n], lhsT=b_, rhs=xab[:, 0:n], start=True, stop=True),
                             [b_r, xab_r], [ps_r])
                        P.op("act", lambda e, n=n, g_=g_, bj=bj, d=d, m=m: e.activation(
                            out=g_[:, 0:n], in_=pst[:, 0:n], func=AF.Sigmoid, bias=cols[:, d, m, bj:bj + 1]), [ps_r, cols_r], [g_r])
                        yield
                    P.op("act", lambda e, n=n, d=d, m=m: e.activation(out=a_[:, 0:n], in_=r_[:, 0:n], func=AF.Exp, scale=cols[:, d, m, 3:4]),
                         [r_r, cols_r], [a_r])
                    yield
                    P.op("dve", lambda e, n=n: e.tensor_tensor(out=q_[:, 0:n], in0=a_[:, 0:n], in1=a_[:, 0:n], op=ALU.mult), [a_r], [q_r])
                    yield
                    P.op("act", lambda e, n=n: e.activation(out=q_[:, 0:n], in_=q_[:, 0:n], func=AF.Sqrt, scale=-1.0, bias=1.0), [q_r], [q_r])
                    yield
                    P.op("dve", lambda e, n=n: e.tensor_tensor(out=i_[:, 0:n], in0=i_[:, 0:n], in1=xa[:, 0:n], op=ALU.mult), [i_r, xa_r], [i_r])
                    yield
                    P.op("dve", lambda e, n=n: e.tensor_tensor(out=q_[:, 0:n], in0=q_[:, 0:n], in1=i_[:, 0:n], op=ALU.mult), [q_r, i_r], [q_r])
                    yield
                    first = (ti == 0)
                    if d == 0:
                        P.op("dve", lambda e, n=n, first=first: e.tensor_tensor_scan(
                            out=h_[:, 0:n], data0=a_[:, 0:n], data1=q_[:, 0:n], initial=(0.0 if first else car[:, 0:1]), op0=ALU.mult,
                            op1=ALU.add), [a_r, q_r, car_r], [h_r])
                        yield
                        P.op("dve", lambda e, n=n: e.tensor_copy(out=car[:, 0:1], in_=h_[:, n - 1:n]), [h_r], [car_r])
                        yield
                        P.dma("sp", dr["HFWD"][s, m * 128:(m + 1) * 128, u0:u0 + n], h_[:, 0:n], [h_r], [k.dres["HFWD"]], h_r)
                        yield
                    else:
                        P.op("dve", lambda e, n=n, first=first: e.tensor_tensor_scan(
                            out=rev(h_, n), data0=rev(a_, n), data1=rev(q_, n), initial=(0.0 if first else car[:, 0:1]), op0=ALU.mult,
                            op1=ALU.add), [a_r, q_r, car_r], [h_r])
                        yield
                        P.op("dve", lambda e: e.tensor_copy(out=car[:, 0:1], in_=h_[:, 0:1]), [h_r], [car_r])
                        yield
                        P.op("dve", lambda e, n=n: e.tensor_tensor(out=h_[:, 0:n], in0=h_[:, 0:n], in1=hf[:, 0:n], op=ALU.add), [h_r, hf_r], [h_r])
                        yield
                        P.op("dve", lambda e, n=n: e.tensor_tensor(out=yab[:, 0:n], in0=h_[:, 0:n], in1=gab[:, 0:n], op=ALU.mult),
                             [h_r, gab_r], [yab_r])
                        yield
                        P.dma("sp", dr["YA"][s, m * 128:(m + 1) * 128, u0:u0 + n], yab[:, 0:n], [yab_r], [k.dres["YA"]], yab_r)
                        yield


def bc(ap, dims):
    return bass.AP(ap.tensor, ap.offset, [list(ap.ap[0])] + [list(d) for d in dims])


def stage_S(k, l, banks=None):
    P, A, dr = k.P, k.A, k.dr
    base = A.top
    tri = []
    for nm in ("tri_f", "tri_b"):
        t, r = A.alloc(nm, [128], F32)
        P.dma("sp", t, dr[nm], [], [r], r)
        tri.append((t, r))
    mneg = []
    for nm in ("mneg_f", "mneg_b"):
        t, r = A.alloc(nm, [128], F32)
        P.dma("sp", t, dr[nm], [], [r], r)
        mneg.append((t, r))
    ones, ones_r = A.alloc("ones", [128], F32)
    P.op("pool", lambda e: e.memset(ones, 1.0), [], [ones_r])
    arow, arow_r = A.alloc("arow", [16], F32)
    drow, drow_r = A.alloc("drow", [8], F32)
    nrow, nrow_r = A.alloc("nrow", [BW], F32)
    P.dma("sp", arow, dr["ssm_a_log"][l:l + 1].rearrange("o a b -> o (a b)").partition_broadcast(128), [], [arow_r], arow_r)
    P.op("act", lambda e: e.activation(out=arow, in_=arow, func=AF.Exp), [arow_r], [arow_r])
    P.op("dve", lambda e: e.tensor_scalar(out=arow, in0=arow, scalar1=-1.0, scalar2=None, op0=ALU.mult), [arow_r], [arow_r])
    P.dma("sp", drow, dr["ssm_d"][l:l + 1].partition_broadcast(128), [], [drow_r], drow_r)
    P.dma("sp", nrow, dr["ssm_norm"][l:l + 1].partition_broadcast(128), [], [nrow_r], nrow_r)
    NCH = 2 if banks is None else 1
    Hs = [(A.alloc("H%d" % c, [8, 64], F32), A.alloc("Hb%d" % c, [8, 64], BF16)) for c in range(NCH)]
    NSL = 4 if banks is None else 2

    def many(name, shape, dt):
        return [A.alloc(name + str(i), shape, dt) for i in range(NSL)]
    T = {}
    for nm, shp, dt in (("xbc", [8, 128], BF16), ("dtc", [16], F32), ("xtok", [8, 64], BF16), ("btok", [2, 128], BF16),
                        ("acol", [8], F32), ("xdt", [8, 64], BF16), ("rhs2", [8, 128], F32), ("csc", [8], F32),
                        ("csr", [8, 128], F32), ("Dm", [8, 128], F32), ("M", [8, 128], BF16), ("Ecs", [8, 128], F32),
                        ("CTs", [8, 128], BF16), ("dec", [8], F32), ("xdd", [8, 64], BF16), ("ysum", [BW], F32),
                        ("zs", [BW], BF16), ("sst", [4], F32), ("yc", [BW], BF16), ("ycT", [4, 128], BF16)):
        T[nm] = many(nm, shp, dt)
    sq, sq_r = A.alloc("ssq", [BW], F32)
    PSC = []
    for ch_ in range(NCH):
        ia, ie, ic = (0 + ch_, 2 + ch_, 4 + ch_) if banks is None else banks[0:3]
        b0f, b0b, rA = k.ps[ia]
        b1f, b1b, rE = k.ps[ie]
        PSC.append(dict(psA=b0b, psA_r=rA, psB=b0f[:, 384:392], psB_r=rA, psE=b1f[:, 0:256], psE_r=rE,
                        psT=b1b[:, 512:1024], psT_r=rE, psC=k.ps[ic][0], psC_r=k.ps[ic][2]))
    iy, isb = (6, 7) if banks is None else banks[3:5]

    def p1(s, d, ci, sl):
        pc = PSC[s if NCH == 2 else 0]
        psA, psA_r, psB, psB_r, psE, psE_r = pc["psA"], pc["psA_r"], pc["psB"], pc["psB_r"], pc["psE"], pc["psE_r"]
        trd, trd_r = tri[d]
        qe = 127 if d == 0 else 0
        u0 = ci * 128
        g = {nm: T[nm][sl] for nm in T}
        xbc, xbc_r = g["xbc"]
        dtc, dtc_r = g["dtc"]
        P.dma("sp", xbc, dr["XBC"][s, :, u0:u0 + 128].rearrange("(c p) u -> p c u", p=128), [k.dres["XBC"]], [xbc_r], xbc_r)
        yield
        P.dma("sp", dtc, dr["DT"][s, u0:u0 + 128, :], [k.dres["DT"]], [dtc_r], dtc_r)
        yield
        for c in range(6):
            P.op("pe", lambda e, c=c: e.transpose(out=psA[:, c * 128:(c + 1) * 128], in_=xbc[:, c, :], identity=k.identb),
                 [xbc_r, k.identb_r], [psA_r], inc=(c == 5))
            yield
        xtok, xtok_r = g["xtok"]
        btok, btok_r = g["btok"]
        P.op("act", lambda e: e.activation(out=xtok, in_=psA[:, 0:512].rearrange("p (h q) -> p h q", h=8), func=AF.Identity),
             [psA_r], [xtok_r])
        yield
        P.op("act", lambda e: e.activation(out=btok, in_=psA[:, 512:768].rearrange("p (h q) -> p h q", h=2), func=AF.Identity),
             [psA_r], [btok_r])
        yield
        acol, acol_r = g["acol"]
        P.op("dve", lambda e: e.tensor_tensor(out=acol, in0=dtc[:, 8 * d:8 * d + 8], in1=arow[:, 8 * d:8 * d + 8], op=ALU.mult),
             [dtc_r, arow_r], [acol_r])
        yield
        xdt, xdt_r = g["xdt"]
        P.op("dve", lambda e: e.tensor_tensor(out=xdt, in0=xtok, in1=bc(dtc[:, 8 * d:8 * d + 8], [[1, 8], [0, 64]]), op=ALU.mult),
             [xtok_r, dtc_r], [xdt_r])
        yield
        rhs2, rhs2_r = g["rhs2"]
        P.op("dve", lambda e: e.tensor_tensor(out=rhs2, in0=bc(acol, [[1, 8], [0, 128]]), in1=bc(trd, [[0, 8], [1, 128]]), op=ALU.mult),
             [acol_r, trd_r], [rhs2_r])
        yield
        P.op("pe", lambda e: e.matmul(psB, lhsT=trd, rhs=acol, start=True, stop=True), [trd_r, acol_r], [psB_r])
        yield
        csc, csc_r = g["csc"]
        P.op("dve", lambda e: e.tensor_copy(out=csc, in_=psB), [psB_r], [csc_r])
        yield
        csr, csr_r = g["csr"]
        for hh in range(2):
            psC, psC_r = pc["psC"], pc["psC_r"]
            P.op("pe", lambda e, psC=psC, hh=hh: e.matmul(psC, lhsT=ones, rhs=rhs2[:, 4 * hh:4 * hh + 4, :], start=True, stop=True),
                 [ones_r, rhs2_r], [psC_r])
            yield
            P.op("act", lambda e, psC=psC, hh=hh: e.activation(out=csr[:, 4 * hh:4 * hh + 4, :], in_=psC.rearrange("p (h q) -> p h q", h=4),
                                                             func=AF.Identity), [psC_r], [csr_r])
            yield
        Dm, Dm_r = g["Dm"]
        P.op("dve", lambda e: e.tensor_tensor(out=Dm, in0=csr, in1=bc(csc, [[1, 8], [0, 128]]), op=ALU.subtract), [csr_r, csc_r], [Dm_r])
        yield
        P.op("dve", lambda e: e.tensor_tensor(out=Dm, in0=Dm, in1=bc(mneg[d][0], [[0, 8], [1, 128]]), op=ALU.add), [Dm_r, mneg[d][1]], [Dm_r])
        yield
        P.op("act", lambda e: e.activation(out=Dm, in_=Dm, func=AF.Exp), [Dm_r], [Dm_r])
        yield
        for gg in range(2):
            P.op("pe", lambda e, gg=gg: e.matmul(psE[:, gg * 128:(gg + 1) * 128], lhsT=xbc[:, 4 + gg, :], rhs=xbc[:, 6 + gg, :],
                                                 start=True, stop=True), [xbc_r], [psE_r], inc=(gg == 1))
            yield
        M, M_r = g["M"]
        for gg in range(2):
            P.op("dve", lambda e, gg=gg: e.tensor_tensor(out=M[:, 4 * gg:4 * gg + 4, :], in0=Dm[:, 4 * gg:4 * gg + 4, :],
                                                         in1=bc(psE[:, gg * 128:(gg + 1) * 128], [[0, 4], [1, 128]]), op=ALU.mult),
                 [Dm_r, psE_r], [M_r])
            yield
        Ecs, Ecs_r = g["Ecs"]
        P.op("act", lambda e: e.activation(out=Ecs, in_=csr, func=AF.Exp), [csr_r], [Ecs_r])
        yield
        CTs, CTs_r = g["CTs"]
        for gg in range(2):
            P.op("dve", lambda e, gg=gg: e.tensor_tensor(out=CTs[:, 4 * gg:4 * gg + 4, :], in0=Ecs[:, 4 * gg:4 * gg + 4, :],
                                                          in1=bc(xbc[:, 6 + gg, :], [[0, 4], [1, 128]]), op=ALU.mult),
                 [Ecs_r, xbc_r], [CTs_r])
            yield
        dec, dec_r = g["dec"]
        P.op("dve", lambda e: e.tensor_tensor(out=dec, in0=csr[:, :, qe], in1=csc, op=ALU.subtract), [csr_r, csc_r], [dec_r])
        yield
        P.op("act", lambda e: e.activation(out=dec, in_=dec, func=AF.Exp), [dec_r], [dec_r])
        yield
        xdd, xdd_r = g["xdd"]
        P.op("dve", lambda e: e.tensor_tensor(out=xdd, in0=xdt, in1=bc(dec, [[1, 8], [0, 64]]), op=ALU.mult), [xdt_r, dec_r], [xdd_r])
        yield
        if d == 1:
            ysum, ysum_r = g["ysum"]
            zs, zs_r = g["zs"]
            P.dma("sp", ysum, dr["YS"][s, u0:u0 + 128, :], [k.dres["YS"]], [ysum_r], ysum_r)
            yield
            P.dma("sp", zs, dr["ZS"][s, u0:u0 + 128, :], [k.dres["ZS"]], [zs_r], zs_r)
            yield

    def p2(s, d, ci, sl, chain):
        pc = PSC[chain]
        psT, psT_r = pc["psT"], pc["psT_r"]
        qe = 127 if d == 0 else 0
        u0 = ci * 128
        g = {nm: T[nm][sl] for nm in T}
        (H, H_r), (Hb, Hb_r) = Hs[chain]
        CTs, CTs_r = g["CTs"]
        Ecs, Ecs_r = g["Ecs"]
        xtok, xtok_r = g["xtok"]
        psY, _, psY_r = k.ps[iy]
        psS, _, psS_r = k.ps[isb]
        M, M_r = g["M"]
        xdt, xdt_r = g["xdt"]
        xdd, xdd_r = g["xdd"]
        btok, btok_r = g["btok"]
        for gg in range(2):
            P.op("pe", lambda e, gg=gg: e.matmul(psS[:, 256 * gg:256 * gg + 256], lhsT=btok[:, gg, :],
                                                 rhs=xdd[:, 4 * gg:4 * gg + 4, :], start=True, stop=True),
                 [btok_r, xdd_r], [psS_r], inc=(gg == 1))
            yield
        for h in range(8):
            P.op("pe", lambda e, h=h: e.matmul(psY[:, 64 * h:64 * h + 64], lhsT=M[:, h, :], rhs=xdt[:, h, :],
                                               start=True, stop=False), [M_r, xdt_r], [psY_r], inc=False)
            yield
            P.op("pe", lambda e, h=h: e.matmul(psY[:, 64 * h:64 * h + 64], lhsT=CTs[:, h, :], rhs=Hb[:, h, :], start=False,
                                               stop=True), [CTs_r, Hb_r], [psY_r], inc=(h == 7))
            yield
        P.op("dve", lambda e: e.tensor_tensor(out=H, in0=H, in1=bc(Ecs[:, :, qe], [[128, 8], [0, 64]]), op=ALU.mult), [H_r, Ecs_r], [H_r])
        yield
        P.op("dve", lambda e: e.tensor_tensor(out=H, in0=H, in1=psS.rearrange("p (h q) -> p h q", h=8), op=ALU.add), [H_r, psS_r], [H_r])
        yield
        P.op("act", lambda e: e.activation(out=Hb, in_=H, func=AF.Identity), [H_r], [Hb_r])
        yield
        ysum, ysum_r = g["ysum"]
        if d == 0:
            P.op("dve", lambda e: e.tensor_tensor(out=ysum.rearrange("p (h q) -> p h q", h=8), in0=xtok, in1=bc(drow, [[1, 8], [0, 64]]),
                                                   op=ALU.mult), [xtok_r, drow_r], [ysum_r])
            yield
            P.op("dve", lambda e: e.tensor_tensor(out=ysum, in0=ysum, in1=psY, op=ALU.add), [ysum_r, psY_r], [ysum_r])
            yield
            P.dma("sp", dr["YS"][s, u0:u0 + 128, :], ysum, [ysum_r], [k.dres["YS"]], ysum_r)
            yield
        else:
            zs, zs_r = g["zs"]
            P.op("dve", lambda e: e.tensor_tensor(out=ysum, in0=ysum, in1=psY, op=ALU.add), [ysum_r, psY_r], [ysum_r])
            yield
            P.op("dve", lambda e: e.tensor_tensor(out=ysum, in0=ysum, in1=zs, op=ALU.mult), [ysum_r, zs_r], [ysum_r])
            yield
            ss, ss_r = g["sst"]
            P.op("act", lambda e: e.activation(out=sq, in_=ysum, func=AF.Square, accum_out=ss[:, 0:1]), [ysum_r], [sq_r, ss_r])
            yield
            P.op("dve", lambda e: e.tensor_scalar(out=ss[:, 1:2], in0=ss[:, 0:1], scalar1=1.0 / BW, scalar2=EPS, op0=ALU.mult,
                                                  op1=ALU.add), [ss_r], [ss_r])
            yield
            P.op("act", lambda e: e.activation(out=ss[:, 2:3], in_=ss[:, 1:2], func=AF.Sqrt), [ss_r], [ss_r])
            yield
            P.op("dve", lambda e: e.reciprocal(out=ss[:, 3:4], in_=ss[:, 2:3]), [ss_r], [ss_r])
            yield
            yc, yc_r = g["yc"]
            P.op("dve", lambda e: e.scalar_tensor_tensor(out=yc, in0=ysum, scalar=ss[:, 3:4], in1=nrow, op0=ALU.mult, op1=ALU.mult),
                 [ysum_r, ss_r, nrow_r], [yc_r])
            yield
            for c in range(4):
                P.op("pe", lambda e, c=c: e.transpose(out=psT[:, c * 128:(c + 1) * 128], in_=yc[:, c * 128:(c + 1) * 128],
                                                      identity=k.identb), [yc_r, k.identb_r], [psT_r], inc=(c == 3))
                yield
            ycT, ycT_r = g["ycT"]
            P.op("act", lambda e: e.activation(out=ycT, in_=psT.rearrange("p (c q) -> p c q", c=4), func=AF.Identity), [psT_r], [ycT_r])
            yield
            P.dma("sp", dr["YC"][s, :, u0:u0 + 128].rearrange("(c p) u -> p c u", p=128), ycT, [ycT_r], [k.dres["YC"]], ycT_r)
            yield

    def lockstep(gens):
        gens = list(gens)
        while gens:
            for g_ in list(gens):
                try:
                    next(g_)
                except StopIteration:
                    gens.remove(g_)

    for d in range(2):
        order = list(range(U // 128)) if d == 0 else [1, 0] + list(range(U // 128 - 1, 1, -1))
        if banks is None:
            for c in range(NCH):
                (H, H_r), (Hb, Hb_r) = Hs[c]
                P.op("pool", lambda e, H=H: e.memset(H, 0.0), [], [H_r])
                P.op("pool", lambda e, Hb=Hb: e.memset(Hb, 0.0), [], [Hb_r])
            prev = None
            for idx, ci in enumerate(order):
                sls = [2 * s + idx % 2 for s in range(NSEQ)]
                lockstep([p1(s, d, ci, sls[s]) for s in range(NSEQ)])
                if prev is not None:
                    for pv in prev:
                        for _ in p2(*pv):
                            pass
                prev = [(s, d, ci, sls[s], s) for s in range(NSEQ)]
                yield
            for pv in prev:
                for _ in p2(*pv):
                    pass
        else:
            for s in range(NSEQ):
                (H, H_r), (Hb, Hb_r) = Hs[0]
                P.op("pool", lambda e, H=H: e.memset(H, 0.0), [], [H_r])
                P.op("pool", lambda e, Hb=Hb: e.memset(Hb, 0.0), [], [Hb_r])
                prev = None
                for idx, ci in enumerate(order):
                    for _ in p1(s, d, ci, idx % 2):
                        yield
                    if prev is not None:
                        for _ in p2(*prev):
                            yield
                    prev = (s, d, ci, idx % 2, 0)
                for _ in p2(*prev):
                    yield
    A.top = base


def xs_src(which):
    def f(k, l, s, u0, n):
        return k.dr["XS"][which, s, u0:u0 + n, :], k.dres["XS"]
    return f


def norm_residual(k, pss, xsrc, xsres, grow, grow_r, dsts, bufs, i):
    P = k.P
    (xt, xt_r), (ss, ss_r), (sq, sq_r), (o, o_r) = bufs[i % 2]
    P.dma("sp", xt, xsrc, [xsres] if xsres else [], [xt_r], xt_r)
    for h in range(2):
        P.op("act", lambda e, h=h, ss=ss, sq=sq: e.activation(out=sq, in_=pss[h][0], func=AF.Square, accum_out=ss[:, h:h + 1]),
             [pss[h][2]], [sq_r, ss_r])
    P.op("dve", lambda e, ss=ss: e.tensor_tensor(out=ss[:, 2:3], in0=ss[:, 0:1], in1=ss[:, 1:2], op=ALU.add), [ss_r], [ss_r])
    P.op("dve", lambda e, ss=ss: e.tensor_scalar(out=ss[:, 3:4], in0=ss[:, 2:3], scalar1=1.0 / D, scalar2=EPS, op0=ALU.mult,
                                                 op1=ALU.add), [ss_r], [ss_r])
    P.op("act", lambda e, ss=ss: e.activation(out=ss[:, 4:5], in_=ss[:, 3:4], func=AF.Sqrt), [ss_r], [ss_r])
    P.op("dve", lambda e, ss=ss: e.reciprocal(out=ss[:, 5:6], in_=ss[:, 4:5]), [ss_r], [ss_r])
    for h in range(2):
        P.op("dve", lambda e, h=h, ss=ss, o=o: e.scalar_tensor_tensor(
            out=o[:, h * 512:(h + 1) * 512], in0=pss[h][0], scalar=ss[:, 5:6], in1=grow[:, h * 512:(h + 1) * 512],
            op0=ALU.mult, op1=ALU.mult), [pss[h][2], ss_r, grow_r], [o_r])
    P.op("pool", lambda e, o=o, xt=xt: e.tensor_tensor(out=o, in0=o, in1=xt, op=ALU.add), [o_r, xt_r], [o_r])
    for (dst, dres) in dsts:
        P.dma("sp", dst, o, [o_r], [dres], o_r)


def nr_bufs(k):
    A = k.A
    return [(A.alloc("nrx%d" % i, [D], F32), A.alloc("nrs%d" % i, [8], F32), A.alloc("nrq%d" % i, [512], F32),
             A.alloc("nro%d" % i, [D], F32)) for i in range(2)]


def stage_M(k, l):
    P, A, dr = k.P, k.A, k.dr
    base = A.top
    wb, wb_r = A.alloc("wb", [12, D], BF16)
    wo, wo_r = A.alloc("wo", [8, D], BF16)
    for kb in range(3):
        P.dma("pool", wb[:, 4 * kb:4 * kb + 4, :], dr["w_branch"][l, kb].rearrange("(c p) n -> p c n", p=128), [], [wb_r], wb_r)
    P.dma("pool", wo, dr["w_out"][l].rearrange("(c p) n -> p c n", p=128), [], [wo_r], wo_r)
    grows = {v: load_modrow(k, l, v, 2, "G1row%d" % v) for v in range(3)}
    yb2 = [[A.alloc("my%d_%d" % (kb, i), [4, 512], BF16) for kb in range(3)] for i in range(2)]
    gt2 = [A.alloc("mg%d" % i, [24, 512], BF16) for i in range(2)]
    mg2 = [A.alloc("mm%d" % i, [8, 512], BF16) for i in range(2)]
    t1, t1_r = A.alloc("mt1", [512], F32)
    t2, t2_r = A.alloc("mt2", [512], F32)
    bufs = nr_bufs(k)
    names = ("YA", "YB", "YC")
    it = 0
    sub = 0
    last = (l == DEPTH - 1)
    subc = [0]

    def p1(s, u0, n, i2):
        ys = yb2[i2]
        for kb in range(3):
            P.dma("sp", ys[kb][0][:, :, 0:n], dr[names[kb]][s, :, u0:u0 + n].rearrange("(c p) u -> p c u", p=128),
                  [k.dres[names[kb]]], [ys[kb][1]], ys[kb][1])
        gt, gt_r = gt2[i2]
        for kb in range(3):
            P.dma("sp", gt[:, 8 * kb:8 * kb + 8, 0:n], dr["GATES"][s, kb * D:(kb + 1) * D, u0:u0 + n].rearrange("(c p) u -> p c u", p=128),
                  [k.dres["GATES"]], [gt_r], gt_r)
        mg, mg_r = mg2[i2]
        for j in range(8):
            pks = [k.ps[(3 * j + kb) % 4] for kb in range(3)]
            for kb in range(3):
                pst, _, ps_r = pks[kb]
                for c in range(4):
                    P.op("pe", lambda e, c=c, kb=kb, j=j, pst=pst, y=ys[kb][0], n=n: e.matmul(
                        pst[:, 0:n], lhsT=wb[:, 4 * kb + c, j * 128:(j + 1) * 128], rhs=y[:, c, 0:n], start=(c == 0),
                        stop=(c == 3)), [wb_r, ys[kb][1]], [ps_r], inc=(c == 3))
            P.op("dve", lambda e, j=j, gt=gt, p0=pks[0][0], n=n: e.tensor_tensor(out=t1[:, 0:n], in0=gt[:, j, 0:n], in1=p0[:, 0:n],
                                                                              op=ALU.mult), [gt_r, pks[0][2]], [t1_r])
            P.op("dve", lambda e, j=j, gt=gt, p1=pks[1][0], n=n: e.tensor_tensor(out=t2[:, 0:n], in0=gt[:, 8 + j, 0:n], in1=p1[:, 0:n],
                                                                              op=ALU.mult), [gt_r, pks[1][2]], [t2_r])
            P.op("dve", lambda e, n=n: e.tensor_tensor(out=t1[:, 0:n], in0=t1[:, 0:n], in1=t2[:, 0:n], op=ALU.add), [t1_r, t2_r], [t1_r])
            P.op("dve", lambda e, j=j, gt=gt, p2=pks[2][0], n=n: e.tensor_tensor(out=t2[:, 0:n], in0=gt[:, 16 + j, 0:n], in1=p2[:, 0:n],
                                                                              op=ALU.mult), [gt_r, pks[2][2]], [t2_r])
            P.op("dve", lambda e, j=j, mg=mg, n=n: e.tensor_tensor(out=mg[:, j, 0:n], in0=t1[:, 0:n], in1=t2[:, 0:n], op=ALU.add),
                 [t1_r, t2_r], [mg_r])

    def p2(s, u0, n, i2):
        mg, mg_r = mg2[i2]
        for i in range(n // 128):
            pss = [k.ps[4 + 2 * (subc[0] % 2)], k.ps[5 + 2 * (subc[0] % 2)]]
            for h in range(2):
                for c in range(8):
                    P.op("pe", lambda e, c=c, h=h, i=i, mg=mg, pst=pss[h][0]: e.matmul(
                        pst, lhsT=mg[:, c, i * 128:(i + 1) * 128], rhs=wo[:, c, h * 512:(h + 1) * 512], start=(c == 0),
                        stop=(c == 7)), [mg_r, wo_r], [pss[h][2]], inc=(c == 7))
            uu = u0 + i * 128
            v = 2 if uu < CTX else s
            xsrc, xsres = stream_src(k, l, s, uu, 128)
            norm_residual(k, pss, xsrc, xsres, grows[v][0], grows[v][1], [(dr["XS"][0, s, uu:uu + 128, :], k.dres["XS"])], bufs, subc[0])
            subc[0] += 1

    tl = [(s, u0, n) for s in range(NSEQ) for (u0, n) in seg_tiles() if not (last and u0 < CTX)]
    prev = None
    for it_, (s, u0, n) in enumerate(tl):
        p1(s, u0, n, it_ % 2)
        if prev is not None:
            p2(*prev)
        prev = (s, u0, n, it_ % 2)
    p2(*prev)
    A.top = base


FC_CTX = 1
FC_LAT = 258 + 65
FC_W = 258 + 65 + SEQ + 65


def fc_pos(u):
    return FC_CTX + u if u < CTX else FC_LAT + (u - CTX)


def stage_F(k, l):
    P, A, dr = k.P, k.A, k.dr
    last = (l == DEPTH - 1)
    base = A.top
    sl = dict(allow_slow_non_contiguous=True)
    hxT, hxT_r = A.alloc("hxT2", [8, U], BF16)
    cwf, cwf_r = A.alloc("cwf", [9, 44], F32)
    cbf, cbf_r = A.alloc("cbf", [44], F32)
    for t9 in range(9):
        P.dma("sp", cwf[:, t9, :], dr["ffn_conv_w"][l, t9 // 3, t9 % 3].rearrange("(m p) -> p m", p=128), [], [cwf_r], cwf_r, **sl)
    P.dma("sp", cbf, dr["ffn_conv_b"][l].rearrange("(m p) -> p m", p=128), [], [cbf_r], cbf_r, **sl)
    tiles = [t for t in seg_tiles() if not (last and t[0] < CTX)]
    for s in range(NSEQ):
        mark = A.top
        norm_mod_T(k, l, s, hxT, hxT_r, 3, xs_src(0), "F")
        wu2 = [A.alloc("wu%d" % i, [2, 8, 128], BF16) for i in range(2)]
        UBW = 258 + 66 * 66
        ub = [[A.alloc("ub%d_%d" % (vg, t), [UBW], BF16) for t in range(2)] for vg in range(2)]
        for vg in range(2):
            for t in range(2):
                P.op("pool", lambda e, b=ub[vg][t][0]: e.memset(b, 0.0), [], [ub[vg][t][1]])
        dg2 = [A.alloc("fdg%d" % i, [18, 128], BF16) for i in range(2)]
        gg2 = [A.alloc("fgg%d" % i, [512], F32) for i in range(2)]
        ac2 = [A.alloc("fac%d" % i, [U], BF16) for i in range(2)]
        pcount = [0]

        def ps_next():
            pcount[0] += 1
            return k.ps[pcount[0] % 6]
        def f_up(j):
            wu, wu_r = wu2[j % 2]
            for vg in range(2):
                c0 = vg * FFN_H + j * 128
                P.dma("pool", wu[:, vg], dr["ffn_w_up"][l, :, c0:c0 + 128].rearrange("(c p) n -> p c n", p=128), [], [wu_r], wu_r)
            dg, dg_r = dg2[j % 2]
            for vg in range(2):
                mm = vg * 22 + j
                for t9 in range(9):
                    P.op("dve", lambda e, vg=vg, t9=t9, dg=dg, mm=mm: e.tensor_scalar(
                        out=dg[:, vg * 9 + t9, :], in0=k.identf, scalar1=cwf[:, t9, mm:mm + 1], scalar2=None, op0=ALU.mult),
                        [k.identf_r, cwf_r], [dg_r])
            for vg in range(2):
                uc, uc_r = ub[vg][j % 2]
                for (u0, n) in tiles:
                    pst, _, ps_r = ps_next()
                    for c in range(8):
                        P.op("pe", lambda e, c=c, vg=vg, wu=wu, pst=pst, u0=u0, n=n: e.matmul(
                            pst[:, 0:n], lhsT=wu[:, vg, c, :], rhs=hxT[:, c, u0:u0 + n], start=(c == 0), stop=(c == 7)),
                            [wu_r, hxT_r], [ps_r], inc=(c == 7))
                    if u0 < CTX:
                        P.op("act", lambda e, pst=pst, u0=u0, n=n, uc=uc: e.activation(out=uc[:, 1 + u0:1 + u0 + n], in_=pst[:, 0:n],
                                                                                  func=AF.Identity), [ps_r], [uc_r])
                    else:
                        r0 = (u0 - CTX) // 64
                        o0 = 258 + (r0 + 1) * 66 + 1
                        P.op("act", lambda e, pst=pst, o0=o0, uc=uc: e.activation(
                            out=bc(uc[:, o0:o0 + 1], [[66, 8], [1, 64]]), in_=pst.rearrange("p (r c) -> p r c", c=64),
                            func=AF.Identity), [ps_r], [uc_r])

        def f_conv(j):
            dg, dg_r = dg2[j % 2]
            ac, ac_r = ac2[j % 2]
            for (u0, n) in tiles:
                pvg = []
                for vg in range(2):
                    uc, uc_r = ub[vg][j % 2]
                    pst, _, ps_r = ps_next()
                    if u0 < CTX:
                        taps = [(3 + kc, uc[:, 1 + u0 + kc - 1:1 + u0 + kc - 1 + n]) for kc in range(3)]
                    else:
                        r0 = (u0 - CTX) // 64
                        taps = []
                        for kr in range(3):
                            for kc in range(3):
                                o0 = 258 + (r0 + kr) * 66 + kc
                                taps.append((kr * 3 + kc, bc(uc[:, o0:o0 + 1], [[66, 8], [1, 64]])))
                    for ti, (t9, rhs_ap) in enumerate(taps):
                        P.op("pe", lambda e, vg=vg, t9=t9, dg=dg, rhs_ap=rhs_ap, pst=pst, n=n, ti=ti, nt=len(taps): e.matmul(
                            pst[:, 0:n], lhsT=dg[:, vg * 9 + t9, :], rhs=rhs_ap,
                            start=(ti == 0), stop=(ti == nt - 1)), [dg_r, uc_r], [ps_r], inc=(ti == len(taps) - 1))
                    pvg.append((pst, ps_r))
                gg, gg_r = gg2[(u0 // 512) % 2]
                P.op("act", lambda e, gg=gg, p=pvg[1][0], n=n, mm=22 + j: e.activation(out=gg[:, 0:n], in_=p[:, 0:n], func=AF.Gelu,
                                                                                   bias=cbf[:, mm:mm + 1]), [pvg[1][1], cbf_r], [gg_r])
                P.op("dve", lambda e, gg=gg, p=pvg[0][0], n=n, u0=u0, ac=ac, mm=j: e.scalar_tensor_tensor(
                    out=ac[:, u0:u0 + n], in0=p[:, 0:n], scalar=cbf[:, mm:mm + 1], in1=gg[:, 0:n], op0=ALU.add, op1=ALU.mult),
                    [pvg[0][1], cbf_r, gg_r], [ac_r])
            P.dma("sp", dr["ACTS"][s, j * 128:(j + 1) * 128, :], ac, [ac_r], [k.dres["ACTS"]], ac_r)

        NJF = FFN_H // 128
        f_up(0)
        for j in range(NJF):
            if j + 1 < NJF:
                f_up(j + 1)
            f_conv(j)
        A.top = mark
    P.barrier()
    A.top = base
    wd, wd_r = A.alloc("wd", [22, D], BF16)
    for q in range(2):
        P.dma("pool", wd[:, 11 * q:11 * q + 11, :], dr["ffn_w_down"][l, 1408 * q:1408 * (q + 1), :].rearrange("(c p) n -> p c n", p=128),
              [], [wd_r], wd_r)
    grows = {v: load_modrow(k, l, v, 5, "G2row%d" % v) for v in range(3)}
    at2 = [A.alloc("fat%d" % i, [22, 512], BF16) for i in range(2)]
    bufs = nr_bufs(k)
    it = 0
    sub = 0
    for s in range(NSEQ):
        for (u0, n) in seg_tiles():
            if last and u0 < CTX:
                continue
            at, at_r = at2[it % 2]
            it += 1
            for q in range(2):
                P.dma("sp", at[:, 11 * q:11 * q + 11, 0:n], dr["ACTS"][s, 1408 * q:1408 * (q + 1), u0:u0 + n].rearrange("(c p) u -> p c u", p=128),
                      [k.dres["ACTS"]], [at_r], at_r)
            for i in range(n // 128):
                pss = [k.ps[2 * (sub % 2)], k.ps[2 * (sub % 2) + 1]]
                for h in range(2):
                    for c in range(22):
                        P.op("pe", lambda e, c=c, h=h, i=i, at=at, pst=pss[h][0]: e.matmul(
                            pst, lhsT=at[:, c, i * 128:(i + 1) * 128], rhs=wd[:, c, h * 512:(h + 1) * 512], start=(c == 0),
                            stop=(c == 21)), [at_r, wd_r], [pss[h][2]], inc=(c == 21))
                uu = u0 + i * 128
                v = 2 if uu < CTX else s
                if last:
                    dsts = [(dr["out"][s, uu - CTX:uu - CTX + 128, :], k.dres["out"])]
                else:
                    dsts = [(dr["XS"][1, s, uu:uu + 128, :], k.dres["XS"])]
                norm_residual(k, pss, dr["XS"][0, s, uu:uu + 128, :], k.dres["XS"], grows[v][0], grows[v][1], dsts, bufs, sub)
                sub += 1
    A.top = base


NP_ = 8192 + 512
HY_TILES_R = [(i * 512, 512, 0) for i in range(8)] + [(4096 + i * 512, 512, 1) for i in range(8)] + [(8192, 256, 0), (8448, 256, 1)]
HY_TILES_N = [(i * 512, 512, 1) for i in range(8)] + [(4096 + i * 512, 512, 0) for i in range(8)] + [(8192, 256, 1), (8448, 256, 0)]


def hy_feats(rev):
    bands = np.linspace(1e-4, 15.0, 16).astype(np.float32)
    out = np.zeros((33, NP_), np.float32)
    for (j0, L, n) in ((0, 4096, 8192), (8192, 256, 512)):
        lag = ((n // 2 - 1) - np.arange(n)) if rev else (np.arange(n) - n // 2)
        pos = np.abs(lag).astype(np.float32)
        t01 = pos / np.float32(L - 1)
        ang = np.float32(2.0 * math.pi / L) * pos[None, :] * bands[:, None]
        out[0, j0:j0 + n] = t01
        out[1:17, j0:j0 + n] = np.cos(ang)
        out[17:33, j0:j0 + n] = np.sin(ang)
    return out


def hy_delta():
    dl = np.abs(np.linspace(math.log(1e-2) / 1.5, math.log(1e-2) / 0.3, 512)).astype(np.float32)
    return np.ascontiguousarray(-dl.reshape(4, 128).T)


def stage_HF(k, l):
    P, A, dr = k.P, k.A, k.dr
    base = A.top
    sl = dict(allow_slow_non_contiguous=True)
    fe, fe_r = A.alloc("fe", [NP_], F32)
    t01b, t01b_r = A.alloc("t01b", [NP_], F32)
    h2, h2_r = A.alloc("h2", [NP_], F32)
    ndl, ndl_r = A.alloc("ndl", [4], F32)
    P.dma("sp", ndl, dr["hy_ndl"], [], [ndl_r], ndl_r)
    w1, w1_r = A.alloc("hw1", [64], F32)
    w2, w2_r = A.alloc("hw2", [64], F32)
    w3, w3_r = A.alloc("hw3", [2048], F32)
    cl, cl_r = A.alloc("hcl", [8], F32)
    hb, hb_r = A.alloc("hbias", [2, 4], F32)
    P.dma("sp", w1[0:33, :], dr["hy_w1"][l], [], [w1_r], w1_r)
    P.dma("sp", w2[0:64, :], dr["hy_w2"][l], [], [w2_r], w2_r)
    P.dma("sp", w3[0:64, :], dr["hy_w3"][l], [], [w3_r], w3_r)
    P.dma("sp", cl[0:64, 0:1], dr["hy_b1"][l].rearrange("(p o) -> p o", o=1), [], [cl_r], cl_r, **sl)
    P.dma("sp", cl[0:64, 1:2], dr["hy_b2"][l].rearrange("(p o) -> p o", o=1), [], [cl_r], cl_r, **sl)
    for q in range(2):
        P.dma("sp", cl[0:64, 2 + q:3 + q], dr["hy_freq"][l, q].rearrange("(p o) -> p o", o=1), [], [cl_r], cl_r, **sl)
        P.dma("sp", hb[:, q, :], dr["hy_bias"][l, q].rearrange("(m p) -> p m", p=128), [], [hb_r], hb_r, **sl)
    tm = [A.alloc("hft%d" % i, [512], F32) for i in range(6)]
    h1, h1_r = tm[5]

    def sinf(psrc, ps_r, bcol, fcol, out_ap, out_r, n):
        (xs, xs_r), (s8, s8_r), (s4, s4_r), (t, t_r), (c, c_r) = tm[0:5]
        P.op("dve", lambda e: e.tensor_scalar(out=xs[0:64, 0:n], in0=psrc[0:64, 0:n], scalar1=cl[0:64, bcol:bcol + 1],
                                              scalar2=cl[0:64, fcol:fcol + 1], op0=ALU.add, op1=ALU.mult), [ps_r, cl_r], [xs_r])
        P.op("act", lambda e: e.activation(out=s8[0:64, 0:n], in_=xs[0:64, 0:n], func=AF.Sin, scale=0.125), [xs_r], [s8_r])
        P.op("act", lambda e: e.activation(out=s4[0:64, 0:n], in_=xs[0:64, 0:n], func=AF.Sin, scale=0.25), [xs_r], [s4_r])
        P.op("dve", lambda e: e.tensor_tensor(out=t[0:64, 0:n], in0=s8[0:64, 0:n], in1=s8[0:64, 0:n], op=ALU.mult), [s8_r], [t_r])
        P.op("dve", lambda e: e.tensor_scalar(out=c[0:64, 0:n], in0=t[0:64, 0:n], scalar1=-2.0, scalar2=1.0, op0=ALU.mult,
                                              op1=ALU.add), [t_r], [c_r])
        P.op("dve", lambda e: e.scalar_tensor_tensor(out=s8[0:64, 0:n], in0=s4[0:64, 0:n], scalar=2.0, in1=c[0:64, 0:n],
                                                      op0=ALU.mult, op1=ALU.mult), [s4_r, c_r], [s8_r])
        P.op("dve", lambda e: e.tensor_tensor(out=t[0:64, 0:n], in0=s4[0:64, 0:n], in1=s4[0:64, 0:n], op=ALU.mult), [s4_r], [t_r])
        P.op("dve", lambda e: e.tensor_scalar(out=c[0:64, 0:n], in0=t[0:64, 0:n], scalar1=-2.0, scalar2=1.0, op0=ALU.mult,
                                              op1=ALU.add), [t_r], [c_r])
        P.op("dve", lambda e: e.scalar_tensor_tensor(out=out_ap, in0=s8[0:64, 0:n], scalar=2.0, in1=c[0:64, 0:n],
                                                      op0=ALU.mult, op1=ALU.mult), [s8_r, c_r], [out_r])

    tp2 = [A.alloc("htp%d" % i, [NP_], BF16) for i in range(2)]
    wt2 = [A.alloc("hwt%d" % i, [512], F32) for i in range(2)]
    it = 0
    for o in range(2):
        tiles_o = HY_TILES_R if o == 0 else HY_TILES_N
        P.dma("sp", fe[0:33, :], dr["hy_fe%d" % o], [], [fe_r], fe_r)
        P.dma("sp", t01b, dr["hy_fe%d" % o][0:1, :].partition_broadcast(128), [], [t01b_r], t01b_r)
        for (j0, n, dd) in tiles_o:
            pst, _, ps_r = k.ps[0]
            P.op("pe", lambda e, j0=j0, n=n, pst=pst: e.matmul(pst[0:64, 0:n], lhsT=w1[0:33, :], rhs=fe[0:33, j0:j0 + n], start=True, stop=True),
                 [w1_r, fe_r], [ps_r])
            sinf(pst, ps_r, 0, 2, h1[0:64, 0:n], h1_r, n)
            pst2, _, ps2_r = k.ps[1]
            P.op("pe", lambda e, n=n, pst2=pst2: e.matmul(pst2[0:64, 0:n], lhsT=w2[0:64, :], rhs=h1[0:64, 0:n], start=True, stop=True),
                 [w2_r, h1_r], [ps2_r])
            sinf(pst2, ps2_r, 1, 3, h2[0:64, j0:j0 + n], h2_r, n)
        for m in range(4):
            tp, tp_r = tp2[(o * 4 + m) % 2]
            for (j0, n, dd) in tiles_o:
                pst, _, ps_r = k.ps[2 + it % 4]
                wt, wt_r = wt2[it % 2]
                it += 1
                col0 = o * 1024 + dd * 512 + m * 128
                P.op("pe", lambda e, j0=j0, n=n, pst=pst, col0=col0: e.matmul(pst[:, 0:n], lhsT=w3[0:64, col0:col0 + 128],
                                                                         rhs=h2[0:64, j0:j0 + n], start=True, stop=True),
                     [w3_r, h2_r], [ps_r])
                P.op("act", lambda e, wt=wt, j0=j0, n=n, m=m: e.activation(out=wt[:, 0:n], in_=t01b[:, j0:j0 + n], func=AF.Exp,
                                                                       scale=ndl[:, m:m + 1]), [t01b_r, ndl_r], [wt_r])
                P.op("dve", lambda e, wt=wt, pst=pst, tp=tp, j0=j0, n=n: e.tensor_tensor(out=tp[:, j0:j0 + n], in0=pst[:, 0:n], in1=wt[:, 0:n],
                                                                                      op=ALU.mult), [ps_r, wt_r], [tp_r])
            for jj in ((4095, 8192 + 255) if o == 0 else (4096, 8192 + 256)):
                P.op("dve", lambda e, tp=tp, jj=jj, o=o, m=m: e.tensor_tensor(out=tp[:, jj:jj + 1], in0=tp[:, jj:jj + 1], in1=hb[:, o, m:m + 1],
                                                                          op=ALU.add), [tp_r, hb_r], [tp_r])
            P.dma("sp", dr["TAPS"][o, m * 128:(m + 1) * 128, :], tp, [tp_r], [k.dres["TAPS"]], tp_r)
    A.top = base


def stage_HC(k, l, shared=False):
    P, A, dr = k.P, k.A, k.dr
    base = A.top
    NJ = U // 128
    NPASS = 2 if shared else 1
    CW = BW // NPASS
    cb0, cb1 = (5, 6) if shared else (0, 1)
    ab0, ab1 = (7, 7) if shared else (2, 3)
    VZ, VZ_r = A.alloc("VZ", [2, NJ, CW], BF16)
    X, X_r = A.alloc("HX", [2, NJ, CW], BF16)
    HW = NP_ - 127
    hc2 = [A.alloc("hc%d" % i, [HW], BF16) for i in range(3)]
    jm, jm_r = A.alloc("jmat", [128], BF16)
    P.dma("sp", jm, dr["jmat"], [], [jm_r], jm_r)
    yt2 = [A.alloc("hyt%d" % i, [CW // 128, 128], BF16) for i in range(2)]
    taps_t = dr["TAPS"].tensor
    for hp in range(NPASS):
        ch0 = hp * CW
        VZg_r = [Res("VZg%d" % g) for g in range(CW // 4)]
        for b in range(NSEQ):
            P.dma("sp", VZ[:, b], dr["X12"][b, :, ch0:ch0 + CW].rearrange("(j p) c -> p j c", p=128), [k.dres["X12"]],
                  [VZ_r] + VZg_r, VZ_r)
        yield
        for o in range(2):
            for b in range(NSEQ):
                P.dma("sp", X[:, b], dr["X12"][b, :, 512 * (o + 1) + ch0:512 * (o + 1) + ch0 + CW].rearrange("(j p) c -> p j c", p=128),
                      [k.dres["X12"]], [X_r], X_r)
            yield
            if o == 0:
                for b in range(NSEQ):
                    for J in range(NJ):
                        for h0 in range(0, CW, 512):
                            pr, _, pr_r = k.ps[ab0 if J % 2 == 0 else ab1]
                            w_ = min(512, CW - h0)
                            P.op("pe", lambda e, b=b, J=J, pr=pr, h0=h0, w_=w_: e.matmul(pr[:, 0:w_], lhsT=jm, rhs=X[:, b, J, h0:h0 + w_],
                                                                                      start=True, stop=True), [jm_r, X_r], [pr_r])
                            P.op("act", lambda e, b=b, J=J, pr=pr, h0=h0, w_=w_: e.activation(out=X[:, b, J, h0:h0 + w_], in_=pr[:, 0:w_],
                                                                                           func=AF.Identity), [pr_r], [X_r])
                        yield
            for cl in range(CW):
                c = ch0 + cl
                hc, hc_r = hc2[cl % 3]
                HWl = HW if l < DEPTH - 1 else 8065
                src = bass.AP(taps_t, (o * BW + c) * NP_, [[1, 128], [1, HWl]])
                P.dma("sp", hc[:, 0:HWl], src, [k.dres["TAPS"]], [hc_r], hc_r)
                pst, _, ps_r = k.ps[cb0 if (cl // 4) % 2 == 0 else cb1]
                cb = (cl % 4) * 68
                mms = []
                for d in [0] + [x for x in range(-31, 32) if x != 0]:
                    J0 = max(2, 2 - d)
                    J1 = min(NJ, NJ - d)
                    mms.append(((3968 - 128 * d) if o == 0 else (3969 + 128 * d), J0, J1 - J0, J0 + d))
                for d in ((0, -1, 1) if l < DEPTH - 1 else ()):
                    J0 = max(0, -d)
                    J1 = min(2, 2 - d)
                    mms.append(((8192 + 128 - 128 * d) if o == 0 else (8192 + 129 + 128 * d), J0, J1 - J0, J0 + d))
                for mi, (w0, J0, nJ, I0) in enumerate(mms):
                    P.op("pe", lambda e, hc=hc, w0=w0, J0=J0, nJ=nJ, I0=I0, pst=pst, cb=cb, cl=cl, mi=mi, nm=len(mms): e.matmul(
                        bc(pst[:, cb + I0:cb + I0 + 1], [[NJ, 2], [1, nJ]]), lhsT=hc[:, w0:w0 + 128],
                        rhs=bc(VZ[:, 0, J0, cl:cl + 1], [[NJ * CW, 2], [CW, nJ]]), start=(mi == 0), stop=(mi == nm - 1)),
                        [hc_r, VZg_r[cl // 4]], [ps_r], inc=(mi == len(mms) - 1))
                    if mi % 8 == 7:
                        yield
                if cl % 4 == 3:
                    c0 = cl - 3
                    for b in range(NSEQ):
                        P.op("dve", lambda e, b=b, c0=c0, pst=pst: e.tensor_tensor(
                            out=VZ[:, b, :, c0:c0 + 4], in0=bc(pst[:, b * NJ:b * NJ + 1], [[1, NJ], [68, 4]]),
                            in1=X[:, b, :, c0:c0 + 4], op=ALU.mult), [ps_r, X_r], [VZg_r[c0 // 4]])
                yield
        it = 0
        NM = CW // 128
        for b in range(NSEQ):
            for J in range(NJ):
                _, psb, ps_r = k.ps[ab0 if it % 2 == 0 else ab1]
                yt, yt_r = yt2[it % 2]
                it += 1
                for m in range(NM):
                    P.op("pe", lambda e, b=b, J=J, m=m, psb=psb: e.transpose(out=psb[:, m * 128:(m + 1) * 128], in_=VZ[:, b, J, m * 128:(m + 1) * 128],
                                                                         identity=k.identb), [VZ_r, k.identb_r] + VZg_r[32 * m:32 * m + 32], [ps_r],
                         inc=(m == NM - 1))
                P.op("act", lambda e, yt=yt, psb=psb, NM=NM: e.activation(out=yt, in_=psb[:, 0:NM * 128].rearrange("p (c q) -> p c q", c=NM),
                                                                       func=AF.Identity), [ps_r], [yt_r])
                P.dma("sp", dr["YB"][b, ch0:ch0 + CW, J * 128:(J + 1) * 128].rearrange("(c p) u -> p c u", p=128), yt, [yt_r], [k.dres["YB"]], yt_r)
                yield
    A.top = base


_CACHE = {}


def make_in_maps(inputs):
    consts = host_consts()
    maps = []
    x = np.asarray(inputs["x"], np.float32)
    ctx = np.asarray(inputs["ctx"], np.float32)
    c = np.asarray(inputs["c"], np.float32)
    c_ctx = np.asarray(inputs["c_ctx"], np.float32)
    for core in range(8):
        m = {}
        m["x"] = np.ascontiguousarray(x[2 * core:2 * core + 2])
        m["ctx"] = np.ascontiguousarray(ctx[2 * core:2 * core + 2])
        cv = np.stack([c[2 * core], c[2 * core + 1], c_ctx], axis=0)
        m["cT"] = np.ascontiguousarray(cv.T.reshape(8, 128, 3).transpose(1, 0, 2))
        for n in W_NAMES:
            m[n] = np.ascontiguousarray(np.asarray(inputs[n], np.float32))
        m.update(consts)
        maps.append(m)
    return maps


def kernel(**inputs):
    if "nc" not in _CACHE:
        _CACHE["nc"] = build()
    nc = _CACHE["nc"]
    maps = make_in_maps(inputs)
    res = run_bass_kernel_spmd(nc, maps, core_ids=list(range(8)))
    out = np.concatenate([np.asarray(r["out"], np.float32) for r in res.results], axis=0)
    return out
```
